# Optimizing a Trainium2 kernel written in Bass

```python
import math
import jax
import jax.numpy as jnp
from jax import lax
import numpy as np

D_MODEL = 1024
BATCH = 2
SEQ = 8192
DEPTH = 2

CTX_LEN = 256
GRID_W = 64
HEAD_DIM = 64
BRANCH_W = 512
N_BRANCH = 4
BLOCK = 128
A_HEADS = 8
A_KV = 2
A_WINDOW = 128
B_HEADS = 8
NB_ROWS = 8
NB_COLS = 16
C_HEADS = 8
C_KV = 2
D_HEADS = 4
D_HEAD_DIM = 64
ROPE_THETA = 10000.0
EPS = 1e-6
NEG_INF = -1e30

SPLIT_SIZES = (
    A_HEADS * HEAD_DIM, A_KV * HEAD_DIM, A_KV * HEAD_DIM, BRANCH_W,
    B_HEADS * HEAD_DIM, B_HEADS * HEAD_DIM, B_HEADS * HEAD_DIM, BRANCH_W,
    C_HEADS * HEAD_DIM, C_KV * HEAD_DIM, C_KV * HEAD_DIM, BRANCH_W,
    D_HEADS * 2 * D_HEAD_DIM, D_HEADS * 2 * D_HEAD_DIM, D_HEADS * 2 * D_HEAD_DIM, BRANCH_W,
    N_BRANCH * D_MODEL,
)
SPLIT_POINTS = tuple(sum(SPLIT_SIZES[: i + 1]) for i in range(len(SPLIT_SIZES) - 1))
IN_COLS = sum(SPLIT_SIZES)

kernel_name = "hybrid_gated_dit_block"


def rms_norm(x, gain):
    xf = x.astype(jnp.float32)
    y = xf * lax.rsqrt(jnp.mean(xf * xf, axis=-1, keepdims=True) + EPS)
    return (y * gain.astype(jnp.float32)).astype(x.dtype)


def split_heads(t, n, tail=(HEAD_DIM,)):
    return t.reshape(t.shape[:2] + (n,) + tuple(tail))


def axial_rope_tables(n, dtype):
    t = jnp.arange(n, dtype=jnp.int32)
    pos = jnp.stack([t // GRID_W, t % GRID_W], axis=-1).astype(jnp.float32)
    n_freq = HEAD_DIM // 4
    freqs = ROPE_THETA ** (-jnp.arange(n_freq, dtype=jnp.float32) / n_freq)
    ang = pos[:, :, None] * freqs[None, None, :]
    ang = jnp.concatenate([ang, ang], axis=-1).reshape(n, HEAD_DIM)
    return jnp.cos(ang).astype(dtype), jnp.sin(ang).astype(dtype)


def rotate_half_axial(x):
    xa = x.reshape(x.shape[:-1] + (2, 2, HEAD_DIM // 4))
    return jnp.stack([-xa[..., 1, :], xa[..., 0, :]], axis=-2).reshape(x.shape)


def apply_rope(x, cos, sin):
    shp = (1, x.shape[1]) + (1,) * (x.ndim - 3) + (HEAD_DIM,)
    return x * cos.reshape(shp) + rotate_half_axial(x) * sin.reshape(shp)


def qk_prep(t, n, gain, cos=None, sin=None, tail=(HEAD_DIM,)):
    t = rms_norm(split_heads(t, n, tail), gain)
    if cos is not None:
        t = apply_rope(t, cos, sin)
    return t


def attend_sets(q, kv_sets, sink=None):
    scale = HEAD_DIM ** -0.5
    logits = [jnp.einsum('bqkgd,bjkd->bkgqj', q, k, preferred_element_type=jnp.float32) * scale
              for k, _ in kv_sets]
    if sink is not None:
        kv, g = q.shape[2], q.shape[3]
        logits.append(jnp.broadcast_to(sink.astype(jnp.float32).reshape(1, kv, g, 1, 1),
                                       logits[0].shape[:-1] + (1,)))
    p = jax.nn.softmax(jnp.concatenate(logits, axis=-1), axis=-1)
    out, off = None, 0
    for k, v in kv_sets:
        n = k.shape[1]
        term = jnp.einsum('bkgqj,bjkd->bqkgd', p[..., off:off + n].astype(v.dtype), v)
        out = term if out is None else out + term
        off += n
    return out


def diff_attend_sets(q, kv_sets, lam):
    scale = D_HEAD_DIM ** -0.5
    logits = [jnp.einsum('bqhcd,bjhcd->bhcqj', q, k, preferred_element_type=jnp.float32) * scale
              for k, _ in kv_sets]
    p = jax.nn.softmax(jnp.concatenate(logits, axis=-1), axis=-1)
    pd = p[:, :, 0] - lam * p[:, :, 1]
    out, off = None, 0
    for k, v in kv_sets:
        n = k.shape[1]
        term = jnp.einsum('bhqj,bjhe->bqhe', pd[..., off:off + n].astype(v.dtype), v)
        out = term if out is None else out + term
        off += n
    return out


def diff_lambda(lam, lam_init):
    lf = lam.astype(jnp.float32)
    return jnp.exp(jnp.sum(lf[0] * lf[1])) - jnp.exp(jnp.sum(lf[2] * lf[3])) + lam_init


def finish_diff(o, gain, lam_init):
    return (rms_norm(o, gain) * (1.0 - lam_init)).reshape(o.shape[:2] + (-1,))


def window_sink_attention(q, k, v, kc, vc, sink):
    bsz, seq = q.shape[:2]
    nblk = seq // BLOCK
    g = A_HEADS // A_KV
    scale = HEAD_DIM ** -0.5
    qb = q.reshape(bsz, nblk, BLOCK, A_KV, g, HEAD_DIM)

    def band(t):
        tp = jnp.pad(t, ((0, 0), (BLOCK, BLOCK), (0, 0), (0, 0)))
        tp = tp.reshape(bsz, nblk + 2, BLOCK, A_KV, HEAD_DIM)
        return jnp.concatenate([tp[:, :-2], tp[:, 1:-1], tp[:, 2:]], axis=2)

    kw, vw = band(k), band(v)
    nw = 3 * BLOCK
    s_win = jnp.einsum('bnqkgd,bnjkd->bnkgqj', qb, kw, preferred_element_type=jnp.float32) * scale
    rel = jnp.arange(nw)[None, :] - BLOCK - jnp.arange(BLOCK)[:, None]
    kpos = jnp.arange(nblk)[:, None] * BLOCK - BLOCK + jnp.arange(nw)[None, :]
    mask = (jnp.abs(rel) <= A_WINDOW)[None] & ((kpos >= 0) & (kpos < seq))[:, None, :]
    s_win = jnp.where(mask[None, :, None, None], s_win, NEG_INF)
    s_ctx = jnp.einsum('bnqkgd,bjkd->bnkgqj', qb, kc, preferred_element_type=jnp.float32) * scale
    s_sink = jnp.broadcast_to(sink.astype(jnp.float32).reshape(1, 1, A_KV, g, 1, 1),
                              s_win.shape[:-1] + (1,))
    p = jax.nn.softmax(jnp.concatenate([s_win, s_ctx, s_sink], axis=-1), axis=-1)
    n_ctx = kc.shape[1]
    o = (jnp.einsum('bnkgqj,bnjkd->bnqkgd', p[..., :nw].astype(v.dtype), vw)
         + jnp.einsum('bnkgqj,bjkd->bnqkgd', p[..., nw:nw + n_ctx].astype(v.dtype), vc))
    return o.reshape(bsz, seq, A_HEADS * HEAD_DIM)


def neighborhood_attention(q, k, v, kc, vc, rpb):
    bsz, seq = q.shape[:2]
    rows = seq // GRID_W
    kr = min(NB_ROWS, rows)
    scale = HEAD_DIM ** -0.5
    qg = q.reshape(bsz, rows, GRID_W, B_HEADS, HEAD_DIM)
    r = jnp.arange(rows)
    rstart = jnp.clip(r - kr // 2, 0, rows - kr)
    ridx = rstart[:, None] + jnp.arange(kr)[None, :]
    krows = k.reshape(bsz, rows, GRID_W, B_HEADS, HEAD_DIM)[:, ridx]
    vrows = v.reshape(bsz, rows, GRID_W, B_HEADS, HEAD_DIM)[:, ridx]
    dr = ridx - r[:, None] + (NB_ROWS - 1)
    span = 2 * NB_COLS
    n_win = kr * span
    outs = []
    for c0 in range(0, GRID_W, NB_COLS):
        start = min(max(c0 - NB_COLS // 2, 0), GRID_W - span)
        qcol = c0 + jnp.arange(NB_COLS)
        kcol = start + jnp.arange(span)
        cstart = jnp.clip(qcol - NB_COLS // 2, 0, GRID_W - NB_COLS)
        colmask = (kcol[None, :] >= cstart[:, None]) & (kcol[None, :] < cstart[:, None] + NB_COLS)
        dc = jnp.clip(kcol[None, :] - qcol[:, None], -(NB_COLS - 1), NB_COLS - 1) + (NB_COLS - 1)
        bias = rpb[:, dr[:, None, :, None], dc[None, :, None, :]]
        bias = jnp.moveaxis(bias, 0, 1).astype(jnp.float32)
        qb = qg[:, :, c0:c0 + NB_COLS]
        kb = krows[:, :, :, start:start + span]
        vb = vrows[:, :, :, start:start + span]
        s = jnp.einsum('brqhd,brkjhd->brhqkj', qb, kb, preferred_element_type=jnp.float32) * scale
        s = jnp.where(colmask[:, None, :], s + bias[None], NEG_INF)
        s = s.reshape(bsz, rows, B_HEADS, NB_COLS, n_win)
        s_ctx = jnp.einsum('brqhd,bjhd->brhqj', qb, kc, preferred_element_type=jnp.float32) * scale
        p = jax.nn.softmax(jnp.concatenate([s, s_ctx], axis=-1), axis=-1)
        pw = p[..., :n_win].reshape(bsz, rows, B_HEADS, NB_COLS, kr, span).astype(v.dtype)
        o = (jnp.einsum('brhqkj,brkjhd->brqhd', pw, vb)
             + jnp.einsum('brhqj,bjhd->brqhd', p[..., n_win:].astype(v.dtype), vc))
        outs.append(o)
    return jnp.concatenate(outs, axis=2).reshape(bsz, seq, B_HEADS * HEAD_DIM)


def dense_block_attention(q, k, v, kc, vc):
    bsz, seq = q.shape[:2]
    nblk = seq // BLOCK
    qb = jnp.moveaxis(q.reshape(bsz, nblk, BLOCK, C_KV, C_HEADS // C_KV, HEAD_DIM), 1, 0)
    o = lax.map(lambda qblk: attend_sets(qblk, [(k, v), (kc, vc)]), qb)
    return jnp.moveaxis(o, 0, 1).reshape(bsz, seq, C_HEADS * HEAD_DIM)


def differential_attention(q, k, v, kc, vc, lam):
    bsz, seq = q.shape[:2]
    nblk = seq // BLOCK
    qb = jnp.moveaxis(q.reshape((bsz, nblk, BLOCK) + q.shape[2:]), 1, 0)
    o = lax.map(lambda qblk: diff_attend_sets(qblk, [(k, v), (kc, vc)], lam), qb)
    return jnp.moveaxis(o, 0, 1).reshape((bsz, seq) + o.shape[3:])


def gated_merge(branches, gate_logits, w_br_l, w_out_l):
    ys = jnp.stack(branches, axis=2)
    proj = jnp.einsum('btnw,nwd->btnd', ys, w_br_l)
    g = jax.nn.sigmoid(gate_logits.reshape(gate_logits.shape[:2] + (N_BRANCH, D_MODEL)))
    return jnp.sum(g * proj, axis=2) @ w_out_l


def setup_inputs(seed: int = 0) -> dict:
    key = jax.random.key(seed)
    ks = jax.random.split(key, 15)

    def nrm(k, shape, s):
        return jax.random.normal(k, shape, jnp.float32) * s

    return {
        "x": nrm(ks[0], (BATCH, SEQ, D_MODEL), 1.0),
        "c": nrm(ks[1], (BATCH, D_MODEL), 1.0),
        "ctx": nrm(ks[2], (BATCH, CTX_LEN, D_MODEL), 1.0),
        "c_ctx": nrm(ks[3], (D_MODEL,), 1.0),
        "norm_w": 1.0 + nrm(ks[4], (DEPTH, D_MODEL), 0.02),
        "w_ada": nrm(ks[5], (DEPTH, D_MODEL, 3 * D_MODEL), 0.5 * D_MODEL ** -0.5),
        "b_ada": nrm(ks[6], (DEPTH, 3 * D_MODEL), 0.01),
        "w_in": nrm(ks[7], (DEPTH, D_MODEL, IN_COLS), D_MODEL ** -0.5),
        "qk_gain": 1.0 + nrm(ks[8], (DEPTH, N_BRANCH, 2, HEAD_DIM), 0.02),
        "sink_a": nrm(ks[9], (DEPTH, A_HEADS), 0.5),
        "rpb_b": nrm(ks[10], (DEPTH, B_HEADS, 2 * NB_ROWS - 1, 2 * NB_COLS - 1), 0.1),
        "lam_d": nrm(ks[11], (DEPTH, 4, D_HEAD_DIM), 0.1),
        "subln_d": 1.0 + nrm(ks[12], (DEPTH, 2 * D_HEAD_DIM), 0.02),
        "w_br": nrm(ks[13], (DEPTH, N_BRANCH, BRANCH_W, D_MODEL), BRANCH_W ** -0.5),
        "w_out": nrm(ks[14], (DEPTH, D_MODEL, D_MODEL), D_MODEL ** -0.5),
    }


def reference(x, c, ctx, c_ctx, norm_w, w_ada, b_ada, w_in, qk_gain, sink_a, rpb_b, lam_d,
              subln_d, w_br, w_out):
    bsz, seq, _ = x.shape
    ctx_len = ctx.shape[1]
    cos, sin = axial_rope_tables(seq, x.dtype)
    ga = A_HEADS // A_KV
    gc = C_HEADS // C_KV
    dtail = (2, D_HEAD_DIM)
    for l in range(DEPTH):
        need_ctx = l < DEPTH - 1
        lam_init = 0.8 - 0.6 * math.exp(-0.3 * l)
        mod_x = jax.nn.silu(c) @ w_ada[l] + b_ada[l]
        mod_c = jax.nn.silu(c_ctx) @ w_ada[l] + b_ada[l]
        shift_x, scale_x, gate_x = jnp.split(mod_x[:, None, :], 3, axis=-1)
        shift_c, scale_c, gate_c = jnp.split(mod_c[None, None, :], 3, axis=-1)
        hx = rms_norm(x, norm_w[l]) * (1.0 + scale_x) + shift_x
        hc = rms_norm(ctx, norm_w[l]) * (1.0 + scale_c) + shift_c
        px = jnp.split(hx @ w_in[l], SPLIT_POINTS, axis=-1)
        pc = jnp.split(hc @ w_in[l], SPLIT_POINTS, axis=-1)
        g = qk_gain[l]
        lam = diff_lambda(lam_d[l], lam_init)

        ka_c = qk_prep(pc[1], A_KV, g[0, 1])
        va_c = split_heads(pc[2], A_KV)
        kb_c = qk_prep(pc[5], B_HEADS, g[1, 1])
        vb_c = split_heads(pc[6], B_HEADS)
        kc_c = qk_prep(pc[9], C_KV, g[2, 1])
        vc_c = split_heads(pc[10], C_KV)
        kd_c = qk_prep(pc[13], D_HEADS, g[3, 1], tail=dtail)
        vd_c = split_heads(pc[14], D_HEADS, (2 * D_HEAD_DIM,))

        qa = qk_prep(px[0], A_HEADS, g[0, 0], cos, sin)
        ka = qk_prep(px[1], A_KV, g[0, 1], cos, sin)
        va = split_heads(px[2], A_KV)
        y_a = window_sink_attention(qa, ka, va, ka_c, va_c, sink_a[l])
        qb = qk_prep(px[4], B_HEADS, g[1, 0])
        kb = qk_prep(px[5], B_HEADS, g[1, 1])
        vb = split_heads(px[6], B_HEADS)
        y_b = neighborhood_attention(qb, kb, vb, kb_c, vb_c, rpb_b[l])
        qc = qk_prep(px[8], C_HEADS, g[2, 0], cos, sin)
        kc = qk_prep(px[9], C_KV, g[2, 1], cos, sin)
        vc = split_heads(px[10], C_KV)
        y_c = dense_block_attention(qc, kc, vc, kc_c, vc_c)
        qd = qk_prep(px[12], D_HEADS, g[3, 0], cos, sin, tail=dtail)
        kd = qk_prep(px[13], D_HEADS, g[3, 1], cos, sin, tail=dtail)
        vd = split_heads(px[14], D_HEADS, (2 * D_HEAD_DIM,))
        y_d = finish_diff(differential_attention(qd, kd, vd, kd_c, vd_c, lam), subln_d[l], lam_init)

        branches_x = [y_a * jax.nn.silu(px[3]), y_b * jax.nn.silu(px[7]),
                      y_c * jax.nn.silu(px[11]), y_d * jax.nn.silu(px[15])]
        x_new = x + gate_x * gated_merge(branches_x, px[16], w_br[l], w_out[l])

        if need_ctx:
            qa_c = qk_prep(pc[0], A_HEADS, g[0, 0])
            ya_c = attend_sets(qa_c.reshape(bsz, ctx_len, A_KV, ga, HEAD_DIM), [(ka_c, va_c)],
                               sink_a[l]).reshape(bsz, ctx_len, -1)
            qb_c = qk_prep(pc[4], B_HEADS, g[1, 0])
            yb_c = attend_sets(qb_c[:, :, :, None], [(kb_c, vb_c)]).reshape(bsz, ctx_len, -1)
            qc_c = qk_prep(pc[8], C_HEADS, g[2, 0])
            yc_c = attend_sets(qc_c.reshape(bsz, ctx_len, C_KV, gc, HEAD_DIM),
                               [(kc_c, vc_c)]).reshape(bsz, ctx_len, -1)
            qd_c = qk_prep(pc[12], D_HEADS, g[3, 0], tail=dtail)
            yd_c = finish_diff(diff_attend_sets(qd_c, [(kd_c, vd_c)], lam), subln_d[l], lam_init)
            branches_c = [ya_c * jax.nn.silu(pc[3]), yb_c * jax.nn.silu(pc[7]),
                          yc_c * jax.nn.silu(pc[11]), yd_c * jax.nn.silu(pc[15])]
            ctx = ctx + gate_c * gated_merge(branches_c, pc[16], w_br[l], w_out[l])
        x = x_new
    return x
```

```python
import math
import numpy as np
from contextlib import ExitStack
import concourse.bass as bass
import concourse.mybir as mybir
from concourse.bass_utils import run_bass_kernel_spmd

F32 = mybir.dt.float32
BF16 = mybir.dt.bfloat16
AF = mybir.ActivationFunctionType
ALU = mybir.AluOpType
AX = mybir.AxisListType

D_MODEL = 1024
SEQ = 8192
CTX_LEN = 256
GRID_W = 64
EPS = 1e-6
NEG = -30000.0
KC = 8
NALL = 66
NEXT = 22
NQT = 18
QT_EXT = list(range(2, 18)) + [20, 21]
NTOKQ = NQT * 128
IN_COLS = 10752

ENGS = ("sp", "act", "dve", "pool", "pe")
NSLOT = 8
SAME_ENG_SYNC = True
import os as _os
OPLIMIT = int(_os.environ.get("K_OPLIMIT", "1000000000"))


class Sched:
    def __init__(self):
        self.ops = {e: [] for e in ENGS}
        self.last_w = {}
        self.readers = {}
        self.ndma = {e: 0 for e in ENGS}
        self.bar = set()
        self.bar_pending = {e: False for e in ENGS}

    def add(self, eng, fn, reads=(), writes=(), dma=False):
        self.total = getattr(self, "total", 0) + 1
        if self.total > OPLIMIT:
            return None
        idx = len(self.ops[eng])
        me = (eng, idx)
        deps = set()
        for r in reads:
            if r in self.last_w:
                deps.add(self.last_w[r])
        for w in writes:
            if w in self.last_w:
                deps.add(self.last_w[w])
            for rd in self.readers.get(w, ()):
                deps.add(rd)
        if self.bar_pending[eng]:
            deps |= self.bar
            self.bar_pending[eng] = False
        deps.discard(me)
        op = dict(fn=fn, deps=deps, dma=dma, idx=idx, signal=False)
        if dma:
            k = self.ndma[eng]
            self.ndma[eng] += 1
            op["slot"] = k % NSLOT
            op["target"] = 16 * (k // NSLOT + 1)
        self.ops[eng].append(op)
        for w in writes:
            self.last_w[w] = me
            self.readers[w] = []
        for r in reads:
            if r not in writes:
                self.readers.setdefault(r, []).append(me)
        return me

    def barrier(self):
        bar = set()
        for e in ENGS:
            if self.ops[e]:
                bar.add((e, len(self.ops[e]) - 1))
            cnt = 0
            for op in reversed(self.ops[e]):
                if op["dma"]:
                    bar.add((e, op["idx"]))
                    cnt += 1
                    if cnt >= NSLOT:
                        break
        self.bar = bar
        self.bar_pending = {e: True for e in ENGS}
        self.last_w = {}
        self.readers = {}

    def finalize(self):
        for e in ENGS:
            for op in self.ops[e]:
                keep = set()
                for (pe, pi) in op["deps"]:
                    prod = self.ops[pe][pi]
                    if pe == e and not prod["dma"]:
                        if e == "pe" or not SAME_ENG_SYNC:
                            continue
                    keep.add((pe, pi))
                    if not prod["dma"]:
                        prod["signal"] = True
                op["deps"] = keep
        for e in ENGS:
            c = 0
            for op in self.ops[e]:
                if op["signal"]:
                    c += 1
                    op["sigval"] = c

    def emit_engine(self, e, eng, sems, dma_sems, final_wait=False):
        seen = {}

        def wait(sem, val, key):
            if seen.get(key, 0) < val:
                eng.wait_ge(sem, val)
                seen[key] = val

        for op in self.ops[e]:
            for (pe, pi) in sorted(op["deps"]):
                prod = self.ops[pe][pi]
                if prod["dma"]:
                    wait(dma_sems[pe][prod["slot"]], prod["target"], (pe, prod["slot"]))
                else:
                    wait(sems[pe], prod["sigval"], pe)
            if op["dma"] and op["target"] > 16:
                wait(dma_sems[e][op["slot"]], op["target"] - 16, (e, op["slot"]))
            ins = op["fn"](eng)
            if op["dma"]:
                ins.then_inc(dma_sems[e][op["slot"]], 16)
            elif op["signal"]:
                ins.then_inc(sems[e], 1)
        if final_wait:
            last = {}
            for op in self.ops[e]:
                if op["dma"]:
                    last[op["slot"]] = op["target"]
            for slot, tgt in last.items():
                wait(dma_sems[e][slot], tgt, (e, slot))


class Bld:
    def __init__(self, nc, S):
        self.nc = nc
        self.S = S

    def dma(self, eng, out, in_, R=(), W=()):
        return self.S.add(eng, lambda e: e.dma_start(out=out, in_=in_), R, W, dma=True)

    def tt(self, eng, out, in0, in1, op, R=(), W=()):
        return self.S.add(eng, lambda e: e.tensor_tensor(out=out, in0=in0, in1=in1, op=op), R, W)

    def ts(self, eng, out, in0, s1, s2, op0, op1=None, R=(), W=()):
        if op1 is None:
            return self.S.add(eng, lambda e: e.tensor_scalar(out=out, in0=in0, scalar1=s1, scalar2=None, op0=op0), R, W)
        return self.S.add(eng, lambda e: e.tensor_scalar(out=out, in0=in0, scalar1=s1, scalar2=s2, op0=op0, op1=op1), R, W)

    def stt(self, eng, out, in0, scalar, in1, op0, op1, R=(), W=()):
        return self.S.add(eng, lambda e: e.scalar_tensor_tensor(out=out, in0=in0, scalar=scalar, in1=in1, op0=op0, op1=op1), R, W)

    def act(self, out, in_, func, R=(), W=(), scale=1.0, bias=0.0, accum=None):
        if accum is not None:
            return self.S.add("act", lambda e: e.activation(out=out, in_=in_, func=func, bias=bias, scale=scale, accum_out=accum), R, W)
        return self.S.add("act", lambda e: e.activation(out=out, in_=in_, func=func, bias=bias, scale=scale), R, W)

    def copy(self, eng, out, in_, R=(), W=()):
        if eng == "act":
            return self.S.add("act", lambda e: e.copy(out=out, in_=in_), R, W)
        return self.S.add(eng, lambda e: e.tensor_copy(out=out, in_=in_), R, W)

    def mm(self, out, lhsT, rhs, start, stop, R=(), W=(), skip=False):
        if skip:
            return self.S.add("pe", lambda e: e.matmul(out, lhsT=lhsT, rhs=rhs, start=start, stop=stop, skip_group_check=True), R, W)
        return self.S.add("pe", lambda e: e.matmul(out, lhsT=lhsT, rhs=rhs, start=start, stop=stop), R, W)

    def tr(self, out, in_, ident, R=(), W=()):
        return self.S.add("pe", lambda e: e.transpose(out=out, in_=in_, identity=ident), R, W)

    def red(self, eng, out, in_, R=(), W=()):
        return self.S.add(eng, lambda e: e.reduce_sum(out=out, in_=in_, axis=AX.X), R, W)

    def recip(self, out, in_, R=(), W=()):
        return self.S.add("dve", lambda e: e.reciprocal(out=out, in_=in_), R, W)

    def memset(self, eng, ap, val, W=()):
        return self.S.add(eng, lambda e: e.memset(ap, val), (), W)


class Rot:
    def __init__(self, items):
        self.items = items
        self.i = 0

    def next(self):
        it = self.items[self.i % len(self.items)]
        self.i += 1
        return it


class Arena:
    def __init__(self, ap, n):
        self.ap = ap
        self.n = n
        self.off = 0

    def reset(self):
        self.off = 0

    def get(self, n):
        assert self.off + n <= self.n, ("arena overflow", self.off, n, self.n)
        v = self.ap[:, self.off:self.off + n]
        self.off += n
        return v


def build_layer(nc, S, b, T, L, need_ctx, dbg=None):
    ident = T["ident"]
    PS = T["PS"]
    PSB = T["PSB"]
    AB = T["arenaB"]
    AFp = T["arenaF"]
    P = T["persist"]
    nqt = NQT if need_ctx else 16
    ngrp = 5 if need_ctx else 4

    def phase():
        S.barrier()
        AB.reset()
        AFp.reset()

    phase()
    cT = AFp.get(8)
    ccT = AFp.get(8)
    sc = AFp.get(8)
    scc = AFp.get(8)
    nw = AFp.get(8)
    badaT = AFp.get(24)
    lamr = AFp.get(256)
    lamp = AFp.get(128)
    lamv = AFp.get(4)
    lconst = AFp.get(2)
    sinkf = AFp.get(1024)
    screp = [AFp.get(1024), AFp.get(1024)]
    wa = [AFp.get(KC * 512), AFp.get(KC * 512)]
    bar = [AFp.get(512), AFp.get(512)]
    tmpm = AFp.get(16)

    b.dma("sp", cT, L["cT"], W=["cT"])
    b.dma("sp", ccT, L["ccT"], W=["ccT"])
    b.dma("sp", nw, L["normwT"], W=["nw"])
    b.dma("sp", badaT, L["badaT"], W=["badaT"])
    b.dma("sp", P["gains"], L["gains"], W=["gains"])
    b.dma("sp", lamr, L["lam_rep"], W=["lamr"])
    b.dma("sp", P["subln"], L["sublnT"], W=["subln"])
    b.dma("sp", lconst, L["lconst"], W=["lconst"])
    b.dma("sp", sinkf[0:1, :], L["sink_rep"], W=["sinkf"])
    g4 = P["gains"].rearrange("p (m t d) -> p m t d", m=4, t=2)
    b.ts("dve", g4[:, :, 0, :], g4[:, :, 0, :], 0.125, None, ALU.mult, R=["gains"], W=["gains"])
    b.act(P["esink"][0:1, :], sinkf[0:1, :], AF.Exp, ["sinkf"], ["esink"])
    b.memset("pool", P["sinkV"][0:1, 0:64], 0.0, W=["sinkV0"])
    b.memset("pool", P["sinkV"][0:1, 64:128], 1.0, W=["sinkV1"])
    l4 = lamr.rearrange("p (a d) -> p a d", a=4)
    lp = lamp.rearrange("p (a d) -> p a d", a=2)
    b.tt("dve", lp[:, 0, :], l4[:, 0, :], l4[:, 1, :], ALU.mult, ["lamr"], ["lamp0"])
    b.tt("dve", lp[:, 1, :], l4[:, 2, :], l4[:, 3, :], ALU.mult, ["lamr"], ["lamp1"])
    b.red("dve", lamv[:, 0:2], lp, ["lamp0", "lamp1"], ["lamv"])
    b.act(lamv[:, 0:2], lamv[:, 0:2], AF.Exp, ["lamv"], ["lamv"])
    b.tt("dve", lamv[:, 2:3], lamv[:, 1:2], lamv[:, 0:1], ALU.subtract, ["lamv"], ["lamv2"])
    b.tt("dve", P["neglam"], lamv[:, 2:3], lconst[:, 0:1], ALU.subtract, ["lamv2", "lconst"], ["neglam"])
    b.tt("dve", P["subln"], P["subln"], lconst[:, 1:2], ALU.mult, ["subln", "lconst"], ["subln"])
    b.act(sc, cT, AF.Silu, ["cT"], ["sc"])
    b.act(scc, ccT, AF.Silu, ["ccT"], ["scc"])
    for wi, s_ in enumerate((sc, scc)):
        b.copy("dve", screp[wi].rearrange("p (k m) -> p k m", k=KC), s_.unsqueeze(2).to_broadcast([128, KC, 128]),
               ["sc", "scc"], ["screp%d" % wi])
    psmod = PS[0]
    for blk in range(6):
        wab = wa[blk % 2]
        wres = "wa%d" % (blk % 2)
        wab3 = wab.rearrange("p (k n) -> p k n", k=KC)
        b.dma("sp", wab3, L["w_ada_r"][:, :, blk * 512:(blk + 1) * 512], W=[wres])
        if blk < 4:
            for wi, s_ in enumerate((sc, scc)):
                for j in range(4):
                    col = wi * 16 + blk * 4 + j
                    for k in range(KC):
                        b.mm(psmod[:, col:col + 1], wab3[:, k, j * 128:(j + 1) * 128], s_[:, k:k + 1], k == 0, k == KC - 1,
                             [wres, "sc", "scc"], ["psmod"])
        else:
            brp = bar[blk % 2]
            bres = "bar%d" % (blk % 2)
            b.dma("sp", brp, L["bada_rep"][:, blk * 512:(blk + 1) * 512], W=[bres])
            for wi in range(2):
                ps_, pres = PS[1 + wi], "psg%d" % wi
                for k in range(KC):
                    b.mm(ps_, screp[wi].rearrange("p (k m) -> p k m", k=KC)[:, k, :], wab3[:, k, :], k == 0, k == KC - 1,
                         [wres, "screp%d" % wi], [pres])
                dst = P["gate_rep"][wi][:, (blk - 4) * 512:(blk - 3) * 512]
                b.tt("dve", dst, ps_, brp, ALU.add, [pres, bres], ["gate_rep%d_%d" % (wi, blk)])
    for wi in range(2):
        b.tt("dve", P["shT"][wi], psmod[:, wi * 16:wi * 16 + 8], badaT[:, 0:8], ALU.add, ["psmod", "badaT"], ["shT%d" % wi])
        b.stt("dve", tmpm[:, 0:8], psmod[:, wi * 16 + 8:wi * 16 + 16], 1.0, badaT[:, 8:16], ALU.add, ALU.add,
              ["psmod", "badaT"], ["tmpm"])
        b.tt("dve", P["sT"][wi], tmpm[:, 0:8], nw, ALU.mult, ["tmpm", "nw"], ["sT%d" % wi])

    if dbg == "mod":
        return

    def make_hT(lane, xsrc, wi, hT_out, Wres):
        xt, xs, ssb = lane["xt"], lane["xs"], lane["ss"]
        ln = lane["name"]
        b.dma("sp", xt, xsrc, W=[ln + "xt"])
        b.act(lane["junk"], xt, AF.Square, [ln + "xt"], [ln + "junk", ln + "ss"], accum=ssb[:, 0:1])
        b.act(ssb[:, 1:2], ssb[:, 0:1], AF.Sqrt, [ln + "ss"], [ln + "ss1"], scale=1.0 / D_MODEL, bias=P["epsc"])
        b.recip(ssb[:, 2:3], ssb[:, 1:2], [ln + "ss1"], [ln + "ss2"])
        b.ts("dve", xs, xt, ssb[:, 2:3], None, ALU.mult, R=[ln + "xt", ln + "ss2"], W=[ln + "xs"])
        pt, ptres = T["rotT"].next()
        for k in range(KC):
            b.tr(pt[:, k * 128:(k + 1) * 128], xs[:, k * 128:(k + 1) * 128], ident, [ln + "xs"], [ptres])
        for k in range(KC):
            b.ts("dve", hT_out[:, k, :], pt[:, k * 128:(k + 1) * 128], P["sT"][wi][:, k:k + 1], P["shT"][wi][:, k:k + 1],
                 ALU.mult, ALU.add, R=[ptres, "sT%d" % wi, "shT%d" % wi], W=[Wres + "_e"])
        b.copy("dve", lane["ss"][:, 3:4], lane["ss"][:, 2:3], [ln + "ss2"], [Wres + "_o"])

    def qk_post(lane, src, sres, G, gain, cs, out2d, Wres, perm=False):
        ln = lane["name"]
        n = G * 64
        xf = lane["xf"][:, 0:n]
        sq = lane["sq"][:, 0:n]
        t1 = lane["t1"][:, 0:n]
        t2 = lane["t2"][:, 0:n]
        ssq = lane["ssq"]
        v3 = lambda a: a.rearrange("p (g d) -> p g d", g=G)
        b.copy("act", xf, src, [sres], [ln + "xf"])
        b.tt("pool", sq, xf, xf, ALU.mult, [ln + "xf"], [ln + "sq"])
        b.red("dve", ssq[:, 0:G], v3(sq), [ln + "sq"], [ln + "ssq"])
        b.act(ssq[:, 8:8 + G], ssq[:, 0:G], AF.Sqrt, [ln + "ssq"], [ln + "ssq1"], scale=1.0 / 64, bias=P["epsc"])
        b.recip(ssq[:, 16:16 + G], ssq[:, 8:8 + G], [ln + "ssq1"], [ln + "ssq2"])
        b.tt("dve", v3(sq), v3(xf), ssq[:, 16:16 + G].unsqueeze(2).to_broadcast([128, G, 64]), ALU.mult,
             [ln + "xf", ln + "ssq2"], [ln + "sq"])
        gb = gain.unsqueeze(1).to_broadcast([128, G, 64])
        if cs is None:
            b.tt("pool", v3(out2d), v3(sq), gb, ALU.mult, [ln + "sq", "gains"], [Wres])
            return
        cos, sin, csres = cs
        b.tt("pool", v3(xf), v3(sq), gb, ALU.mult, [ln + "sq", "gains"], [ln + "xf"])
        b.tt("dve", v3(t1), v3(xf), cos.unsqueeze(1).to_broadcast([128, G, 64]), ALU.mult, [ln + "xf"] + csres, [ln + "t1"])
        v5 = lambda a: a.rearrange("p (g a h d) -> p g a h d", g=G, a=2, h=2)
        s4 = sin.rearrange("p (a h d) -> p a h d", a=2, h=2)
        for hh in range(2):
            b.tt("pool", v5(t2)[:, :, :, hh, :], v5(xf)[:, :, :, 1 - hh, :],
                 s4[:, :, hh, :].unsqueeze(1).to_broadcast([128, G, 2, 16]), ALU.mult, [ln + "xf"] + csres, [ln + "t2%d" % hh])
        if perm:
            o = out2d.rearrange("p (i g d) -> p g i d", i=4, g=2)
            a1 = t1.rearrange("p (g i d) -> p g i d", g=2, i=4)
            a2 = t2.rearrange("p (g i d) -> p g i d", g=2, i=4)
        else:
            o, a1, a2 = v3(out2d), v3(t1), v3(t2)
        b.tt("dve", o, a1, a2, ALU.add, [ln + "t1", ln + "t20", ln + "t21"], [Wres])

    def mk_lane(name):
        return dict(name=name, xt=AFp.get(1024), junk=AFp.get(1024), ss=AFp.get(4), xs=AB.get(1024),
                    xf=AFp.get(512), sq=AFp.get(512), t1=AFp.get(512), t2=AFp.get(512), ssq=AFp.get(24))

    def load_cs(tabc, tabs, row0, buf, res):
        b.dma("sp", buf[:, 0:64], tabc[row0:row0 + 128, :], W=[res + "c"])
        b.dma("sp", buf[:, 64:128], tabs[row0:row0 + 128, :], W=[res + "s"])

    gq = lambda m: P["gains"][:, (m * 2) * 64:(m * 2 + 1) * 64]
    gk = lambda m: P["gains"][:, (m * 2 + 1) * 64:(m * 2 + 2) * 64]

    phase()
    Wkv = AB.get(KC * 1280).rearrange("p (k n) -> p k n", k=KC)
    b.dma("pool", Wkv[:, :, 0:256], L["w_in_r"][:, :, 3840:4096], W=["Wkv"])
    b.dma("pool", Wkv[:, :, 256:1280], L["w_in_r"][:, :, 5120:6144], W=["Wkv"])
    lanes = [mk_lane("la0"), mk_lane("la1")]
    hTb = [AB.get(KC * 128).rearrange("p (k t) -> p k t", k=KC) for _ in range(2)]
    csb = [AFp.get(128), AFp.get(128)]
    knC = AB.get(512)
    knD = AB.get(2048)
    vCs = AB.get(4 * 256)
    vDs = AB.get(4 * 1024)
    kTs = AB.get(512)
    kTDs = AB.get(2048)
    b.memset("pool", vCs, 1.0, W=["vCs"])
    b.memset("pool", vDs, 1.0, W=["vDs"])
    rotM = Rot([(PS[2], "M0"), (PS[3], "M1"), (PS[4], "M2"), (PS[5], "M3")])
    rotK = Rot([(PSB[6], "K0"), (PSB[7], "K1")])
    T["rotT"] = Rot([(PSB[0], "T0"), (PSB[1], "T1")])
    ti = 0
    for g in range(17):
        if dbg == "all1" and g >= 1:
            break
        nt = 4 if g < 16 else 2
        for tl in range(nt):
            t = g * 4 + tl
            lane = lanes[ti % 2]
            hT = hTb[ti % 2]
            hres = "hT%d" % (ti % 2)
            hR = [hres + "_e", hres + "_o"]
            cbuf = csb[ti % 2]
            cres = "cs%d" % (ti % 2)
            ti += 1
            if t < 64:
                src = L["xall"][t * 128:(t + 1) * 128, :]
                wi = 0
            else:
                src = L["ctx"][(t - 64) * 128:(t - 63) * 128, :]
                wi = 1
            load_cs(L["cosall"], L["sinall"], t * 128, cbuf, cres)
            make_hT(lane, src, wi, hT, hres)
            cs = (cbuf[:, 0:64], cbuf[:, 64:128], [cres + "c", cres + "s"])
            ps_, pres = rotM.next()
            for k in range(KC):
                b.mm(ps_[:, 0:256], hT[:, k, :], Wkv[:, k, 0:256], k == 0, k == KC - 1, hR + ["Wkv"], [pres])
            qk_post(lane, ps_[:, 0:128], pres, 2, gk(2), cs, knC[:, tl * 128:(tl + 1) * 128], "knC")
            b.copy("act", vCs.rearrange("p (t g e) -> p t g e", t=4, g=2)[:, tl, :, 0:64],
                   ps_[:, 128:256].rearrange("p (g d) -> p g d", g=2), [pres], ["vCs"])
            ps_, pres = rotM.next()
            for k in range(KC):
                b.mm(ps_, hT[:, k, :], Wkv[:, k, 256:768], k == 0, k == KC - 1, hR + ["Wkv"], [pres])
            qk_post(lane, ps_, pres, 8, gk(3), cs, knD[:, tl * 512:(tl + 1) * 512], "knD")
            ps_, pres = rotM.next()
            for k in range(KC):
                b.mm(ps_, hT[:, k, :], Wkv[:, k, 768:1280], k == 0, k == KC - 1, hR + ["Wkv"], [pres])
            b.copy("dve", vDs.rearrange("p (t h v e) -> p t h v e", t=4, h=4, v=2)[:, tl, :, :, 0:64],
                   ps_.rearrange("p (h v d) -> p h v d", h=4, v=2), [pres], ["vDs"])
        tok0 = g * 512
        ntok = nt * 128
        pk, pkres = rotK.next()
        for tl in range(nt):
            b.tr(pk[:, tl * 128:(tl + 1) * 128], knC[:, tl * 128:(tl + 1) * 128], ident, ["knC"], [pkres])
        b.copy("dve", kTs[:, 0:ntok], pk[:, 0:ntok], [pkres], ["kTs"])
        b.dma("sp", L["kTC"][:, tok0:tok0 + ntok], kTs[:, 0:ntok], ["kTs"], ["kTC"])
        for hp in range(2):
            pk, pkres = rotK.next()
            for hl in range(2):
                h = hp * 2 + hl
                for tl in range(nt):
                    b.tr(pk[:, hl * 512 + tl * 128:hl * 512 + (tl + 1) * 128],
                         knD[:, tl * 512 + h * 128:tl * 512 + (h + 1) * 128], ident, ["knD"], [pkres])
            for hl in range(2):
                h = hp * 2 + hl
                b.copy("dve", kTDs[:, h * 512:h * 512 + ntok], pk[:, hl * 512:hl * 512 + ntok], [pkres], ["kTDs%d" % h])
                b.dma("sp", L["kTD"][h, :, tok0:tok0 + ntok], kTDs[:, h * 512:h * 512 + ntok], ["kTDs%d" % h], ["kTD"])
        b.dma("sp", L["vC"][tok0:tok0 + ntok, :].rearrange("(t p) c -> p t c", p=128),
              vCs.rearrange("p (t c) -> p t c", t=4)[:, 0:nt, :], ["vCs"], ["vC"])
        b.dma("sp", L["vD"][tok0:tok0 + ntok, :].rearrange("(t p) c -> p t c", p=128),
              vDs.rearrange("p (t c) -> p t c", t=4)[:, 0:nt, :], ["vDs"], ["vD"])

    if dbg in ("all", "all1"):
        return

    phase()
    hTe = AB.get(KC * NEXT * 128).rearrange("p (k t) -> p k t", k=KC)
    cse = AFp.get(NEXT * 128).rearrange("p (t c) -> p t c", t=NEXT)
    lanes = [mk_lane("le0"), mk_lane("le1")]
    T["rotT"] = Rot([(PSB[0], "T0"), (PSB[1], "T1")])
    b.dma("sp", cse[:, :, 0:64], L["cosext"].rearrange("(t p) c -> p t c", p=128), W=["csec"])
    b.dma("sp", cse[:, :, 64:128], L["sinext"].rearrange("(t p) c -> p t c", p=128), W=["cses"])
    for t in range(NEXT):
        if t < 20:
            src, wi = L["xext"][t * 128:(t + 1) * 128, :], 0
        else:
            src, wi = L["ctx"][(t - 20) * 128:(t - 19) * 128, :], 1
        make_hT(lanes[t % 2], src, wi, hTe[:, :, t * 128:(t + 1) * 128], "hTe")
    Wb = [AB.get(KC * 512).rearrange("p (k n) -> p k n", k=KC) for _ in range(2)]
    qn = [AB.get(512), AB.get(512)]
    stg = [AB.get(512), AB.get(512)]
    vst = [AB.get(1024), AB.get(1024)]
    fst = [AB.get(512), AB.get(512)]
    for i in range(2):
        b.memset("pool", vst[i], 1.0, W=["vst%d" % i])
    rotM = Rot([(PS[2], "M0"), (PS[3], "M1"), (PS[4], "M2"), (PS[5], "M3")])
    rotK = Rot([(PSB[6], "K0"), (PSB[7], "K1")])
    qtiles = QT_EXT[:nqt]
    blocks = [("q", 0, 0, 512), ("kvA", 0, 512, 256), ("gp", 0, 768, 512),
              ("q", 1, 1280, 512), ("kB", 1, 1792, 512), ("vB", 1, 2304, 512), ("gp", 1, 2816, 512),
              ("q", 2, 3328, 512), ("gp", 2, 4096, 512), ("q", 3, 4608, 512), ("gp", 3, 6144, 512)]
    blocks += [("mg", j, 6656 + 512 * j, 512) for j in range(8)]
    qTd = [L["qTA"], L["qTB"], L["qTC"], None]
    cnt = 0
    for bi, (kind, m, c0, ncol) in enumerate(blocks):
        W_ = Wb[bi % 2]
        wres = "Wb%d" % (bi % 2)
        b.dma("pool", W_[:, :, 0:ncol], L["w_in_r"][:, :, c0:c0 + ncol], W=[wres])
        if kind in ("q", "kvA", "kB", "vB"):
            tl_list = qtiles if kind == "q" else list(range(NEXT))
            for qi, t in enumerate(tl_list):
                lane = lanes[cnt % 2]
                par = cnt % 2
                cnt += 1
                ps_, pres = rotM.next()
                for k in range(KC):
                    b.mm(ps_[:, 0:ncol], hTe[:, k, t * 128:(t + 1) * 128], W_[:, k, 0:ncol], k == 0, k == KC - 1, ["hTe_e", "hTe_o", wres], [pres])
                cs = (cse[:, t, 0:64], cse[:, t, 64:128], ["csec", "cses"])
                if kind == "q":
                    qo = qn[par]
                    qk_post(lane, ps_, pres, 8, gq(m), None if m == 1 else cs, qo, "qn%d" % par, perm=(m in (0, 2)))
                    pk, pkres = rotK.next()
                    for i in range(4):
                        b.tr(pk[:, i * 128:(i + 1) * 128], qo[:, i * 128:(i + 1) * 128], ident, ["qn%d" % par], [pkres])
                    b.copy("dve", stg[par], pk[:, 0:512], [pkres], ["stg%d" % par])
                    if m < 3:
                        b.dma("sp", qTd[m][qi], stg[par], ["stg%d" % par], ["qT%d" % m])
                    else:
                        b.dma("sp", L["qTD"][qi // 4, :, :, (qi % 4) * 128:(qi % 4 + 1) * 128],
                              stg[par].rearrange("p (h t) -> p h t", h=4), ["stg%d" % par], ["qT3"])
                elif kind == "kvA":
                    qo = qn[par]
                    qk_post(lane, ps_[:, 0:128], pres, 2, gk(0), cs, qo[:, 0:128], "qn%d" % par)
                    pk, pkres = rotK.next()
                    b.tr(pk[:, 0:128], qo[:, 0:128], ident, ["qn%d" % par], [pkres])
                    b.copy("dve", stg[par][:, 0:128], pk[:, 0:128], [pkres], ["stg%d" % par])
                    b.dma("sp", L["kTA"][:, t * 128:(t + 1) * 128], stg[par][:, 0:128], ["stg%d" % par], ["kTA"])
                    vv = vst[par][:, 0:256].rearrange("p (g e) -> p g e", g=2)
                    b.copy("act", vv[:, :, 0:64], ps_[:, 128:256].rearrange("p (g d) -> p g d", g=2), [pres], ["vst%d" % par])
                    b.dma("sp", L["vA"][t * 128:(t + 1) * 128, :], vst[par][:, 0:256], ["vst%d" % par], ["vA"])
                elif kind == "kB":
                    qo = qn[par]
                    qk_post(lane, ps_, pres, 8, gk(1), None, qo, "qn%d" % par)
                    pk, pkres = rotK.next()
                    for i in range(4):
                        b.tr(pk[:, i * 128:(i + 1) * 128], qo[:, i * 128:(i + 1) * 128], ident, ["qn%d" % par], [pkres])
                    b.copy("dve", stg[par], pk[:, 0:512], [pkres], ["stg%d" % par])
                    b.dma("sp", L["kTB"][:, :, t * 128:(t + 1) * 128].rearrange("i p t -> p i t"),
                          stg[par].rearrange("p (i t) -> p i t", i=4), ["stg%d" % par], ["kTB"])
                else:
                    vv = vst[par].rearrange("p (h e) -> p h e", h=8)
                    b.copy("dve", vv[:, :, 0:64], ps_.rearrange("p (h d) -> p h d", h=8), [pres], ["vst%d" % par])
                    b.dma("sp", L["vB"][t * 128:(t + 1) * 128, :], vst[par], ["vst%d" % par], ["vB"])
        else:
            func = AF.Silu if kind == "gp" else AF.Sigmoid
            for gi in range(ngrp):
                if gi < 4:
                    e0, ntok, q0 = (2 + gi * 4) * 128, 512, gi * 512
                else:
                    e0, ntok, q0 = 20 * 128, 256, 2048
                for jc in range(4):
                    par = cnt % 2
                    cnt += 1
                    ps_, pres = rotM.next()
                    for k in range(KC):
                        b.mm(ps_[:, 0:ntok], W_[:, k, jc * 128:(jc + 1) * 128], hTe[:, k, e0:e0 + ntok], k == 0, k == KC - 1,
                             ["hTe_e", "hTe_o", wres], [pres])
                    b.act(fst[par][:, 0:ntok], ps_[:, 0:ntok], func, [pres], ["fst%d" % par])
                    if kind == "gp":
                        dst = L["gpT"][m * 4 + jc, :, q0:q0 + ntok]
                    else:
                        dst = L["mgT"][(m // 2) * 8 + (m % 2) * 4 + jc, :, q0:q0 + ntok]
                    b.dma("sp", dst, fst[par][:, 0:ntok], ["fst%d" % par], ["fT"])
    if dbg == "ext":
        return

    rotS = Rot([(PS[0], "S0"), (PS[1], "S1"), (PS[2], "S2")])
    ACC = [(PS[3], "acc0"), (PS[4], "acc1"), (PS[5], "acc2"), (PS[6], "acc3")]

    def att_common():
        c = {}
        c["Pb"] = Rot([(AB.get(512), "P%d" % i) for i in range(3)])
        c["qTb"] = [AB.get(512), AB.get(512)]
        c["gpb"] = [AB.get(512), AB.get(512)]
        c["yst"] = [AB.get(512), AB.get(512)]
        c["rbuf"] = AFp.get(512)
        c["tmp"] = AFp.get(512)
        return c

    def run_unit(c, N, keys, accs, sink=None, Rq=()):
        nk = len(keys)

        def do_S(j):
            kd = keys[j]
            sp, sres = rotS.next()
            first = True
            for (rhs_tab, c0, c1) in kd.get("tab", ()):
                b.mm(sp[:, c0:c1], ident, rhs_tab, first, False, kd["R"], [sres], skip=True)
                first = False
            for (kT, qT, c0, c1) in kd["S"]:
                b.mm(sp[:, c0:c1], kT, qT, first, True, list(kd["R"]) + list(Rq), [sres], skip=True)
                first = False
            return sp, sres

        cur = do_S(0)
        for j in range(nk):
            nxt = do_S(j + 1) if j + 1 < nk else None
            sp, sres = cur
            pb, pres = c["Pb"].next()
            b.act(pb[:, 0:N], sp[:, 0:N], AF.Exp, [sres], [pres])
            started = set()
            for (V, ai, c0, c1) in keys[j]["PV"]:
                acc, ares = accs[ai]
                st = (j == 0) and (ai not in started)
                started.add(ai)
                b.mm(acc[:, c0:c1], V, pb[:, c0:c1], st, (j == nk - 1) and sink is None, [pres] + list(keys[j]["R"]), [ares], skip=True)
            cur = nxt
        if sink is not None:
            acc, ares = accs[0]
            b.mm(acc[:, 0:N], sink[0], sink[1], False, True, ["esink", "sinkV0", "sinkV1"], [ares], skip=True)

    def epi_heads(c, acc, ares, heads, gpb, gres, yst, yres):
        N = 128 * len(heads)
        rbuf, tmp = c["rbuf"], c["tmp"]
        b.recip(rbuf[64:128, 0:N], acc[64:128, 0:N], [ares], ["rbuf"])
        g3 = gpb.rearrange("p (c t) -> p c t", c=4)
        y3 = yst.rearrange("p (c t) -> p c t", c=4)
        for i, h in enumerate(heads):
            cc, pb_ = h // 2, (h % 2) * 64
            sl = slice(i * 128, (i + 1) * 128)
            b.tt("dve", tmp[pb_:pb_ + 64, sl], acc[0:64, sl], rbuf[64:128, sl], ALU.mult, [ares, "rbuf"], ["tmp"])
            b.tt("pool", y3[pb_:pb_ + 64, cc, :], tmp[pb_:pb_ + 64, sl], g3[pb_:pb_ + 64, cc, :], ALU.mult,
                 ["tmp", gres], [yres])

    def load_q(c, m, qt, par):
        b.dma("sp", c["qTb"][par], [L["qTA"], L["qTB"], L["qTC"]][m][qt], W=["qTb%d" % par])
        b.dma("sp", c["gpb"][par].rearrange("p (c t) -> p c t", c=4),
              L["gpT"][m * 4:(m + 1) * 4, :, qt * 128:(qt + 1) * 128].rearrange("c p t -> p c t"), W=["gpb%d" % par])

    def store_y(c, m, qt, par):
        b.dma("sp", L["yT"][m * 4:(m + 1) * 4, :, qt * 128:(qt + 1) * 128].rearrange("c p t -> p c t"),
              c["yst"][par].rearrange("p (c t) -> p c t", c=4), ["yst%d" % par], ["yT"])

    phase()
    c = att_common()
    KTA = AB.get(NEXT * 128)
    VA = AB.get(NEXT * 256).rearrange("p (t g e) -> p t g e", t=NEXT, g=2)
    tabA = AB.get(4 * 512).rearrange("p (c n) -> p c n", c=4)
    b.dma("sp", KTA, L["kTA"], W=["KT"])
    b.dma("sp", VA.rearrange("p t g e -> p t (g e)"), L["vA"].rearrange("(t p) c -> p t c", p=128), W=["V"])
    b.dma("pool", tabA, L["tabA"].rearrange("c p n -> p c n"), W=["tab"])
    ui = 0
    for qt in range(nqt):
        par = qt % 2
        e = QT_EXT[qt]
        load_q(c, 0, qt, par)
        for g in range(2):
            if qt < 16:
                kl = [(e - 1, 0 if qt == 0 else 1), (e, None), (e + 1, 3 if qt == 15 else 2), (20, None), (21, None)]
            else:
                kl = [(20, None), (21, None)]
            keys = []
            for (t, ti_) in kl:
                kd = dict(S=[(KTA[g * 64:(g + 1) * 64, t * 128:(t + 1) * 128], c["qTb"][par][g * 64:(g + 1) * 64, :], 0, 512)],
                          PV=[(VA[:, t, g, :], 0, 0, 512)], R=["KT", "V", "tab"])
                if ti_ is not None:
                    kd["tab"] = [(tabA[:, ti_, :], 0, 512)]
                keys.append(kd)
            acc = ACC[ui % 2]
            ui += 1
            run_unit(c, 512, keys, [acc], sink=(P["sinkV"][0:1, :], P["esink"][0:1, g * 512:(g + 1) * 512]), Rq=["qTb%d" % par])
            epi_heads(c, acc[0], acc[1], [g * 4 + i for i in range(4)], c["gpb"][par], "gpb%d" % par, c["yst"][par], "yst%d" % par)
        store_y(c, 0, qt, par)
    if dbg == "attA":
        return

    phase()
    c = att_common()
    KTB = AB.get(4 * NEXT * 128).rearrange("p (i t) -> p i t", i=4)
    VB = AB.get(NEXT * 1024).rearrange("p (t h e) -> p t h e", t=NEXT, h=8)
    tabB = AB.get(2 * 6 * 512).rearrange("p (q s n) -> p q s n", q=2, s=6)
    tabBf = AFp.get(2 * 6 * 512)
    b.dma("sp", KTB, L["kTB"].rearrange("i p t -> p i t"), W=["KT"])
    for t0 in range(0, NEXT, 11):
        b.dma("sp", VB[:, t0:t0 + 11].rearrange("p t h e -> p t (h e)"),
              L["vB"][t0 * 128:(t0 + 11) * 128, :].rearrange("(t p) c -> p t c", p=128), W=["V"])
    case_of = lambda qt: 0 if qt == 0 else 1 if qt == 1 else 3 if qt == 14 else 4 if qt == 15 else 2
    dl_of = {0: list(range(-2, 4)), 1: list(range(-2, 3)), 2: list(range(-2, 3)), 3: list(range(-2, 3)), 4: list(range(-3, 3))}
    cur_case = None
    ui = 0
    for qt in range(nqt):
        par = qt % 2
        e = QT_EXT[qt]
        load_q(c, 1, qt, par)
        if qt < 16 and case_of(qt) != cur_case:
            cur_case = case_of(qt)
            if _os.environ.get("K_SWAP2"):
                nch = int(_os.environ.get("K_SWAP2"))
                for a in range(nch):
                    w_ = 6144 // nch
                    b.dma("sp", tabBf[:, a * w_:(a + 1) * w_], L["tabB"][cur_case][:, a * w_:(a + 1) * w_], W=["tabf%d" % a])
            elif _os.environ.get("K_SWAP"):
                b.dma("sp", tabBf.rearrange("p (a n) -> p a n", a=6), L["w_br_r"][:, 0:6, :], W=["tabf"])
            else:
                b.dma("sp", tabBf, L["tabB"][cur_case], W=["tabf"])
            b.copy("pool", tabB.rearrange("p q s n -> p (q s n)"), tabBf, ["tabf"], ["tab"])
        for quad in range(2):
            if qt < 16:
                kl = [(e + dl, si) for si, dl in enumerate(dl_of[cur_case])] + [(20, None), (21, None)]
            else:
                kl = [(20, None), (21, None)]
            keys = []
            for (t, si) in kl:
                kd = dict(S=[], PV=[], R=["KT", "V", "tab"])
                if si is not None:
                    kd["tab"] = []
                for hh in range(4):
                    h = 2 * hh + quad
                    i_, g_ = hh, quad
                    kd["S"].append((KTB[g_ * 64:(g_ + 1) * 64, i_, t * 128:(t + 1) * 128],
                                    c["qTb"][par][g_ * 64:(g_ + 1) * 64, i_ * 128:(i_ + 1) * 128], hh * 128, (hh + 1) * 128))
                    kd["PV"].append((VB[:, t, h, :], 0, hh * 128, (hh + 1) * 128))
                    if si is not None:
                        kd["tab"].append((tabB[:, quad, si, hh * 128:(hh + 1) * 128], hh * 128, (hh + 1) * 128))
                keys.append(kd)
            acc = ACC[ui % 2]
            ui += 1
            run_unit(c, 512, keys, [acc], Rq=["qTb%d" % par])
            epi_heads(c, acc[0], acc[1], [2 * i + quad for i in range(4)], c["gpb"][par], "gpb%d" % par, c["yst"][par], "yst%d" % par)
        store_y(c, 1, qt, par)
    if dbg == "attB":
        return

    phase()
    c = att_common()
    KTC = AB.get(NALL * 128)
    VC = AB.get(NALL * 256).rearrange("p (t g e) -> p t g e", t=NALL, g=2)
    b.dma("sp", KTC, L["kTC"], W=["KT"])
    for t0 in range(0, NALL, 11):
        b.dma("sp", VC[:, t0:t0 + 11].rearrange("p t g e -> p t (g e)"),
              L["vC"][t0 * 128:(t0 + 11) * 128, :].rearrange("(t p) c -> p t c", p=128), W=["V"])
    ui = 0
    for qt in range(nqt):
        par = qt % 2
        load_q(c, 2, qt, par)
        for g in range(2):
            tl = list(range(NALL)) if qt < 16 else [64, 65]
            keys = [dict(S=[(KTC[g * 64:(g + 1) * 64, t * 128:(t + 1) * 128], c["qTb"][par][g * 64:(g + 1) * 64, :], 0, 512)],
                         PV=[(VC[:, t, g, :], 0, 0, 512)], R=["KT", "V"]) for t in tl]
            acc = ACC[ui % 2]
            ui += 1
            run_unit(c, 512, keys, [acc], Rq=["qTb%d" % par])
            epi_heads(c, acc[0], acc[1], [g * 4 + i for i in range(4)], c["gpb"][par], "gpb%d" % par, c["yst"][par], "yst%d" % par)
        store_y(c, 2, qt, par)
    if dbg == "attC":
        return

    phase()
    c = att_common()
    KTD = AB.get(NALL * 128)
    VD = AB.get(NALL * 256).rearrange("p (t v e) -> p t v e", t=NALL, v=2)
    rb = [AFp.get(512), AFp.get(512)]
    dT = AFp.get(512)
    dsq = AFp.get(512)
    tA = AFp.get(512)
    tB = AFp.get(512)
    ui = 0
    for h in range(4):
        b.dma("sp", KTD, L["kTD"][h], W=["KT"])
        for t0 in range(0, NALL, 11):
            b.dma("sp", VD[:, t0:t0 + 11].rearrange("p t v e -> p t (v e)"),
                  L["vD"][t0 * 128:(t0 + 11) * 128, h * 256:(h + 1) * 256].rearrange("(t p) c -> p t c", p=128), W=["V"])
        for gi in range(ngrp):
            par = ui % 2
            ui += 1
            N = 512 if gi < 4 else 256
            q0 = gi * 512
            b.dma("sp", c["qTb"][par], L["qTD"][gi, :, h, :], W=["qTb%d" % par])
            b.dma("sp", c["gpb"][par][:, 0:N], L["gpT"][12 + h, :, q0:q0 + N], W=["gpb%d" % par])
            tl = list(range(NALL)) if gi < 4 else [64, 65]
            for cc in range(2):
                keys = [dict(S=[(KTD[cc * 64:(cc + 1) * 64, t * 128:(t + 1) * 128], c["qTb"][par][cc * 64:(cc + 1) * 64, 0:N], 0, N)],
                             PV=[(VD[:, t, 0, :], 0, 0, N), (VD[:, t, 1, :], 1, 0, N)], R=["KT", "V"]) for t in tl]
                run_unit(c, N, keys, [ACC[cc * 2], ACC[cc * 2 + 1]], Rq=["qTb%d" % par])
            for cc in range(2):
                b.recip(rb[cc][64:128, 0:N], ACC[cc * 2][0][64:128, 0:N], [ACC[cc * 2][1]], ["rb%d" % cc])
            for v in range(2):
                b.tt("dve", tA[0:64, 0:N], ACC[v][0][0:64, 0:N], rb[0][64:128, 0:N], ALU.mult, [ACC[v][1], "rb0"], ["tA"])
                b.tt("dve", tB[0:64, 0:N], ACC[2 + v][0][0:64, 0:N], rb[1][64:128, 0:N], ALU.mult, [ACC[2 + v][1], "rb1"], ["tB"])
                b.stt("dve", dT[v * 64:(v + 1) * 64, 0:N], tB[0:64, 0:N], P["neglam"][0:64, :], tA[0:64, 0:N], ALU.mult, ALU.add,
                      ["tA", "tB", "neglam"], ["dT"])
            b.tt("pool", dsq[:, 0:N], dT[:, 0:N], dT[:, 0:N], ALU.mult, ["dT"], ["dsq"])
            b.mm(PS[7][:, 0:N], P["onesF"], dsq[:, 0:N], True, True, ["dsq", "onesF"], ["ssps"])
            b.act(dsq[:, 0:N], PS[7][:, 0:N], AF.Sqrt, ["ssps"], ["dsq"], scale=1.0 / 128, bias=P["epsc"])
            b.recip(tA[:, 0:N], dsq[:, 0:N], ["dsq"], ["tA"])
            b.tt("dve", dT[:, 0:N], dT[:, 0:N], tA[:, 0:N], ALU.mult, ["dT", "tA"], ["dT"])
            b.stt("dve", c["yst"][par][:, 0:N], dT[:, 0:N], P["subln"], c["gpb"][par][:, 0:N], ALU.mult, ALU.mult,
                  ["dT", "subln", "gpb%d" % par], ["yst%d" % par])
            b.dma("sp", L["yT"][12 + h, :, q0:q0 + N], c["yst"][par][:, 0:N], ["yst%d" % par], ["yT"])
    if dbg == "attD":
        return

    phase()
    wbr = AB.get(16 * 1024).rearrange("p (c d) -> p c d", c=16)
    wout = AB.get(8 * 1024).rearrange("p (c d) -> p c d", c=8)
    b.dma("pool", wbr, L["w_br_r"], W=["wbr"])
    b.dma("pool", wout, L["w_out_r"], W=["wout"])
    yTg = [AB.get(16 * 512).rearrange("p (c t) -> p c t", c=16) for _ in range(2)]
    mgg = [AB.get(32 * 512).rearrange("p (c t) -> p c t", c=32) for _ in range(1)]
    mrg = AB.get(8 * 512).rearrange("p (c t) -> p c t", c=8)
    macc = AFp.get(512)
    mtmp = AFp.get(512)
    xres = [AFp.get(1024), AFp.get(1024)]
    xo = [AFp.get(1024), AFp.get(1024)]
    rotN = Rot([(PS[i], "N%d" % i) for i in range(4)])
    rotO = Rot([(PS[4 + i], "O%d" % i) for i in range(4)])
    oi = 0
    for gi in range(ngrp):
        N = 512 if gi < 4 else 256
        q0 = gi * 512
        yg = yTg[gi % 2]
        yres = "yTg%d" % (gi % 2)
        mg = mgg[0]
        b.dma("sp", yg[:, :, 0:N], L["yT"][:, :, q0:q0 + N].rearrange("c p t -> p c t"), W=[yres])
        b.dma("sp", mg[:, :, 0:N], L["mgT"][:, :, q0:q0 + N].rearrange("c p t -> p c t"), W=["mgg"])
        for dc in range(8):
            for n in range(4):
                ps_, pres = rotN.next()
                for wc in range(4):
                    b.mm(ps_[:, 0:N], wbr[:, n * 4 + wc, dc * 128:(dc + 1) * 128], yg[:, n * 4 + wc, 0:N], wc == 0, wc == 3,
                         ["wbr", yres], [pres])
                if n == 0:
                    b.tt("dve", macc[:, 0:N], ps_[:, 0:N], mg[:, n * 8 + dc, 0:N], ALU.mult, [pres, "mgg"], ["macc"])
                else:
                    b.tt("dve", mtmp[:, 0:N], ps_[:, 0:N], mg[:, n * 8 + dc, 0:N], ALU.mult, [pres, "mgg"], ["mtmp"])
                    if n < 3:
                        b.tt("pool", macc[:, 0:N], macc[:, 0:N], mtmp[:, 0:N], ALU.add, ["macc", "mtmp"], ["macc"])
                    else:
                        b.tt("pool", mrg[:, dc, 0:N], macc[:, 0:N], mtmp[:, 0:N], ALU.add, ["macc", "mtmp"], ["mrg"])
        for tl in range(N // 128):
            par = oi % 2
            oi += 1
            if gi < 4:
                qt = gi * 4 + tl
                src = L["xext"][(2 + qt) * 128:(3 + qt) * 128, :]
                dst = L["xnew"][qt * 128:(qt + 1) * 128, :]
                wi = 0
            else:
                src = L["ctx"][tl * 128:(tl + 1) * 128, :]
                dst = L["ctxnew"][tl * 128:(tl + 1) * 128, :]
                wi = 1
            b.dma("sp", xres[par], src, W=["xres%d" % par])
            for half in range(2):
                ps_, pres = rotO.next()
                for dc in range(8):
                    b.mm(ps_, mrg[:, dc, tl * 128:(tl + 1) * 128], wout[:, dc, half * 512:(half + 1) * 512], dc == 0, dc == 7,
                         ["mrg", "wout"], [pres])
                b.tt("dve", xo[par][:, half * 512:(half + 1) * 512], ps_, P["gate_rep"][wi][:, half * 512:(half + 1) * 512], ALU.mult,
                     [pres], ["xo%d_%d" % (par, half)])
                b.tt("pool", xo[par][:, half * 512:(half + 1) * 512], xo[par][:, half * 512:(half + 1) * 512],
                     xres[par][:, half * 512:(half + 1) * 512], ALU.add, ["xo%d_%d" % (par, half), "xres%d" % par], ["xo%d_%d" % (par, half)])
            b.dma("sp", dst, xo[par], ["xo%d_0" % par, "xo%d_1" % par], ["out"])


SCRATCH = {
    "kTC": ([128, NALL * 128], BF16), "vC": ([NALL * 128, 256], BF16),
    "kTD": ([4, 128, NALL * 128], BF16), "vD": ([NALL * 128, 1024], BF16),
    "kTA": ([128, NEXT * 128], BF16), "vA": ([NEXT * 128, 256], BF16),
    "kTB": ([4, 128, NEXT * 128], BF16), "vB": ([NEXT * 128, 1024], BF16),
    "qTA": ([NQT, 128, 512], BF16), "qTB": ([NQT, 128, 512], BF16), "qTC": ([NQT, 128, 512], BF16),
    "qTD": ([5, 128, 4, 512], BF16),
    "gpT": ([16, 128, NTOKQ], BF16), "mgT": ([32, 128, NTOKQ], BF16), "yT": ([16, 128, NTOKQ], BF16),
}
LAYER_INPUTS = {
    "xall": [64 * 128, D_MODEL], "xext": [20 * 128, D_MODEL], "ctx": [256, D_MODEL],
    "cosall": [NALL * 128, 64], "sinall": [NALL * 128, 64], "cosext": [NEXT * 128, 64], "sinext": [NEXT * 128, 64],
    "cT": [128, 8], "ccT": [128, 8], "normwT": [128, 8], "badaT": [128, 24], "bada_rep": [128, 3072],
    "w_ada_r": [128, KC, 3072], "w_in_r": [128, KC, IN_COLS], "w_br_r": [128, 16, 1024], "w_out_r": [128, KC, 1024],
    "gains": [128, 512], "lam_rep": [128, 256], "sublnT": [128, 1], "lconst": [128, 2], "sink_rep": [1, 1024],
    "tabA": [4, 128, 512], "tabB": [5, 128, 2 * 6 * 512],
}


def build_program(dbg=None, dbg_out=()):
    nc = bass.Bass("TRN2", target_bir_lowering=False)
    es = ExitStack()
    S = Sched()
    b = Bld(nc, S)
    L = {}
    for name, shape in LAYER_INPUTS.items():
        L[name] = nc.dram_tensor(name, shape, F32, kind="ExternalInput").ap()
    for name, (shape, dt) in SCRATCH.items():
        kind = "ExternalOutput" if name in dbg_out else "Internal"
        L[name] = nc.dram_tensor(name, shape, dt, kind=kind).ap()
    L["xnew"] = nc.dram_tensor("xnew", [16 * 128, D_MODEL], F32, kind="ExternalOutput").ap()
    L["ctxnew"] = nc.dram_tensor("ctxnew", [256, D_MODEL], F32, kind="ExternalOutput").ap()

    sb = lambda name, shape, dt: es.enter_context(nc.sbuf_tensor("sb_" + name, shape, dt))[:]
    T = {}
    T["ident"] = sb("ident", [128, 128], BF16)
    ones_b = sb("ones_b", [128, 128], BF16)
    T["arenaB"] = Arena(sb("arenaB", [128, 61440], BF16), 61440)
    T["arenaF"] = Arena(sb("arenaF", [128, 13500], F32), 13500)
    P = {}
    P["gains"] = sb("gains", [128, 512], F32)
    P["sT"] = [sb("sT%d" % i, [128, 8], F32) for i in range(2)]
    P["shT"] = [sb("shT%d" % i, [128, 8], F32) for i in range(2)]
    P["gate_rep"] = [sb("gate_rep%d" % i, [128, 1024], F32) for i in range(2)]
    P["esink"] = sb("esink", [1, 1024], BF16)
    P["sinkV"] = sb("sinkV", [1, 128], BF16)
    P["neglam"] = sb("neglam", [128, 1], F32)
    P["subln"] = sb("subln", [128, 1], F32)
    P["epsc"] = sb("epsc", [128, 1], F32)
    P["onesF"] = sb("onesF", [128, 128], F32)
    T["persist"] = P
    ps = [es.enter_context(nc.psum_tensor("ps%d" % i, [128, 512], F32))[:] for i in range(8)]
    T["PS"] = ps
    T["PSB"] = [p.bitcast(BF16) for p in ps]

    b.memset("pool", ones_b, 1.0, W=["ones_b"])
    S.add("pool", lambda e: e.affine_select(out=T["ident"], in_=ones_b, pattern=[[-1, 128]], compare_op=ALU.is_equal,
                                            fill=0.0, base=0, channel_multiplier=1), ["ones_b"], ["ident"])
    b.memset("pool", P["epsc"], EPS, W=["epsc"])
    b.memset("pool", P["onesF"], 1.0, W=["onesF"])

    build_layer(nc, S, b, T, L, need_ctx=True, dbg=dbg)
    return nc, S, es


def launch(nc, S, es, in_maps, trace=False):
    sems = {e: es.enter_context(nc.semaphore("sem_" + e)) for e in ENGS}
    dma_sems = {e: [es.enter_context(nc.semaphore("dsem_%s_%d" % (e, i))) for i in range(NSLOT)]
                for e in ("sp", "act", "pool")}
    S.finalize()
    with nc.Block() as block:
        @block.sync
        def _(e):
            S.emit_engine("sp", e, sems, dma_sems, final_wait=True)

        @block.scalar
        def _(e):
            S.emit_engine("act", e, sems, dma_sems, final_wait=True)

        @block.vector
        def _(e):
            S.emit_engine("dve", e, sems, dma_sems)

        @block.gpsimd
        def _(e):
            S.emit_engine("pool", e, sems, dma_sems, final_wait=True)

        @block.tensor
        def _(e):
            S.emit_engine("pe", e, sems, dma_sems)


def rope_tables():
    t = np.arange(SEQ)
    pos = np.stack([t // GRID_W, t % GRID_W], -1).astype(np.float32)
    nf = 16
    freqs = (np.float32(10000.0) ** (-np.arange(nf, dtype=np.float32) / np.float32(nf))).astype(np.float32)
    ang = pos[:, :, None] * freqs[None, None, :]
    ang = np.concatenate([ang, ang], -1).reshape(SEQ, 64).astype(np.float32)
    cos = np.cos(ang).astype(np.float32)
    sin = np.sin(ang).astype(np.float32).reshape(SEQ, 2, 2, 16).copy()
    sin[:, :, 0, :] *= -1.0
    return cos, sin.reshape(SEQ, 64)


def tab_a(s):
    j = np.arange(128)[:, None]
    q = np.arange(128)[None, :]
    m1 = np.where(j >= q, 0.0, NEG).astype(np.float32)
    p1 = np.where(j <= q, 0.0, NEG).astype(np.float32)
    full = np.full((128, 128), NEG, np.float32)
    cases = [full if s == 0 else m1, m1, p1, full if s == 3 else p1]
    return np.stack([np.tile(c, (1, 4)) for c in cases], 0)


DL_OF = {0: list(range(-2, 4)), 1: list(range(-2, 3)), 2: list(range(-2, 3)), 3: list(range(-2, 3)), 4: list(range(-3, 3))}


def tab_b(s, rpb):
    out = np.full((5, 128, 2, 6, 4, 128), NEG, np.float32)
    j = np.arange(128)[:, None]
    q = np.arange(128)[None, :]
    for case, i in enumerate((0, 1, 5, 14, 15)):
        n = 16 * s + i
        for si, dl in enumerate(DL_OF[case]):
            kt = n + dl
            if kt < 0 or kt > 63:
                continue
            r = 2 * n + q // 64
            qc = q % 64
            kr = 2 * kt + j // 64
            kc = j % 64
            rstart = np.clip(r - 4, 0, 120)
            cstart = np.clip(qc - 8, 0, 48)
            valid = (kr >= rstart) & (kr < rstart + 8) & (kc >= cstart) & (kc < cstart + 16)
            dr = np.clip(kr - r + 7, 0, 14)
            dc = np.clip(kc - qc, -15, 15) + 15
            for h in range(8):
                vals = rpb[h][dr, dc]
                out[case, :, h % 2, si, h // 2, :] = np.where(valid, vals, NEG)
    return out.reshape(5, 128, 2 * 6 * 512)


def prep_layer_inputs(l, x, ctx, inp, consts):
    cos, sin = consts["cos"], consts["sin"]
    f = lambda a: np.ascontiguousarray(a, dtype=np.float32)
    r8 = lambda v: f(v.reshape(8, 128).T)
    lam_init = 0.8 - 0.6 * math.exp(-0.3 * l)
    cosall = f(np.concatenate([cos, np.ones((256, 64), np.float32)], 0))
    sinall = f(np.concatenate([sin, np.zeros((256, 64), np.float32)], 0))
    shared = {
        "normwT": r8(inp["norm_w"][l]),
        "badaT": f(inp["b_ada"][l].reshape(24, 128).T),
        "bada_rep": f(np.broadcast_to(inp["b_ada"][l][None, :], (128, 3072))),
        "w_ada_r": f(inp["w_ada"][l].reshape(8, 128, 3072).transpose(1, 0, 2)),
        "w_in_r": f(inp["w_in"][l].reshape(8, 128, IN_COLS).transpose(1, 0, 2)),
        "w_br_r": f(inp["w_br"][l].reshape(16, 128, 1024).transpose(1, 0, 2)),
        "w_out_r": f(inp["w_out"][l].reshape(8, 128, 1024).transpose(1, 0, 2)),
        "gains": f(np.broadcast_to(inp["qk_gain"][l].reshape(1, 512), (128, 512))),
        "lam_rep": f(np.broadcast_to(inp["lam_d"][l].reshape(1, 256), (128, 256))),
        "sublnT": f(inp["subln_d"][l].reshape(128, 1)),
        "lconst": f(np.broadcast_to(np.array([[lam_init, 1.0 - lam_init]], np.float32), (128, 2))),
        "sink_rep": f(np.repeat(inp["sink_a"][l], 128).reshape(1, 1024)),
        "ccT": r8(inp["c_ctx"]),
        "cosall": cosall, "sinall": sinall,
    }
    maps = []
    for core in range(8):
        bi, s = core // 4, core % 4
        m = dict(shared)
        m["xall"] = f(x[bi])
        m["ctx"] = f(ctx[bi])
        m["cT"] = r8(inp["c"][bi])
        lo, hi = (16 * s - 2) * 128, (16 * s + 18) * 128
        xe = np.zeros((20 * 128, D_MODEL), np.float32)
        ce = np.ones((NEXT * 128, 64), np.float32)
        se = np.zeros((NEXT * 128, 64), np.float32)
        a, z = max(lo, 0), min(hi, SEQ)
        xe[a - lo:z - lo] = x[bi][a:z]
        ce[a - lo:z - lo] = cos[a:z]
        se[a - lo:z - lo] = sin[a:z]
        m["xext"], m["cosext"], m["sinext"] = xe, ce, se
        m["tabA"] = consts["tabA"][s]
        m["tabB"] = tab_b(s, inp["rpb_b"][l])
        maps.append(m)
    return maps


_PROG = {}


def get_program():
    if "p" not in _PROG:
        nc, S, es = build_program()
        launch(nc, S, es, None)
        _PROG["p"] = (nc, S, es)
    return _PROG["p"]


def kernel(**inputs):
    inp = {k: np.asarray(v) for k, v in inputs.items()}
    x = np.asarray(inp["x"], np.float32)
    ctx = np.asarray(inp["ctx"], np.float32)
    cos, sin = rope_tables()
    consts = {"cos": cos, "sin": sin, "tabA": [tab_a(s) for s in range(4)]}
    nc, S, es = get_program()
    for l in range(2):
        maps = prep_layer_inputs(l, x, ctx, inp, consts)
        res = run_bass_kernel_spmd(nc, maps, core_ids=list(range(8)))
        outs = res.results
        x = np.stack([np.concatenate([outs[bi * 4 + s]["xnew"] for s in range(4)], 0) for bi in range(2)], 0)
        ctx = np.stack([outs[bi * 4]["ctxnew"] for bi in range(2)], 0)
    return x.astype(np.float32)
```

```python
import math
import numpy as np
from contextlib import ExitStack
import concourse.bass as bass
import concourse.mybir as mybir
from concourse.bass_utils import run_bass_kernel_spmd

F32 = mybir.dt.float32
BF16 = mybir.dt.bfloat16
AF = mybir.ActivationFunctionType
ALU = mybir.AluOpType
AX = mybir.AxisListType

D_MODEL = 1024
SEQ = 8192
CTX_LEN = 256
GRID_W = 64
EPS = 1e-6
NEG = -30000.0
KC = 8
NALL = 66
NEXT = 22
NQT = 18
QT_EXT = list(range(2, 18)) + [20, 21]
NTOKQ = NQT * 128
IN_COLS = 10752

ENGS = ("sp", "act", "dve", "pool", "pe")
NSLOT = 8
SAME_ENG_SYNC = True
import os as _os
OPLIMIT = int(_os.environ.get("K_OPLIMIT", "1000000000"))


class Sched:
    def __init__(self):
        self.ops = {e: [] for e in ENGS}
        self.last_w = {}
        self.readers = {}
        self.ndma = {e: 0 for e in ENGS}
        self.bar = set()
        self.bar_pending = {e: False for e in ENGS}

    def add(self, eng, fn, reads=(), writes=(), dma=False, cc=False):
        self.total = getattr(self, "total", 0) + 1
        if self.total > OPLIMIT:
            return None
        idx = len(self.ops[eng])
        me = (eng, idx)
        deps = set()
        for r in reads:
            if r in self.last_w:
                deps.add(self.last_w[r])
        for w in writes:
            if w in self.last_w:
                deps.add(self.last_w[w])
            for rd in self.readers.get(w, ()):
                deps.add(rd)
        if self.bar_pending[eng]:
            deps |= self.bar
            self.bar_pending[eng] = False
        deps.discard(me)
        op = dict(fn=fn, deps=deps, dma=dma or cc, idx=idx, signal=False, cc=cc)
        if cc:
            self.ncc = getattr(self, "ncc", 0) + 1
            op["cc_target"] = self.ncc
        if dma and not cc:
            k = self.ndma[eng]
            self.ndma[eng] += 1
            op["slot"] = k % NSLOT
            op["target"] = 16 * (k // NSLOT + 1)
        self.ops[eng].append(op)
        for w in writes:
            self.last_w[w] = me
            self.readers[w] = []
        for r in reads:
            if r not in writes:
                self.readers.setdefault(r, []).append(me)
        return me

    def barrier(self):
        bar = set()
        for e in ENGS:
            if self.ops[e]:
                bar.add((e, len(self.ops[e]) - 1))
            cnt = 0
            for op in reversed(self.ops[e]):
                if op["dma"]:
                    bar.add((e, op["idx"]))
                    if not op["cc"]:
                        cnt += 1
                    if cnt >= NSLOT:
                        break
        self.bar = bar
        self.bar_pending = {e: True for e in ENGS}
        self.last_w = {}
        self.readers = {}

    def finalize(self):
        for e in ENGS:
            for op in self.ops[e]:
                keep = set()
                for (pe, pi) in op["deps"]:
                    prod = self.ops[pe][pi]
                    if pe == e and not prod["dma"]:
                        if e == "pe" or not SAME_ENG_SYNC:
                            continue
                    keep.add((pe, pi))
                    if not prod["dma"]:
                        prod["signal"] = True
                op["deps"] = keep
        for e in ENGS:
            c = 0
            for op in self.ops[e]:
                if op["signal"]:
                    c += 1
                    op["sigval"] = c

    def emit_engine(self, e, eng, sems, dma_sems, final_wait=False):
        seen = {}

        def wait(sem, val, key):
            if seen.get(key, 0) < val:
                eng.wait_ge(sem, val)
                seen[key] = val

        for op in self.ops[e]:
            for (pe, pi) in sorted(op["deps"]):
                prod = self.ops[pe][pi]
                if prod["cc"]:
                    wait(dma_sems["cc"], prod["cc_target"], "cc")
                elif prod["dma"]:
                    wait(dma_sems[pe][prod["slot"]], prod["target"], (pe, prod["slot"]))
                else:
                    wait(sems[pe], prod["sigval"], pe)
            if op["dma"] and not op["cc"] and op["target"] > 16:
                wait(dma_sems[e][op["slot"]], op["target"] - 16, (e, op["slot"]))
            ins = op["fn"](eng)
            if op["cc"]:
                ins.then_inc(dma_sems["cc"])
            elif op["dma"]:
                ins.then_inc(dma_sems[e][op["slot"]], 16)
            elif op["signal"]:
                ins.then_inc(sems[e], 1)
        if final_wait:
            last = {}
            for op in self.ops[e]:
                if op["dma"] and not op["cc"]:
                    last[op["slot"]] = op["target"]
            for slot, tgt in last.items():
                wait(dma_sems[e][slot], tgt, (e, slot))


class Bld:
    def __init__(self, nc, S):
        self.nc = nc
        self.S = S

    def dma(self, eng, out, in_, R=(), W=()):
        return self.S.add(eng, lambda e: e.dma_start(out=out, in_=in_), R, W, dma=True)

    def tt(self, eng, out, in0, in1, op, R=(), W=()):
        return self.S.add(eng, lambda e: e.tensor_tensor(out=out, in0=in0, in1=in1, op=op), R, W)

    def ts(self, eng, out, in0, s1, s2, op0, op1=None, R=(), W=()):
        if op1 is None:
            return self.S.add(eng, lambda e: e.tensor_scalar(out=out, in0=in0, scalar1=s1, scalar2=None, op0=op0), R, W)
        return self.S.add(eng, lambda e: e.tensor_scalar(out=out, in0=in0, scalar1=s1, scalar2=s2, op0=op0, op1=op1), R, W)

    def stt(self, eng, out, in0, scalar, in1, op0, op1, R=(), W=()):
        return self.S.add(eng, lambda e: e.scalar_tensor_tensor(out=out, in0=in0, scalar=scalar, in1=in1, op0=op0, op1=op1), R, W)

    def act(self, out, in_, func, R=(), W=(), scale=1.0, bias=0.0, accum=None):
        if accum is not None:
            return self.S.add("act", lambda e: e.activation(out=out, in_=in_, func=func, bias=bias, scale=scale, accum_out=accum), R, W)
        return self.S.add("act", lambda e: e.activation(out=out, in_=in_, func=func, bias=bias, scale=scale), R, W)

    def copy(self, eng, out, in_, R=(), W=()):
        if eng == "act":
            return self.S.add("act", lambda e: e.copy(out=out, in_=in_), R, W)
        return self.S.add(eng, lambda e: e.tensor_copy(out=out, in_=in_), R, W)

    def mm(self, out, lhsT, rhs, start, stop, R=(), W=(), skip=False):
        if skip:
            return self.S.add("pe", lambda e: e.matmul(out, lhsT=lhsT, rhs=rhs, start=start, stop=stop, skip_group_check=True), R, W)
        return self.S.add("pe", lambda e: e.matmul(out, lhsT=lhsT, rhs=rhs, start=start, stop=stop), R, W)

    def tr(self, out, in_, ident, R=(), W=()):
        return self.S.add("pe", lambda e: e.transpose(out=out, in_=in_, identity=ident), R, W)

    def red(self, eng, out, in_, R=(), W=()):
        return self.S.add(eng, lambda e: e.reduce_sum(out=out, in_=in_, axis=AX.X), R, W)

    def recip(self, out, in_, R=(), W=()):
        return self.S.add("dve", lambda e: e.reciprocal(out=out, in_=in_), R, W)

    def memset(self, eng, ap, val, W=()):
        return self.S.add(eng, lambda e: e.memset(ap, val), (), W)


class Rot:
    def __init__(self, items):
        self.items = items
        self.i = 0

    def next(self):
        it = self.items[self.i % len(self.items)]
        self.i += 1
        return it


class Arena:
    def __init__(self, ap, n):
        self.ap = ap
        self.n = n
        self.off = 0

    def reset(self):
        self.off = 0

    def get(self, n):
        assert self.off + n <= self.n, ("arena overflow", self.off, n, self.n)
        v = self.ap[:, self.off:self.off + n]
        self.off += n
        return v


def build_layer(nc, S, b, T, L, need_ctx, dbg=None):
    ident = T["ident"]
    PS = T["PS"]
    PSB = T["PSB"]
    AB = T["arenaB"]
    AFp = T["arenaF"]
    P = T["persist"]
    nqt = NQT if need_ctx else 16
    ngrp = 5 if need_ctx else 4

    def phase():
        S.barrier()
        AB.reset()
        AFp.reset()

    phase()
    cT = AFp.get(8)
    ccT = AFp.get(8)
    sc = AFp.get(8)
    scc = AFp.get(8)
    nw = AFp.get(8)
    badaT = AFp.get(24)
    lamr = AFp.get(256)
    lamp = AFp.get(128)
    lamv = AFp.get(4)
    lconst = AFp.get(2)
    sinkf = AFp.get(1024)
    screp = [AFp.get(1024), AFp.get(1024)]
    wa = [AFp.get(KC * 512), AFp.get(KC * 512)]
    bar = [AFp.get(512), AFp.get(512)]
    tmpm = AFp.get(16)

    b.dma("sp", cT, L["cT"], W=["cT"])
    b.dma("sp", ccT, L["ccT"], W=["ccT"])
    b.dma("sp", nw, L["normwT"], W=["nw"])
    b.dma("sp", badaT, L["badaT"], W=["badaT"])
    b.dma("sp", P["gains"], L["gains"], W=["gains"])
    b.dma("sp", lamr, L["lam_rep"], W=["lamr"])
    b.dma("sp", P["subln"], L["sublnT"], W=["subln"])
    b.dma("sp", lconst, L["lconst"], W=["lconst"])
    b.dma("sp", sinkf[0:1, :], L["sink_rep"], W=["sinkf"])
    g4 = P["gains"].rearrange("p (m t d) -> p m t d", m=4, t=2)
    b.ts("dve", g4[:, :, 0, :], g4[:, :, 0, :], 0.125, None, ALU.mult, R=["gains"], W=["gains"])
    b.act(P["esink"][0:1, :], sinkf[0:1, :], AF.Exp, ["sinkf"], ["esink"])
    b.memset("pool", P["sinkV"][0:1, 0:64], 0.0, W=["sinkV0"])
    b.memset("pool", P["sinkV"][0:1, 64:128], 1.0, W=["sinkV1"])
    l4 = lamr.rearrange("p (a d) -> p a d", a=4)
    lp = lamp.rearrange("p (a d) -> p a d", a=2)
    b.tt("dve", lp[:, 0, :], l4[:, 0, :], l4[:, 1, :], ALU.mult, ["lamr"], ["lamp0"])
    b.tt("dve", lp[:, 1, :], l4[:, 2, :], l4[:, 3, :], ALU.mult, ["lamr"], ["lamp1"])
    b.red("dve", lamv[:, 0:2], lp, ["lamp0", "lamp1"], ["lamv"])
    b.act(lamv[:, 0:2], lamv[:, 0:2], AF.Exp, ["lamv"], ["lamv"])
    b.tt("dve", lamv[:, 2:3], lamv[:, 1:2], lamv[:, 0:1], ALU.subtract, ["lamv"], ["lamv2"])
    b.tt("dve", P["neglam"], lamv[:, 2:3], lconst[:, 0:1], ALU.subtract, ["lamv2", "lconst"], ["neglam"])
    b.tt("dve", P["subln"], P["subln"], lconst[:, 1:2], ALU.mult, ["subln", "lconst"], ["subln"])
    b.act(sc, cT, AF.Silu, ["cT"], ["sc"])
    b.act(scc, ccT, AF.Silu, ["ccT"], ["scc"])
    for wi, s_ in enumerate((sc, scc)):
        b.copy("dve", screp[wi].rearrange("p (k m) -> p k m", k=KC), s_.unsqueeze(2).to_broadcast([128, KC, 128]),
               ["sc", "scc"], ["screp%d" % wi])
    psmod = PS[0]
    for blk in range(6):
        wab = wa[blk % 2]
        wres = "wa%d" % (blk % 2)
        wab3 = wab.rearrange("p (k n) -> p k n", k=KC)
        b.dma("sp", wab3, L["w_ada_r"][:, :, blk * 512:(blk + 1) * 512], W=[wres])
        if blk < 4:
            for wi, s_ in enumerate((sc, scc)):
                for j in range(4):
                    col = wi * 16 + blk * 4 + j
                    for k in range(KC):
                        b.mm(psmod[:, col:col + 1], wab3[:, k, j * 128:(j + 1) * 128], s_[:, k:k + 1], k == 0, k == KC - 1,
                             [wres, "sc", "scc"], ["psmod"])
        else:
            brp = bar[blk % 2]
            bres = "bar%d" % (blk % 2)
            b.dma("sp", brp, L["bada_rep"][:, blk * 512:(blk + 1) * 512], W=[bres])
            for wi in range(2):
                ps_, pres = PS[1 + wi], "psg%d" % wi
                for k in range(KC):
                    b.mm(ps_, screp[wi].rearrange("p (k m) -> p k m", k=KC)[:, k, :], wab3[:, k, :], k == 0, k == KC - 1,
                         [wres, "screp%d" % wi], [pres])
                dst = P["gate_rep"][wi][:, (blk - 4) * 512:(blk - 3) * 512]
                b.tt("dve", dst, ps_, brp, ALU.add, [pres, bres], ["gate_rep%d_%d" % (wi, blk)])
    for wi in range(2):
        b.tt("dve", P["shT"][wi], psmod[:, wi * 16:wi * 16 + 8], badaT[:, 0:8], ALU.add, ["psmod", "badaT"], ["shT%d" % wi])
        b.stt("dve", tmpm[:, 0:8], psmod[:, wi * 16 + 8:wi * 16 + 16], 1.0, badaT[:, 8:16], ALU.add, ALU.add,
              ["psmod", "badaT"], ["tmpm"])
        b.tt("dve", P["sT"][wi], tmpm[:, 0:8], nw, ALU.mult, ["tmpm", "nw"], ["sT%d" % wi])

    if dbg == "mod":
        return

    def make_hT(lane, xsrc, wi, hT_out, Wres):
        xt, xs, ssb = lane["xt"], lane["xs"], lane["ss"]
        ln = lane["name"]
        b.dma("sp", xt, xsrc, W=[ln + "xt"])
        b.act(lane["junk"], xt, AF.Square, [ln + "xt"], [ln + "junk", ln + "ss"], accum=ssb[:, 0:1])
        b.act(ssb[:, 1:2], ssb[:, 0:1], AF.Sqrt, [ln + "ss"], [ln + "ss1"], scale=1.0 / D_MODEL, bias=P["epsc"])
        b.recip(ssb[:, 2:3], ssb[:, 1:2], [ln + "ss1"], [ln + "ss2"])
        b.ts("dve", xs, xt, ssb[:, 2:3], None, ALU.mult, R=[ln + "xt", ln + "ss2"], W=[ln + "xs"])
        pt, ptres = T["rotT"].next()
        for k in range(KC):
            b.tr(pt[:, k * 128:(k + 1) * 128], xs[:, k * 128:(k + 1) * 128], ident, [ln + "xs"], [ptres])
        for k in range(KC):
            b.ts("dve", hT_out[:, k, :], pt[:, k * 128:(k + 1) * 128], P["sT"][wi][:, k:k + 1], P["shT"][wi][:, k:k + 1],
                 ALU.mult, ALU.add, R=[ptres, "sT%d" % wi, "shT%d" % wi], W=[Wres + "_e"])
        b.copy("dve", lane["ss"][:, 3:4], lane["ss"][:, 2:3], [ln + "ss2"], [Wres + "_o"])

    def qk_post(lane, src, sres, G, gain, cs, out2d, Wres, perm=False):
        ln = lane["name"]
        n = G * 64
        xf = lane["xf"][:, 0:n]
        sq = lane["sq"][:, 0:n]
        t1 = lane["t1"][:, 0:n]
        t2 = lane["t2"][:, 0:n]
        ssq = lane["ssq"]
        v3 = lambda a: a.rearrange("p (g d) -> p g d", g=G)
        b.copy("act", xf, src, [sres], [ln + "xf"])
        b.tt("pool", sq, xf, xf, ALU.mult, [ln + "xf"], [ln + "sq"])
        b.red("dve", ssq[:, 0:G], v3(sq), [ln + "sq"], [ln + "ssq"])
        b.act(ssq[:, 8:8 + G], ssq[:, 0:G], AF.Sqrt, [ln + "ssq"], [ln + "ssq1"], scale=1.0 / 64, bias=P["epsc"])
        b.recip(ssq[:, 16:16 + G], ssq[:, 8:8 + G], [ln + "ssq1"], [ln + "ssq2"])
        b.tt("dve", v3(sq), v3(xf), ssq[:, 16:16 + G].unsqueeze(2).to_broadcast([128, G, 64]), ALU.mult,
             [ln + "xf", ln + "ssq2"], [ln + "sq"])
        gb = gain.unsqueeze(1).to_broadcast([128, G, 64])
        if cs is None:
            b.tt("pool", v3(out2d), v3(sq), gb, ALU.mult, [ln + "sq", "gains"], [Wres])
            return
        cos, sin, csres = cs
        b.tt("pool", v3(xf), v3(sq), gb, ALU.mult, [ln + "sq", "gains"], [ln + "xf"])
        b.tt("dve", v3(t1), v3(xf), cos.unsqueeze(1).to_broadcast([128, G, 64]), ALU.mult, [ln + "xf"] + csres, [ln + "t1"])
        v5 = lambda a: a.rearrange("p (g a h d) -> p g a h d", g=G, a=2, h=2)
        s4 = sin.rearrange("p (a h d) -> p a h d", a=2, h=2)
        for hh in range(2):
            b.tt("pool", v5(t2)[:, :, :, hh, :], v5(xf)[:, :, :, 1 - hh, :],
                 s4[:, :, hh, :].unsqueeze(1).to_broadcast([128, G, 2, 16]), ALU.mult, [ln + "xf"] + csres, [ln + "t2%d" % hh])
        if perm:
            o = out2d.rearrange("p (i g d) -> p g i d", i=4, g=2)
            a1 = t1.rearrange("p (g i d) -> p g i d", g=2, i=4)
            a2 = t2.rearrange("p (g i d) -> p g i d", g=2, i=4)
        else:
            o, a1, a2 = v3(out2d), v3(t1), v3(t2)
        b.tt("dve", o, a1, a2, ALU.add, [ln + "t1", ln + "t20", ln + "t21"], [Wres])

    def mk_lane(name):
        return dict(name=name, xt=AFp.get(1024), junk=AFp.get(1024), ss=AFp.get(4), xs=AB.get(1024),
                    xf=AFp.get(512), sq=AFp.get(512), t1=AFp.get(512), t2=AFp.get(512), ssq=AFp.get(24))

    def load_cs(tabc, tabs, row0, buf, res):
        b.dma("sp", buf[:, 0:64], tabc[row0:row0 + 128, :], W=[res + "c"])
        b.dma("sp", buf[:, 64:128], tabs[row0:row0 + 128, :], W=[res + "s"])

    gq = lambda m: P["gains"][:, (m * 2) * 64:(m * 2 + 1) * 64]
    gk = lambda m: P["gains"][:, (m * 2 + 1) * 64:(m * 2 + 2) * 64]

    phase()
    Wkv = AB.get(KC * 1280).rearrange("p (k n) -> p k n", k=KC)
    b.dma("pool", Wkv[:, :, 0:256], L["w_in_r"][:, :, 3840:4096], W=["Wkv"])
    b.dma("pool", Wkv[:, :, 256:1280], L["w_in_r"][:, :, 5120:6144], W=["Wkv"])
    lanes = [mk_lane("la0"), mk_lane("la1")]
    hTb = [AB.get(KC * 128).rearrange("p (k t) -> p k t", k=KC) for _ in range(2)]
    csb = [AFp.get(128), AFp.get(128)]
    knC = AB.get(512)
    knD = AB.get(2048)
    vCs = AB.get(4 * 256)
    vDs = AB.get(4 * 1024)
    kTs = AB.get(512)
    kTDs = AB.get(2048)
    b.memset("pool", vCs, 1.0, W=["vCs"])
    b.memset("pool", vDs, 1.0, W=["vDs"])
    rotM = Rot([(PS[2], "M0"), (PS[3], "M1"), (PS[4], "M2"), (PS[5], "M3")])
    rotK = Rot([(PSB[6], "K0"), (PSB[7], "K1")])
    T["rotT"] = Rot([(PSB[0], "T0"), (PSB[1], "T1")])
    ti = 0
    for g in range(17):
        if dbg == "all1" and g >= 1:
            break
        nt = 4 if g < 16 else 2
        for tl in range(nt):
            t = g * 4 + tl
            lane = lanes[ti % 2]
            hT = hTb[ti % 2]
            hres = "hT%d" % (ti % 2)
            hR = [hres + "_e", hres + "_o"]
            cbuf = csb[ti % 2]
            cres = "cs%d" % (ti % 2)
            ti += 1
            if t < 64:
                src = L["xall_tile"](t)
                wi = 0
            else:
                src = L["ctx"][(t - 64) * 128:(t - 63) * 128, :]
                wi = 1
            load_cs(L["cosall"], L["sinall"], t * 128, cbuf, cres)
            make_hT(lane, src, wi, hT, hres)
            cs = (cbuf[:, 0:64], cbuf[:, 64:128], [cres + "c", cres + "s"])
            ps_, pres = rotM.next()
            for k in range(KC):
                b.mm(ps_[:, 0:256], hT[:, k, :], Wkv[:, k, 0:256], k == 0, k == KC - 1, hR + ["Wkv"], [pres])
            qk_post(lane, ps_[:, 0:128], pres, 2, gk(2), cs, knC[:, tl * 128:(tl + 1) * 128], "knC")
            b.copy("act", vCs.rearrange("p (t g e) -> p t g e", t=4, g=2)[:, tl, :, 0:64],
                   ps_[:, 128:256].rearrange("p (g d) -> p g d", g=2), [pres], ["vCs"])
            ps_, pres = rotM.next()
            for k in range(KC):
                b.mm(ps_, hT[:, k, :], Wkv[:, k, 256:768], k == 0, k == KC - 1, hR + ["Wkv"], [pres])
            qk_post(lane, ps_, pres, 8, gk(3), cs, knD[:, tl * 512:(tl + 1) * 512], "knD")
            ps_, pres = rotM.next()
            for k in range(KC):
                b.mm(ps_, hT[:, k, :], Wkv[:, k, 768:1280], k == 0, k == KC - 1, hR + ["Wkv"], [pres])
            b.copy("dve", vDs.rearrange("p (t h v e) -> p t h v e", t=4, h=4, v=2)[:, tl, :, :, 0:64],
                   ps_.rearrange("p (h v d) -> p h v d", h=4, v=2), [pres], ["vDs"])
        tok0 = g * 512
        ntok = nt * 128
        pk, pkres = rotK.next()
        for tl in range(nt):
            b.tr(pk[:, tl * 128:(tl + 1) * 128], knC[:, tl * 128:(tl + 1) * 128], ident, ["knC"], [pkres])
        b.copy("dve", kTs[:, 0:ntok], pk[:, 0:ntok], [pkres], ["kTs"])
        b.dma("sp", L["kTC"][:, tok0:tok0 + ntok], kTs[:, 0:ntok], ["kTs"], ["kTC"])
        for hp in range(2):
            pk, pkres = rotK.next()
            for hl in range(2):
                h = hp * 2 + hl
                for tl in range(nt):
                    b.tr(pk[:, hl * 512 + tl * 128:hl * 512 + (tl + 1) * 128],
                         knD[:, tl * 512 + h * 128:tl * 512 + (h + 1) * 128], ident, ["knD"], [pkres])
            for hl in range(2):
                h = hp * 2 + hl
                b.copy("dve", kTDs[:, h * 512:h * 512 + ntok], pk[:, hl * 512:hl * 512 + ntok], [pkres], ["kTDs%d" % h])
                b.dma("sp", L["kTD"][h, :, tok0:tok0 + ntok], kTDs[:, h * 512:h * 512 + ntok], ["kTDs%d" % h], ["kTD"])
        b.dma("sp", L["vC"][tok0:tok0 + ntok, :].rearrange("(t p) c -> p t c", p=128),
              vCs.rearrange("p (t c) -> p t c", t=4)[:, 0:nt, :], ["vCs"], ["vC"])
        b.dma("sp", L["vD"][tok0:tok0 + ntok, :].rearrange("(t p) c -> p t c", p=128),
              vDs.rearrange("p (t c) -> p t c", t=4)[:, 0:nt, :], ["vDs"], ["vD"])

    if dbg in ("all", "all1"):
        return

    phase()
    hTe = AB.get(KC * NEXT * 128).rearrange("p (k t) -> p k t", k=KC)
    cse = AFp.get(NEXT * 128).rearrange("p (t c) -> p t c", t=NEXT)
    lanes = [mk_lane("le0"), mk_lane("le1")]
    T["rotT"] = Rot([(PSB[0], "T0"), (PSB[1], "T1")])
    b.dma("sp", cse[:, :, 0:64], L["cosext"].rearrange("(t p) c -> p t c", p=128), W=["csec"])
    b.dma("sp", cse[:, :, 64:128], L["sinext"].rearrange("(t p) c -> p t c", p=128), W=["cses"])
    for t in range(NEXT):
        if t < 20:
            src, wi = L["xext"][t * 128:(t + 1) * 128, :], 0
        else:
            src, wi = L["ctx"][(t - 20) * 128:(t - 19) * 128, :], 1
        make_hT(lanes[t % 2], src, wi, hTe[:, :, t * 128:(t + 1) * 128], "hTe")
    Wb = [AB.get(KC * 512).rearrange("p (k n) -> p k n", k=KC) for _ in range(2)]
    qn = [AB.get(512), AB.get(512)]
    stg = [AB.get(512), AB.get(512)]
    vst = [AB.get(1024), AB.get(1024)]
    fst = [AB.get(512), AB.get(512)]
    for i in range(2):
        b.memset("pool", vst[i], 1.0, W=["vst%d" % i])
    rotM = Rot([(PS[2], "M0"), (PS[3], "M1"), (PS[4], "M2"), (PS[5], "M3")])
    rotK = Rot([(PSB[6], "K0"), (PSB[7], "K1")])
    qtiles = QT_EXT[:nqt]
    blocks = [("q", 0, 0, 512), ("kvA", 0, 512, 256), ("gp", 0, 768, 512),
              ("q", 1, 1280, 512), ("kB", 1, 1792, 512), ("vB", 1, 2304, 512), ("gp", 1, 2816, 512),
              ("q", 2, 3328, 512), ("gp", 2, 4096, 512), ("q", 3, 4608, 512), ("gp", 3, 6144, 512)]
    blocks += [("mg", j, 6656 + 512 * j, 512) for j in range(8)]
    qTd = [L["qTA"], L["qTB"], L["qTC"], None]
    cnt = 0
    for bi, (kind, m, c0, ncol) in enumerate(blocks):
        W_ = Wb[bi % 2]
        wres = "Wb%d" % (bi % 2)
        b.dma("pool", W_[:, :, 0:ncol], L["w_in_r"][:, :, c0:c0 + ncol], W=[wres])
        if kind in ("q", "kvA", "kB", "vB"):
            tl_list = qtiles if kind == "q" else list(range(NEXT))
            for qi, t in enumerate(tl_list):
                lane = lanes[cnt % 2]
                par = cnt % 2
                cnt += 1
                ps_, pres = rotM.next()
                for k in range(KC):
                    b.mm(ps_[:, 0:ncol], hTe[:, k, t * 128:(t + 1) * 128], W_[:, k, 0:ncol], k == 0, k == KC - 1, ["hTe_e", "hTe_o", wres], [pres])
                cs = (cse[:, t, 0:64], cse[:, t, 64:128], ["csec", "cses"])
                if kind == "q":
                    qo = qn[par]
                    qk_post(lane, ps_, pres, 8, gq(m), None if m == 1 else cs, qo, "qn%d" % par, perm=(m in (0, 2)))
                    pk, pkres = rotK.next()
                    for i in range(4):
                        b.tr(pk[:, i * 128:(i + 1) * 128], qo[:, i * 128:(i + 1) * 128], ident, ["qn%d" % par], [pkres])
                    b.copy("dve", stg[par], pk[:, 0:512], [pkres], ["stg%d" % par])
                    if m < 3:
                        b.dma("sp", qTd[m][qi], stg[par], ["stg%d" % par], ["qT%d" % m])
                    else:
                        b.dma("sp", L["qTD"][qi // 4, :, :, (qi % 4) * 128:(qi % 4 + 1) * 128],
                              stg[par].rearrange("p (h t) -> p h t", h=4), ["stg%d" % par], ["qT3"])
                elif kind == "kvA":
                    qo = qn[par]
                    qk_post(lane, ps_[:, 0:128], pres, 2, gk(0), cs, qo[:, 0:128], "qn%d" % par)
                    pk, pkres = rotK.next()
                    b.tr(pk[:, 0:128], qo[:, 0:128], ident, ["qn%d" % par], [pkres])
                    b.copy("dve", stg[par][:, 0:128], pk[:, 0:128], [pkres], ["stg%d" % par])
                    b.dma("sp", L["kTA"][:, t * 128:(t + 1) * 128], stg[par][:, 0:128], ["stg%d" % par], ["kTA"])
                    vv = vst[par][:, 0:256].rearrange("p (g e) -> p g e", g=2)
                    b.copy("act", vv[:, :, 0:64], ps_[:, 128:256].rearrange("p (g d) -> p g d", g=2), [pres], ["vst%d" % par])
                    b.dma("sp", L["vA"][t * 128:(t + 1) * 128, :], vst[par][:, 0:256], ["vst%d" % par], ["vA"])
                elif kind == "kB":
                    qo = qn[par]
                    qk_post(lane, ps_, pres, 8, gk(1), None, qo, "qn%d" % par)
                    pk, pkres = rotK.next()
                    for i in range(4):
                        b.tr(pk[:, i * 128:(i + 1) * 128], qo[:, i * 128:(i + 1) * 128], ident, ["qn%d" % par], [pkres])
                    b.copy("dve", stg[par], pk[:, 0:512], [pkres], ["stg%d" % par])
                    b.dma("sp", L["kTB"][:, :, t * 128:(t + 1) * 128].rearrange("i p t -> p i t"),
                          stg[par].rearrange("p (i t) -> p i t", i=4), ["stg%d" % par], ["kTB"])
                else:
                    vv = vst[par].rearrange("p (h e) -> p h e", h=8)
                    b.copy("dve", vv[:, :, 0:64], ps_.rearrange("p (h d) -> p h d", h=8), [pres], ["vst%d" % par])
                    b.dma("sp", L["vB"][t * 128:(t + 1) * 128, :], vst[par], ["vst%d" % par], ["vB"])
        else:
            func = AF.Silu if kind == "gp" else AF.Sigmoid
            for gi in range(ngrp):
                if gi < 4:
                    e0, ntok, q0 = (2 + gi * 4) * 128, 512, gi * 512
                else:
                    e0, ntok, q0 = 20 * 128, 256, 2048
                for jc in range(4):
                    par = cnt % 2
                    cnt += 1
                    ps_, pres = rotM.next()
                    for k in range(KC):
                        b.mm(ps_[:, 0:ntok], W_[:, k, jc * 128:(jc + 1) * 128], hTe[:, k, e0:e0 + ntok], k == 0, k == KC - 1,
                             ["hTe_e", "hTe_o", wres], [pres])
                    b.act(fst[par][:, 0:ntok], ps_[:, 0:ntok], func, [pres], ["fst%d" % par])
                    if kind == "gp":
                        dst = L["gpT"][m * 4 + jc, :, q0:q0 + ntok]
                    else:
                        dst = L["mgT"][(m // 2) * 8 + (m % 2) * 4 + jc, :, q0:q0 + ntok]
                    b.dma("sp", dst, fst[par][:, 0:ntok], ["fst%d" % par], ["fT"])
    if dbg == "ext":
        return

    rotS = Rot([(PS[0], "S0"), (PS[1], "S1"), (PS[2], "S2")])
    ACC = [(PS[3], "acc0"), (PS[4], "acc1"), (PS[5], "acc2"), (PS[6], "acc3")]

    def att_common():
        c = {}
        c["Pb"] = Rot([(AB.get(512), "P%d" % i) for i in range(3)])
        c["qTb"] = [AB.get(512), AB.get(512)]
        c["gpb"] = [AB.get(512), AB.get(512)]
        c["yst"] = [AB.get(512), AB.get(512)]
        c["rbuf"] = AFp.get(512)
        c["tmp"] = AFp.get(512)
        return c

    def run_unit(c, N, keys, accs, sink=None, Rq=(), rot=None, prot=None):
        nk = len(keys)
        rot = rot or rotS
        prot = prot or c["Pb"]

        def do_S(j):
            kd = keys[j]
            sp, sres = rot.next()
            started = set()
            for (rhs_tab, c0, c1) in kd.get("tab", ()):
                b.mm(sp[:, c0:c1], ident, rhs_tab, (c0 // 512) not in started, False, kd["R"], [sres], skip=True)
                started.add(c0 // 512)
            for (kT, qT, c0, c1) in kd["S"]:
                b.mm(sp[:, c0:c1], kT, qT, (c0 // 512) not in started, True, list(kd["R"]) + list(Rq), [sres], skip=True)
                started.add(c0 // 512)
            return sp, sres

        cur = do_S(0)
        for j in range(nk):
            nxt = do_S(j + 1) if j + 1 < nk else None
            sp, sres = cur
            pb, pres = prot.next()
            b.act(pb[:, 0:N], sp[:, 0:N], AF.Exp, [sres], [pres])
            started = set()
            for pv in keys[j]["PV"]:
                V, ai, c0, c1 = pv[:4]
                d0, d1 = (pv[4], pv[5]) if len(pv) > 4 else (c0, c1)
                acc, ares = accs[ai]
                st = (j == 0) and (ai not in started)
                started.add(ai)
                b.mm(acc[:, d0:d1], V, pb[:, c0:c1], st, (j == nk - 1) and sink is None, [pres] + list(keys[j]["R"]), [ares], skip=True)
            cur = nxt
        if sink is not None:
            acc, ares = accs[0]
            b.mm(acc[:, 0:N], sink[0], sink[1], False, True, ["esink", "sinkV0", "sinkV1"], [ares], skip=True)

    def epi_heads(c, acc, ares, heads, gpb, gres, yst, yres):
        N = 128 * len(heads)
        rbuf, tmp = c["rbuf"], c["tmp"]
        b.recip(rbuf[64:128, 0:N], acc[64:128, 0:N], [ares], ["rbuf"])
        g3 = gpb.rearrange("p (c t) -> p c t", c=4)
        y3 = yst.rearrange("p (c t) -> p c t", c=4)
        for i, h in enumerate(heads):
            cc, pb_ = h // 2, (h % 2) * 64
            sl = slice(i * 128, (i + 1) * 128)
            b.tt("dve", tmp[pb_:pb_ + 64, sl], acc[0:64, sl], rbuf[64:128, sl], ALU.mult, [ares, "rbuf"], ["tmp"])
            b.tt("pool", y3[pb_:pb_ + 64, cc, :], tmp[pb_:pb_ + 64, sl], g3[pb_:pb_ + 64, cc, :], ALU.mult,
                 ["tmp", gres], [yres])

    def load_q(c, m, qt, par):
        b.dma("sp", c["qTb"][par], [L["qTA"], L["qTB"], L["qTC"]][m][qt], W=["qTb%d" % par])
        b.dma("sp", c["gpb"][par].rearrange("p (c t) -> p c t", c=4),
              L["gpT"][m * 4:(m + 1) * 4, :, qt * 128:(qt + 1) * 128].rearrange("c p t -> p c t"), W=["gpb%d" % par])

    def store_y(c, m, qt, par):
        b.dma("sp", L["yT"][m * 4:(m + 1) * 4, :, qt * 128:(qt + 1) * 128].rearrange("c p t -> p c t"),
              c["yst"][par].rearrange("p (c t) -> p c t", c=4), ["yst%d" % par], ["yT"])

    phase()
    c = att_common()
    KTA = AB.get(NEXT * 128)
    VA = AB.get(NEXT * 256).rearrange("p (t g e) -> p t g e", t=NEXT, g=2)
    tabA = AB.get(4 * 512).rearrange("p (c n) -> p c n", c=4)
    b.dma("sp", KTA, L["kTA"], W=["KT"])
    b.dma("sp", VA.rearrange("p t g e -> p t (g e)"), L["vA"].rearrange("(t p) c -> p t c", p=128), W=["V"])
    b.dma("pool", tabA, L["tabA"].rearrange("c p n -> p c n"), W=["tab"])
    ui = 0
    for qt in range(nqt):
        par = qt % 2
        e = QT_EXT[qt]
        load_q(c, 0, qt, par)
        for g in range(2):
            if qt < 16:
                kl = [(e - 1, 0 if qt == 0 else 1), (e, None), (e + 1, 3 if qt == 15 else 2), (20, None), (21, None)]
            else:
                kl = [(20, None), (21, None)]
            keys = []
            for (t, ti_) in kl:
                kd = dict(S=[(KTA[g * 64:(g + 1) * 64, t * 128:(t + 1) * 128], c["qTb"][par][g * 64:(g + 1) * 64, :], 0, 512)],
                          PV=[(VA[:, t, g, :], 0, 0, 512)], R=["KT", "V", "tab"])
                if ti_ is not None:
                    kd["tab"] = [(tabA[:, ti_, :], 0, 512)]
                keys.append(kd)
            acc = ACC[ui % 2]
            ui += 1
            run_unit(c, 512, keys, [acc], sink=(P["sinkV"][0:1, :], P["esink"][0:1, g * 512:(g + 1) * 512]), Rq=["qTb%d" % par])
            epi_heads(c, acc[0], acc[1], [g * 4 + i for i in range(4)], c["gpb"][par], "gpb%d" % par, c["yst"][par], "yst%d" % par)
        store_y(c, 0, qt, par)
    if dbg == "attA":
        return

    phase()
    c = att_common()
    KTB = AB.get(4 * NEXT * 128).rearrange("p (i t) -> p i t", i=4)
    VB = AB.get(NEXT * 1024).rearrange("p (t h e) -> p t h e", t=NEXT, h=8)
    tabB = AB.get(2 * 6 * 512).rearrange("p (q s n) -> p q s n", q=2, s=6)
    tabBf = AFp.get(2 * 6 * 512)
    b.dma("sp", KTB, L["kTB"].rearrange("i p t -> p i t"), W=["KT"])
    for t0 in range(0, NEXT, 11):
        b.dma("sp", VB[:, t0:t0 + 11].rearrange("p t h e -> p t (h e)"),
              L["vB"][t0 * 128:(t0 + 11) * 128, :].rearrange("(t p) c -> p t c", p=128), W=["V"])
    case_of = lambda qt: 0 if qt == 0 else 1 if qt == 1 else 3 if qt == 14 else 4 if qt == 15 else 2
    dl_of = {0: list(range(-2, 4)), 1: list(range(-2, 3)), 2: list(range(-2, 3)), 3: list(range(-2, 3)), 4: list(range(-3, 3))}
    cur_case = None
    ui = 0
    for qt in range(nqt):
        par = qt % 2
        e = QT_EXT[qt]
        load_q(c, 1, qt, par)
        if qt < 16 and case_of(qt) != cur_case:
            cur_case = case_of(qt)
            b.dma("sp", tabBf, L["tabB"][cur_case], W=["tabf"])
            b.copy("pool", tabB.rearrange("p q s n -> p (q s n)"), tabBf, ["tabf"], ["tab"])
        for quad in range(2):
            if qt < 16:
                kl = [(e + dl, si) for si, dl in enumerate(dl_of[cur_case])] + [(20, None), (21, None)]
            else:
                kl = [(20, None), (21, None)]
            keys = []
            for (t, si) in kl:
                kd = dict(S=[], PV=[], R=["KT", "V", "tab"])
                if si is not None:
                    kd["tab"] = []
                for hh in range(4):
                    h = 2 * hh + quad
                    i_, g_ = hh, quad
                    kd["S"].append((KTB[g_ * 64:(g_ + 1) * 64, i_, t * 128:(t + 1) * 128],
                                    c["qTb"][par][g_ * 64:(g_ + 1) * 64, i_ * 128:(i_ + 1) * 128], hh * 128, (hh + 1) * 128))
                    kd["PV"].append((VB[:, t, h, :], 0, hh * 128, (hh + 1) * 128))
                    if si is not None:
                        kd["tab"].append((tabB[:, quad, si, hh * 128:(hh + 1) * 128], hh * 128, (hh + 1) * 128))
                keys.append(kd)
            acc = ACC[ui % 2]
            ui += 1
            run_unit(c, 512, keys, [acc], Rq=["qTb%d" % par])
            epi_heads(c, acc[0], acc[1], [2 * i + quad for i in range(4)], c["gpb"][par], "gpb%d" % par, c["yst"][par], "yst%d" % par)
        store_y(c, 1, qt, par)
    if dbg == "attB":
        return

    phase()
    c = att_common()
    KTC = AB.get(NALL * 128)
    VC = AB.get(NALL * 256).rearrange("p (t g e) -> p t g e", t=NALL, g=2)
    b.dma("sp", KTC, L["kTC"], W=["KT"])
    for t0 in range(0, NALL, 11):
        b.dma("sp", VC[:, t0:t0 + 11].rearrange("p t g e -> p t (g e)"),
              L["vC"][t0 * 128:(t0 + 11) * 128, :].rearrange("(t p) c -> p t c", p=128), W=["V"])
    PSW = T["PSW"]
    rotS2 = Rot([(PSW[:, 0:1024], "SW0"), (PSW[:, 1024:2048], "SW1"), (PSW[:, 2048:3072], "SW2")])
    Pb2 = Rot([(AB.get(1024), "PW%d" % i) for i in range(3)])
    ACC2 = [(PS[6], "acc6"), (PS[7], "acc7")]
    for qt in range(nqt):
        par = qt % 2
        load_q(c, 2, qt, par)
        tl = list(range(NALL)) if qt < 16 else [64, 65]
        keys = [dict(S=[(KTC[g * 64:(g + 1) * 64, t * 128:(t + 1) * 128], c["qTb"][par][g * 64:(g + 1) * 64, :], g * 512, (g + 1) * 512)
                        for g in range(2)],
                     PV=[(VC[:, t, g, :], g, g * 512, (g + 1) * 512, 0, 512) for g in range(2)], R=["KT", "V"]) for t in tl]
        run_unit(c, 1024, keys, ACC2, Rq=["qTb%d" % par], rot=rotS2, prot=Pb2)
        for g in range(2):
            epi_heads(c, ACC2[g][0], ACC2[g][1], [g * 4 + i for i in range(4)], c["gpb"][par], "gpb%d" % par, c["yst"][par], "yst%d" % par)
        store_y(c, 2, qt, par)
    if dbg == "attC":
        return

    phase()
    c = att_common()
    KTD = AB.get(NALL * 128)
    VD = AB.get(NALL * 256).rearrange("p (t v e) -> p t v e", t=NALL, v=2)
    rb = [AFp.get(512), AFp.get(512)]
    dT = AFp.get(512)
    dsq = AFp.get(512)
    tA = AFp.get(512)
    tB = AFp.get(512)
    PSW = T["PSW"]
    rotSD = Rot([(PSW[:, 0:1024], "SW0"), (PSW[:, 1024:2048], "SW1")])
    PbD = Rot([(AB.get(1024), "PW%d" % i) for i in range(3)])
    ACCD = [(PS[4], "acc4"), (PS[5], "acc5"), (PS[6], "acc6"), (PS[7], "acc7")]
    ACC = ACCD
    ui = 0
    for h in range(4):
        b.dma("sp", KTD, L["kTD"][h], W=["KT"])
        for t0 in range(0, NALL, 11):
            b.dma("sp", VD[:, t0:t0 + 11].rearrange("p t v e -> p t (v e)"),
                  L["vD"][t0 * 128:(t0 + 11) * 128, h * 256:(h + 1) * 256].rearrange("(t p) c -> p t c", p=128), W=["V"])
        for gi in range(ngrp):
            par = ui % 2
            ui += 1
            N = 512 if gi < 4 else 256
            q0 = gi * 512
            b.dma("sp", c["qTb"][par], L["qTD"][gi, :, h, :], W=["qTb%d" % par])
            b.dma("sp", c["gpb"][par][:, 0:N], L["gpT"][12 + h, :, q0:q0 + N], W=["gpb%d" % par])
            tl = list(range(NALL)) if gi < 4 else [64, 65]
            keys = [dict(S=[(KTD[cc * 64:(cc + 1) * 64, t * 128:(t + 1) * 128], c["qTb"][par][cc * 64:(cc + 1) * 64, 0:N], cc * 512, cc * 512 + N)
                            for cc in range(2)],
                         PV=[(VD[:, t, v, :], cc * 2 + v, cc * 512, cc * 512 + N, 0, N) for cc in range(2) for v in range(2)],
                         R=["KT", "V"]) for t in tl]
            run_unit(c, 512 + N, keys, ACCD, Rq=["qTb%d" % par], rot=rotSD, prot=PbD)
            for cc in range(2):
                b.recip(rb[cc][64:128, 0:N], ACC[cc * 2][0][64:128, 0:N], [ACC[cc * 2][1]], ["rb%d" % cc])
            for v in range(2):
                b.tt("dve", tA[0:64, 0:N], ACC[v][0][0:64, 0:N], rb[0][64:128, 0:N], ALU.mult, [ACC[v][1], "rb0"], ["tA"])
                b.tt("dve", tB[0:64, 0:N], ACC[2 + v][0][0:64, 0:N], rb[1][64:128, 0:N], ALU.mult, [ACC[2 + v][1], "rb1"], ["tB"])
                b.stt("dve", dT[v * 64:(v + 1) * 64, 0:N], tB[0:64, 0:N], P["neglam"][0:64, :], tA[0:64, 0:N], ALU.mult, ALU.add,
                      ["tA", "tB", "neglam"], ["dT"])
            b.tt("pool", dsq[:, 0:N], dT[:, 0:N], dT[:, 0:N], ALU.mult, ["dT"], ["dsq"])
            ssp, ssres = rotSD.next()
            b.mm(ssp[:, 0:N], P["onesF"], dsq[:, 0:N], True, True, ["dsq", "onesF"], [ssres])
            b.act(dsq[:, 0:N], ssp[:, 0:N], AF.Sqrt, [ssres], ["dsq"], scale=1.0 / 128, bias=P["epsc"])
            b.recip(tA[:, 0:N], dsq[:, 0:N], ["dsq"], ["tA"])
            b.tt("dve", dT[:, 0:N], dT[:, 0:N], tA[:, 0:N], ALU.mult, ["dT", "tA"], ["dT"])
            b.stt("dve", c["yst"][par][:, 0:N], dT[:, 0:N], P["subln"], c["gpb"][par][:, 0:N], ALU.mult, ALU.mult,
                  ["dT", "subln", "gpb%d" % par], ["yst%d" % par])
            b.dma("sp", L["yT"][12 + h, :, q0:q0 + N], c["yst"][par][:, 0:N], ["yst%d" % par], ["yT"])
    if dbg == "attD":
        return

    phase()
    wbr = AB.get(16 * 1024).rearrange("p (c d) -> p c d", c=16)
    wout = AB.get(8 * 1024).rearrange("p (c d) -> p c d", c=8)
    b.dma("pool", wbr, L["w_br_r"], W=["wbr"])
    b.dma("pool", wout, L["w_out_r"], W=["wout"])
    yTg = [AB.get(16 * 512).rearrange("p (c t) -> p c t", c=16) for _ in range(2)]
    mgg = [AB.get(32 * 512).rearrange("p (c t) -> p c t", c=32) for _ in range(1)]
    mrg = AB.get(8 * 512).rearrange("p (c t) -> p c t", c=8)
    macc = AFp.get(512)
    mtmp = AFp.get(512)
    xres = [AFp.get(1024), AFp.get(1024)]
    xo = [AFp.get(1024), AFp.get(1024)]
    rotN = Rot([(PS[i], "N%d" % i) for i in range(4)])
    rotO = Rot([(PS[4 + i], "O%d" % i) for i in range(4)])
    oi = 0
    for gi in range(ngrp):
        N = 512 if gi < 4 else 256
        q0 = gi * 512
        yg = yTg[gi % 2]
        yres = "yTg%d" % (gi % 2)
        mg = mgg[0]
        b.dma("sp", yg[:, :, 0:N], L["yT"][:, :, q0:q0 + N].rearrange("c p t -> p c t"), W=[yres])
        b.dma("sp", mg[:, :, 0:N], L["mgT"][:, :, q0:q0 + N].rearrange("c p t -> p c t"), W=["mgg"])
        for dc in range(8):
            for n in range(4):
                ps_, pres = rotN.next()
                for wc in range(4):
                    b.mm(ps_[:, 0:N], wbr[:, n * 4 + wc, dc * 128:(dc + 1) * 128], yg[:, n * 4 + wc, 0:N], wc == 0, wc == 3,
                         ["wbr", yres], [pres])
                if n == 0:
                    b.tt("dve", macc[:, 0:N], ps_[:, 0:N], mg[:, n * 8 + dc, 0:N], ALU.mult, [pres, "mgg"], ["macc"])
                else:
                    b.tt("dve", mtmp[:, 0:N], ps_[:, 0:N], mg[:, n * 8 + dc, 0:N], ALU.mult, [pres, "mgg"], ["mtmp"])
                    if n < 3:
                        b.tt("pool", macc[:, 0:N], macc[:, 0:N], mtmp[:, 0:N], ALU.add, ["macc", "mtmp"], ["macc"])
                    else:
                        b.tt("pool", mrg[:, dc, 0:N], macc[:, 0:N], mtmp[:, 0:N], ALU.add, ["macc", "mtmp"], ["mrg"])
        for tl in range(N // 128):
            par = oi % 2
            oi += 1
            if gi < 4:
                qt = gi * 4 + tl
                src = L["xext"][(2 + qt) * 128:(3 + qt) * 128, :]
                dst = L["xnew"][qt * 128:(qt + 1) * 128, :]
                wi = 0
            else:
                src = L["ctx"][tl * 128:(tl + 1) * 128, :]
                dst = L["ctxnew"][tl * 128:(tl + 1) * 128, :]
                wi = 1
            b.dma("sp", xres[par], src, W=["xres%d" % par])
            for half in range(2):
                ps_, pres = rotO.next()
                for dc in range(8):
                    b.mm(ps_, mrg[:, dc, tl * 128:(tl + 1) * 128], wout[:, dc, half * 512:(half + 1) * 512], dc == 0, dc == 7,
                         ["mrg", "wout"], [pres])
                b.tt("dve", xo[par][:, half * 512:(half + 1) * 512], ps_, P["gate_rep"][wi][:, half * 512:(half + 1) * 512], ALU.mult,
                     [pres], ["xo%d_%d" % (par, half)])
                b.tt("pool", xo[par][:, half * 512:(half + 1) * 512], xo[par][:, half * 512:(half + 1) * 512],
                     xres[par][:, half * 512:(half + 1) * 512], ALU.add, ["xo%d_%d" % (par, half), "xres%d" % par], ["xo%d_%d" % (par, half)])
            b.dma("sp", dst, xo[par], ["xo%d_0" % par, "xo%d_1" % par], ["out"])


SCRATCH = {
    "kTC": ([128, NALL * 128], BF16), "vC": ([NALL * 128, 256], BF16),
    "kTD": ([4, 128, NALL * 128], BF16), "vD": ([NALL * 128, 1024], BF16),
    "kTA": ([128, NEXT * 128], BF16), "vA": ([NEXT * 128, 256], BF16),
    "kTB": ([4, 128, NEXT * 128], BF16), "vB": ([NEXT * 128, 1024], BF16),
    "qTA": ([NQT, 128, 512], BF16), "qTB": ([NQT, 128, 512], BF16), "qTC": ([NQT, 128, 512], BF16),
    "qTD": ([5, 128, 4, 512], BF16),
    "gpT": ([16, 128, NTOKQ], BF16), "mgT": ([32, 128, NTOKQ], BF16), "yT": ([16, 128, NTOKQ], BF16),
}
SHARED_INPUTS = {
    "xall": [64 * 128, D_MODEL], "xext": [20 * 128, D_MODEL], "ctx": [256, D_MODEL],
    "cosall": [NALL * 128, 64], "sinall": [NALL * 128, 64], "cosext": [NEXT * 128, 64], "sinext": [NEXT * 128, 64],
    "cT": [128, 8], "ccT": [128, 8], "tabA": [4, 128, 512], "halo_sel": [128, 6],
}
PER_LAYER_INPUTS = {
    "normwT": [128, 8], "badaT": [128, 24], "bada_rep": [128, 3072],
    "w_ada_r": [128, KC, 3072], "w_in_r": [128, KC, IN_COLS], "w_br_r": [128, 16, 1024], "w_out_r": [128, KC, 1024],
    "gains": [128, 512], "lam_rep": [128, 256], "sublnT": [128, 1], "lconst": [128, 2], "sink_rep": [1, 1024],
    "tabB": [5, 128, 2 * 6 * 512],
}
GROUPS = [[0, 1, 2, 3], [4, 5, 6, 7]]


def build_program(dbg=None, dbg_out=()):
    nc = bass.Bass("TRN2", target_bir_lowering=False)
    es = ExitStack()
    S = Sched()
    b = Bld(nc, S)
    shared = {}
    for name, shape in SHARED_INPUTS.items():
        shared[name] = nc.dram_tensor(name, shape, F32, kind="ExternalInput").ap()
    for name, (shape, dt) in SCRATCH.items():
        kind = "ExternalOutput" if name in dbg_out else "Internal"
        shared[name] = nc.dram_tensor(name, shape, dt, kind=kind).ap()
    Ls = []
    for l in range(2):
        L = dict(shared)
        for name, shape in PER_LAYER_INPUTS.items():
            L[name] = nc.dram_tensor("%s_%d" % (name, l), shape, F32, kind="ExternalInput").ap()
        Ls.append(L)
    x1own = nc.dram_tensor("x1own", [16 * 128, D_MODEL], F32, kind="Internal").ap()
    x1all = nc.dram_tensor("x1all", [16, 4 * 128, D_MODEL], F32, kind="Internal").ap()
    xext2 = nc.dram_tensor("xext2", [20 * 128, D_MODEL], F32, kind="Internal").ap()
    ctx1 = nc.dram_tensor("ctx1", [256, D_MODEL], F32, kind="Internal").ap()
    ctx2 = nc.dram_tensor("ctx2", [256, D_MODEL], F32, kind="Internal").ap()
    out = nc.dram_tensor("out", [16 * 128, D_MODEL], F32, kind="ExternalOutput").ap()
    Ls[0]["xnew"], Ls[0]["ctxnew"] = x1own, ctx1
    x1tile = lambda n: x1all[n % 16, (n // 16) * 128:(n // 16 + 1) * 128, :]
    xall0 = shared["xall"]
    Ls[0]["xall_tile"] = lambda n: xall0[n * 128:(n + 1) * 128, :]
    Ls[1]["xall_tile"] = x1tile
    Ls[1]["xext"], Ls[1]["ctx"] = xext2, ctx1
    Ls[1]["xnew"], Ls[1]["ctxnew"] = out, ctx2

    sb = lambda name, shape, dt: es.enter_context(nc.sbuf_tensor("sb_" + name, shape, dt))[:]
    T = {}
    T["ident"] = sb("ident", [128, 128], BF16)
    ones_b = sb("ones_b", [128, 128], BF16)
    T["arenaB"] = Arena(sb("arenaB", [128, 61440], BF16), 61440)
    T["arenaF"] = Arena(sb("arenaF", [128, 13500], F32), 13500)
    P = {}
    P["gains"] = sb("gains", [128, 512], F32)
    P["sT"] = [sb("sT%d" % i, [128, 8], F32) for i in range(2)]
    P["shT"] = [sb("shT%d" % i, [128, 8], F32) for i in range(2)]
    P["gate_rep"] = [sb("gate_rep%d" % i, [128, 1024], F32) for i in range(2)]
    P["esink"] = sb("esink", [1, 1024], BF16)
    P["sinkV"] = sb("sinkV", [1, 128], BF16)
    P["neglam"] = sb("neglam", [128, 1], F32)
    P["subln"] = sb("subln", [128, 1], F32)
    P["epsc"] = sb("epsc", [128, 1], F32)
    P["onesF"] = sb("onesF", [128, 128], F32)
    T["persist"] = P
    psw = es.enter_context(nc.psum_tensor("psw", [128, 4096], F32))[:]
    ps = [psw[:, i * 512:(i + 1) * 512] for i in range(8)]
    T["PS"] = ps
    T["PSW"] = psw
    T["PSB"] = [p.bitcast(BF16) for p in ps]

    b.memset("pool", ones_b, 1.0, W=["ones_b"])
    S.add("pool", lambda e: e.affine_select(out=T["ident"], in_=ones_b, pattern=[[-1, 128]], compare_op=ALU.is_equal,
                                            fill=0.0, base=0, channel_multiplier=1), ["ones_b"], ["ident"])
    b.memset("pool", P["epsc"], EPS, W=["epsc"])
    b.memset("pool", P["onesF"], 1.0, W=["onesF"])

    if dbg == "xchg":
        b.dma("sp", x1own, shared["xext"][2 * 128:18 * 128, :], W=["x1own"])
    else:
        build_layer(nc, S, b, T, Ls[0], need_ctx=True, dbg=dbg)
        if dbg is not None:
            return nc, S, es

    S.barrier()
    T["arenaB"].reset()
    T["arenaF"].reset()
    AFp = T["arenaF"]
    for t in range(16):
        S.add("pool", lambda e, t=t: e.collective_compute("AllGather", ALU.bypass, replica_groups=GROUPS,
                                                          ins=[x1own[t * 128:(t + 1) * 128, :].opt()], outs=[x1all[t].opt()]),
              (), ["x1all"], cc=True)
    b.dma("sp", xext2[2 * 128:18 * 128, :], x1own, W=["xext2own"])
    hs = AFp.get(6)
    b.dma("sp", hs, shared["halo_sel"], W=["hs"])
    cand = [AFp.get(1024) for _ in range(3)]
    hacc = [AFp.get(1024) for _ in range(2)]
    k = 0
    for side in range(2):
        for j in range(2):
            for r in range(3):
                rr = r if side == 0 else r + 1
                tile_ = 16 * rr + 14 + j if side == 0 else 16 * rr + j
                b.dma("sp", cand[r], x1tile(tile_), ["x1all"], ["cand%d" % r])
            acc_ = hacc[k % 2]
            ares = "hacc%d" % (k % 2)
            k += 1
            b.ts("dve", acc_, cand[0], hs[:, side * 3:side * 3 + 1], None, ALU.mult, R=["cand0", "hs"], W=[ares])
            for r in (1, 2):
                b.stt("dve", acc_, cand[r], hs[:, side * 3 + r:side * 3 + r + 1], acc_, ALU.mult, ALU.add,
                      ["cand%d" % r, "hs", ares], [ares])
            e_ = j if side == 0 else 18 + j
            b.dma("sp", xext2[e_ * 128:(e_ + 1) * 128, :], acc_, [ares], ["xext2h"])

    if dbg == "xchg":
        S.barrier()
        b.dma("sp", out, xext2[0:16 * 128, :], W=["out"])
        return nc, S, es
    build_layer(nc, S, b, T, Ls[1], need_ctx=False, dbg=None)
    return nc, S, es


def launch(nc, S, es, in_maps, trace=False):
    sems = {e: es.enter_context(nc.semaphore("sem_" + e)) for e in ENGS}
    dma_sems = {e: [es.enter_context(nc.semaphore("dsem_%s_%d" % (e, i))) for i in range(NSLOT)]
                for e in ("sp", "act", "pool")}
    dma_sems["cc"] = es.enter_context(nc.semaphore("cc_sem"))
    S.finalize()
    with nc.Block() as block:
        @block.sync
        def _(e):
            S.emit_engine("sp", e, sems, dma_sems, final_wait=True)

        @block.scalar
        def _(e):
            S.emit_engine("act", e, sems, dma_sems, final_wait=True)

        @block.vector
        def _(e):
            S.emit_engine("dve", e, sems, dma_sems)

        @block.gpsimd
        def _(e):
            S.emit_engine("pool", e, sems, dma_sems, final_wait=True)

        @block.tensor
        def _(e):
            S.emit_engine("pe", e, sems, dma_sems)


def rope_tables():
    t = np.arange(SEQ)
    pos = np.stack([t // GRID_W, t % GRID_W], -1).astype(np.float32)
    nf = 16
    freqs = (np.float32(10000.0) ** (-np.arange(nf, dtype=np.float32) / np.float32(nf))).astype(np.float32)
    ang = pos[:, :, None] * freqs[None, None, :]
    ang = np.concatenate([ang, ang], -1).reshape(SEQ, 64).astype(np.float32)
    cos = np.cos(ang).astype(np.float32)
    sin = np.sin(ang).astype(np.float32).reshape(SEQ, 2, 2, 16).copy()
    sin[:, :, 0, :] *= -1.0
    return cos, sin.reshape(SEQ, 64)


def tab_a(s):
    j = np.arange(128)[:, None]
    q = np.arange(128)[None, :]
    m1 = np.where(j >= q, 0.0, NEG).astype(np.float32)
    p1 = np.where(j <= q, 0.0, NEG).astype(np.float32)
    full = np.full((128, 128), NEG, np.float32)
    cases = [full if s == 0 else m1, m1, p1, full if s == 3 else p1]
    return np.stack([np.tile(c, (1, 4)) for c in cases], 0)


DL_OF = {0: list(range(-2, 4)), 1: list(range(-2, 3)), 2: list(range(-2, 3)), 3: list(range(-2, 3)), 4: list(range(-3, 3))}


def tab_b(s, rpb):
    out = np.full((5, 128, 2, 6, 4, 128), NEG, np.float32)
    j = np.arange(128)[:, None]
    q = np.arange(128)[None, :]
    for case, i in enumerate((0, 1, 5, 14, 15)):
        n = 16 * s + i
        for si, dl in enumerate(DL_OF[case]):
            kt = n + dl
            if kt < 0 or kt > 63:
                continue
            r = 2 * n + q // 64
            qc = q % 64
            kr = 2 * kt + j // 64
            kc = j % 64
            rstart = np.clip(r - 4, 0, 120)
            cstart = np.clip(qc - 8, 0, 48)
            valid = (kr >= rstart) & (kr < rstart + 8) & (kc >= cstart) & (kc < cstart + 16)
            dr = np.clip(kr - r + 7, 0, 14)
            dc = np.clip(kc - qc, -15, 15) + 15
            for h in range(8):
                vals = rpb[h][dr, dc]
                out[case, :, h % 2, si, h // 2, :] = np.where(valid, vals, NEG)
    return out.reshape(5, 128, 2 * 6 * 512)


def prep_inputs(inp, consts):
    cos, sin = consts["cos"], consts["sin"]
    f = lambda a: np.ascontiguousarray(a, dtype=np.float32)
    r8 = lambda v: f(v.reshape(8, 128).T)
    x, ctx = inp["x"], inp["ctx"]
    shared = {
        "ccT": r8(inp["c_ctx"]),
        "cosall": f(np.concatenate([cos, np.ones((256, 64), np.float32)], 0)),
        "sinall": f(np.concatenate([sin, np.zeros((256, 64), np.float32)], 0)),
    }
    for l in range(2):
        lam_init = 0.8 - 0.6 * math.exp(-0.3 * l)
        per = {
            "normwT": r8(inp["norm_w"][l]),
            "badaT": f(inp["b_ada"][l].reshape(24, 128).T),
            "bada_rep": f(np.broadcast_to(inp["b_ada"][l][None, :], (128, 3072))),
            "w_ada_r": f(inp["w_ada"][l].reshape(8, 128, 3072).transpose(1, 0, 2)),
            "w_in_r": f(inp["w_in"][l].reshape(8, 128, IN_COLS).transpose(1, 0, 2)),
            "w_br_r": f(inp["w_br"][l].reshape(16, 128, 1024).transpose(1, 0, 2)),
            "w_out_r": f(inp["w_out"][l].reshape(8, 128, 1024).transpose(1, 0, 2)),
            "gains": f(np.broadcast_to(inp["qk_gain"][l].reshape(1, 512), (128, 512))),
            "lam_rep": f(np.broadcast_to(inp["lam_d"][l].reshape(1, 256), (128, 256))),
            "sublnT": f(inp["subln_d"][l].reshape(128, 1)),
            "lconst": f(np.broadcast_to(np.array([[lam_init, 1.0 - lam_init]], np.float32), (128, 2))),
            "sink_rep": f(np.repeat(inp["sink_a"][l], 128).reshape(1, 1024)),
        }
        for k_, v in per.items():
            shared["%s_%d" % (k_, l)] = v
    maps = []
    for core in range(8):
        bi, s = core // 4, core % 4
        m = dict(shared)
        m["xall"] = f(x[bi])
        m["ctx"] = f(ctx[bi])
        m["cT"] = r8(inp["c"][bi])
        lo, hi = (16 * s - 2) * 128, (16 * s + 18) * 128
        xe = np.zeros((20 * 128, D_MODEL), np.float32)
        ce = np.ones((NEXT * 128, 64), np.float32)
        se = np.zeros((NEXT * 128, 64), np.float32)
        a, z = max(lo, 0), min(hi, SEQ)
        xe[a - lo:z - lo] = x[bi][a:z]
        ce[a - lo:z - lo] = cos[a:z]
        se[a - lo:z - lo] = sin[a:z]
        m["xext"], m["cosext"], m["sinext"] = xe, ce, se
        m["tabA"] = consts["tabA"][s]
        hsel = np.zeros((128, 6), np.float32)
        if s > 0:
            hsel[:, s - 1] = 1.0
        if s < 3:
            hsel[:, 3 + s] = 1.0
        m["halo_sel"] = hsel
        for l in range(2):
            m["tabB_%d" % l] = tab_b(s, inp["rpb_b"][l])
        maps.append(m)
    return maps


_PROG = {}


def get_program():
    if "p" not in _PROG:
        nc, S, es = build_program()
        launch(nc, S, es, None)
        _PROG["p"] = (nc, S, es)
    return _PROG["p"]


def kernel(**inputs):
    inp = {k: np.asarray(v, dtype=np.float32) for k, v in inputs.items()}
    cos, sin = rope_tables()
    consts = {"cos": cos, "sin": sin, "tabA": [tab_a(s) for s in range(4)]}
    nc, S, es = get_program()
    maps = prep_inputs(inp, consts)
    res = run_bass_kernel_spmd(nc, maps, core_ids=list(range(8)))
    outs = res.results
    x = np.stack([np.concatenate([outs[bi * 4 + s]["out"] for s in range(4)], 0) for bi in range(2)], 0)
    return x.astype(np.float32)
```

```python
import math
import numpy as np
from contextlib import ExitStack
import concourse.bass as bass
import concourse.mybir as mybir
from concourse.bass_utils import run_bass_kernel_spmd

F32 = mybir.dt.float32
BF16 = mybir.dt.bfloat16
AF = mybir.ActivationFunctionType
ALU = mybir.AluOpType
AX = mybir.AxisListType

D_MODEL = 1024
SEQ = 8192
CTX_LEN = 256
GRID_W = 64
EPS = 1e-6
NEG = -30000.0
KC = 8
NALL = 66
NEXT = 22
NQT = 18
QT_EXT = list(range(2, 18)) + [20, 21]
NTOKQ = NQT * 128
IN_COLS = 10752

ENGS = ("sp", "act", "dve", "pool", "pe")
NSLOT = 8
SAME_ENG_SYNC = True
import os as _os
OPLIMIT = int(_os.environ.get("K_OPLIMIT", "1000000000"))


class Sched:
    def __init__(self):
        self.ops = {e: [] for e in ENGS}
        self.last_w = {}
        self.readers = {}
        self.ndma = {e: 0 for e in ENGS}
        self.bar = set()
        self.bar_pending = {e: False for e in ENGS}

    def add(self, eng, fn, reads=(), writes=(), dma=False, cc=False):
        self.total = getattr(self, "total", 0) + 1
        if self.total > OPLIMIT:
            return None
        idx = len(self.ops[eng])
        me = (eng, idx)
        deps = set()
        for r in reads:
            if r in self.last_w:
                deps.add(self.last_w[r])
        for w in writes:
            if w in self.last_w:
                deps.add(self.last_w[w])
            for rd in self.readers.get(w, ()):
                deps.add(rd)
        if self.bar_pending[eng]:
            deps |= self.bar
            self.bar_pending[eng] = False
        deps.discard(me)
        op = dict(fn=fn, deps=deps, dma=dma or cc, idx=idx, signal=False, cc=cc)
        if cc:
            self.ncc = getattr(self, "ncc", 0) + 1
            op["cc_target"] = self.ncc
        if dma and not cc:
            k = self.ndma[eng]
            self.ndma[eng] += 1
            op["slot"] = k % NSLOT
            op["target"] = 16 * (k // NSLOT + 1)
        self.ops[eng].append(op)
        for w in writes:
            self.last_w[w] = me
            self.readers[w] = []
        for r in reads:
            if r not in writes:
                self.readers.setdefault(r, []).append(me)
        return me

    def barrier(self):
        bar = set()
        for e in ENGS:
            if self.ops[e]:
                bar.add((e, len(self.ops[e]) - 1))
            cnt = 0
            for op in reversed(self.ops[e]):
                if op["dma"]:
                    bar.add((e, op["idx"]))
                    if not op["cc"]:
                        cnt += 1
                    if cnt >= NSLOT:
                        break
        self.bar = bar
        self.bar_pending = {e: True for e in ENGS}
        self.last_w = {}
        self.readers = {}

    def finalize(self):
        for e in ENGS:
            for op in self.ops[e]:
                keep = set()
                for (pe, pi) in op["deps"]:
                    prod = self.ops[pe][pi]
                    if pe == e and not prod["dma"]:
                        if e == "pe" or not SAME_ENG_SYNC:
                            continue
                    keep.add((pe, pi))
                    if not prod["dma"]:
                        prod["signal"] = True
                op["deps"] = keep
        for e in ENGS:
            c = 0
            for op in self.ops[e]:
                if op["signal"]:
                    c += 1
                    op["sigval"] = c

    def emit_engine(self, e, eng, sems, dma_sems, final_wait=False):
        seen = {}

        def wait(sem, val, key):
            if seen.get(key, 0) < val:
                eng.wait_ge(sem, val)
                seen[key] = val

        for op in self.ops[e]:
            for (pe, pi) in sorted(op["deps"]):
                prod = self.ops[pe][pi]
                if prod["cc"]:
                    wait(dma_sems["cc"], prod["cc_target"], "cc")
                elif prod["dma"]:
                    wait(dma_sems[pe][prod["slot"]], prod["target"], (pe, prod["slot"]))
                else:
                    wait(sems[pe], prod["sigval"], pe)
            if op["dma"] and not op["cc"] and op["target"] > 16:
                wait(dma_sems[e][op["slot"]], op["target"] - 16, (e, op["slot"]))
            ins = op["fn"](eng)
            if op["cc"]:
                ins.then_inc(dma_sems["cc"])
            elif op["dma"]:
                ins.then_inc(dma_sems[e][op["slot"]], 16)
            elif op["signal"]:
                ins.then_inc(sems[e], 1)
        if final_wait:
            last = {}
            for op in self.ops[e]:
                if op["dma"] and not op["cc"]:
                    last[op["slot"]] = op["target"]
            for slot, tgt in last.items():
                wait(dma_sems[e][slot], tgt, (e, slot))


class Bld:
    def __init__(self, nc, S):
        self.nc = nc
        self.S = S

    def dma(self, eng, out, in_, R=(), W=()):
        return self.S.add(eng, lambda e: e.dma_start(out=out, in_=in_), R, W, dma=True)

    def tt(self, eng, out, in0, in1, op, R=(), W=()):
        return self.S.add(eng, lambda e: e.tensor_tensor(out=out, in0=in0, in1=in1, op=op), R, W)

    def ts(self, eng, out, in0, s1, s2, op0, op1=None, R=(), W=()):
        if op1 is None:
            return self.S.add(eng, lambda e: e.tensor_scalar(out=out, in0=in0, scalar1=s1, scalar2=None, op0=op0), R, W)
        return self.S.add(eng, lambda e: e.tensor_scalar(out=out, in0=in0, scalar1=s1, scalar2=s2, op0=op0, op1=op1), R, W)

    def stt(self, eng, out, in0, scalar, in1, op0, op1, R=(), W=()):
        return self.S.add(eng, lambda e: e.scalar_tensor_tensor(out=out, in0=in0, scalar=scalar, in1=in1, op0=op0, op1=op1), R, W)

    def act(self, out, in_, func, R=(), W=(), scale=1.0, bias=0.0, accum=None):
        if accum is not None:
            return self.S.add("act", lambda e: e.activation(out=out, in_=in_, func=func, bias=bias, scale=scale, accum_out=accum), R, W)
        return self.S.add("act", lambda e: e.activation(out=out, in_=in_, func=func, bias=bias, scale=scale), R, W)

    def copy(self, eng, out, in_, R=(), W=()):
        if eng == "act":
            return self.S.add("act", lambda e: e.copy(out=out, in_=in_), R, W)
        return self.S.add(eng, lambda e: e.tensor_copy(out=out, in_=in_), R, W)

    def mm(self, out, lhsT, rhs, start, stop, R=(), W=(), skip=False):
        if skip:
            return self.S.add("pe", lambda e: e.matmul(out, lhsT=lhsT, rhs=rhs, start=start, stop=stop, skip_group_check=True), R, W)
        return self.S.add("pe", lambda e: e.matmul(out, lhsT=lhsT, rhs=rhs, start=start, stop=stop), R, W)

    def tr(self, out, in_, ident, R=(), W=()):
        return self.S.add("pe", lambda e: e.transpose(out=out, in_=in_, identity=ident), R, W)

    def red(self, eng, out, in_, R=(), W=()):
        return self.S.add(eng, lambda e: e.reduce_sum(out=out, in_=in_, axis=AX.X), R, W)

    def recip(self, out, in_, R=(), W=()):
        return self.S.add("dve", lambda e: e.reciprocal(out=out, in_=in_), R, W)

    def memset(self, eng, ap, val, W=()):
        return self.S.add(eng, lambda e: e.memset(ap, val), (), W)


class Rot:
    def __init__(self, items):
        self.items = items
        self.i = 0

    def next(self):
        it = self.items[self.i % len(self.items)]
        self.i += 1
        return it


class Arena:
    def __init__(self, ap, n):
        self.ap = ap
        self.n = n
        self.off = 0

    def reset(self):
        self.off = 0

    def get(self, n):
        assert self.off + n <= self.n, ("arena overflow", self.off, n, self.n)
        v = self.ap[:, self.off:self.off + n]
        self.off += n
        return v


def build_layer(nc, S, b, T, L, need_ctx, dbg=None):
    ident = T["ident"]
    PS = T["PS"]
    PSB = T["PSB"]
    AB = T["arenaB"]
    AFp = T["arenaF"]
    P = T["persist"]
    nqt = NQT if need_ctx else 16
    ngrp = 5 if need_ctx else 4

    def phase():
        S.barrier()
        AB.reset()
        AFp.reset()

    phase()
    cT = AFp.get(8)
    ccT = AFp.get(8)
    sc = AFp.get(8)
    scc = AFp.get(8)
    nw = AFp.get(8)
    badaT = AFp.get(24)
    lamr = AFp.get(256)
    lamp = AFp.get(128)
    lamv = AFp.get(4)
    lconst = AFp.get(2)
    sinkf = AFp.get(1024)
    screp = [AFp.get(1024), AFp.get(1024)]
    wa = [AFp.get(KC * 512), AFp.get(KC * 512)]
    bar = [AFp.get(512), AFp.get(512)]
    tmpm = AFp.get(16)

    b.dma("sp", cT, L["cT"], W=["cT"])
    b.dma("sp", ccT, L["ccT"], W=["ccT"])
    b.dma("sp", nw, L["normwT"], W=["nw"])
    b.dma("sp", badaT, L["badaT"], W=["badaT"])
    b.dma("sp", P["gains"], L["gains"], W=["gains"])
    b.dma("sp", lamr, L["lam_rep"], W=["lamr"])
    b.dma("sp", P["subln"], L["sublnT"], W=["subln"])
    b.dma("sp", lconst, L["lconst"], W=["lconst"])
    b.dma("sp", sinkf[0:1, :], L["sink_rep"], W=["sinkf"])
    g4 = P["gains"].rearrange("p (m t d) -> p m t d", m=4, t=2)
    b.ts("dve", g4[:, :, 0, :], g4[:, :, 0, :], 0.125, None, ALU.mult, R=["gains"], W=["gains"])
    b.act(P["esink"][0:1, :], sinkf[0:1, :], AF.Exp, ["sinkf"], ["esink"])
    b.memset("pool", P["sinkV"][0:1, 0:64], 0.0, W=["sinkV0"])
    b.memset("pool", P["sinkV"][0:1, 64:128], 1.0, W=["sinkV1"])
    l4 = lamr.rearrange("p (a d) -> p a d", a=4)
    lp = lamp.rearrange("p (a d) -> p a d", a=2)
    b.tt("dve", lp[:, 0, :], l4[:, 0, :], l4[:, 1, :], ALU.mult, ["lamr"], ["lamp0"])
    b.tt("dve", lp[:, 1, :], l4[:, 2, :], l4[:, 3, :], ALU.mult, ["lamr"], ["lamp1"])
    b.red("dve", lamv[:, 0:2], lp, ["lamp0", "lamp1"], ["lamv"])
    b.act(lamv[:, 0:2], lamv[:, 0:2], AF.Exp, ["lamv"], ["lamv"])
    b.tt("dve", lamv[:, 2:3], lamv[:, 1:2], lamv[:, 0:1], ALU.subtract, ["lamv"], ["lamv2"])
    b.tt("dve", P["neglam"], lamv[:, 2:3], lconst[:, 0:1], ALU.subtract, ["lamv2", "lconst"], ["neglam"])
    b.tt("dve", P["subln"], P["subln"], lconst[:, 1:2], ALU.mult, ["subln", "lconst"], ["subln"])
    b.act(sc, cT, AF.Silu, ["cT"], ["sc"])
    b.act(scc, ccT, AF.Silu, ["ccT"], ["scc"])
    for wi, s_ in enumerate((sc, scc)):
        b.copy("dve", screp[wi].rearrange("p (k m) -> p k m", k=KC), s_.unsqueeze(2).to_broadcast([128, KC, 128]),
               ["sc", "scc"], ["screp%d" % wi])
    psmod = PS[0]
    for blk in range(6):
        wab = wa[blk % 2]
        wres = "wa%d" % (blk % 2)
        wab3 = wab.rearrange("p (k n) -> p k n", k=KC)
        b.dma("sp", wab3, L["w_ada_r"][:, :, blk * 512:(blk + 1) * 512], W=[wres])
        if blk < 4:
            for wi, s_ in enumerate((sc, scc)):
                for j in range(4):
                    col = wi * 16 + blk * 4 + j
                    for k in range(KC):
                        b.mm(psmod[:, col:col + 1], wab3[:, k, j * 128:(j + 1) * 128], s_[:, k:k + 1], k == 0, k == KC - 1,
                             [wres, "sc", "scc"], ["psmod"])
        else:
            brp = bar[blk % 2]
            bres = "bar%d" % (blk % 2)
            b.dma("sp", brp, L["bada_rep"][:, blk * 512:(blk + 1) * 512], W=[bres])
            for wi in range(2):
                ps_, pres = PS[1 + wi], "psg%d" % wi
                for k in range(KC):
                    b.mm(ps_, screp[wi].rearrange("p (k m) -> p k m", k=KC)[:, k, :], wab3[:, k, :], k == 0, k == KC - 1,
                         [wres, "screp%d" % wi], [pres])
                dst = P["gate_rep"][wi][:, (blk - 4) * 512:(blk - 3) * 512]
                b.tt("dve", dst, ps_, brp, ALU.add, [pres, bres], ["gate_rep%d_%d" % (wi, blk)])
    for wi in range(2):
        b.tt("dve", P["shT"][wi], psmod[:, wi * 16:wi * 16 + 8], badaT[:, 0:8], ALU.add, ["psmod", "badaT"], ["shT%d" % wi])
        b.stt("dve", tmpm[:, 0:8], psmod[:, wi * 16 + 8:wi * 16 + 16], 1.0, badaT[:, 8:16], ALU.add, ALU.add,
              ["psmod", "badaT"], ["tmpm"])
        b.tt("dve", P["sT"][wi], tmpm[:, 0:8], nw, ALU.mult, ["tmpm", "nw"], ["sT%d" % wi])

    if dbg == "mod":
        return

    def make_hT(lane, xsrc, wi, hT_out, Wres):
        xt, xs, ssb = lane["xt"], lane["xs"], lane["ss"]
        ln = lane["name"]
        b.dma("sp", xt, xsrc, W=[ln + "xt"])
        b.act(lane["junk"], xt, AF.Square, [ln + "xt"], [ln + "junk", ln + "ss"], accum=ssb[:, 0:1])
        b.act(ssb[:, 1:2], ssb[:, 0:1], AF.Sqrt, [ln + "ss"], [ln + "ss1"], scale=1.0 / D_MODEL, bias=P["epsc"])
        b.recip(ssb[:, 2:3], ssb[:, 1:2], [ln + "ss1"], [ln + "ss2"])
        b.ts("dve", xs, xt, ssb[:, 2:3], None, ALU.mult, R=[ln + "xt", ln + "ss2"], W=[ln + "xs"])
        pt, ptres = T["rotT"].next()
        for k in range(KC):
            b.tr(pt[:, k * 128:(k + 1) * 128], xs[:, k * 128:(k + 1) * 128], ident, [ln + "xs"], [ptres])
        for k in range(KC):
            b.ts("dve", hT_out[:, k, :], pt[:, k * 128:(k + 1) * 128], P["sT"][wi][:, k:k + 1], P["shT"][wi][:, k:k + 1],
                 ALU.mult, ALU.add, R=[ptres, "sT%d" % wi, "shT%d" % wi], W=[Wres + "_e"])
        b.copy("dve", lane["ss"][:, 3:4], lane["ss"][:, 2:3], [ln + "ss2"], [Wres + "_o"])

    def qk_post(lane, src, sres, G, gain, cs, out2d, Wres, perm=False):
        ln = lane["name"]
        n = G * 64
        xf = lane["xf"][:, 0:n]
        sq = lane["sq"][:, 0:n]
        t1 = lane["t1"][:, 0:n]
        t2 = lane["t2"][:, 0:n]
        ssq = lane["ssq"]
        v3 = lambda a: a.rearrange("p (g d) -> p g d", g=G)
        b.copy("act", xf, src, [sres], [ln + "xf"])
        b.tt("pool", sq, xf, xf, ALU.mult, [ln + "xf"], [ln + "sq"])
        b.red("dve", ssq[:, 0:G], v3(sq), [ln + "sq"], [ln + "ssq"])
        b.act(ssq[:, 8:8 + G], ssq[:, 0:G], AF.Sqrt, [ln + "ssq"], [ln + "ssq1"], scale=1.0 / 64, bias=P["epsc"])
        b.recip(ssq[:, 16:16 + G], ssq[:, 8:8 + G], [ln + "ssq1"], [ln + "ssq2"])
        b.tt("dve", v3(sq), v3(xf), ssq[:, 16:16 + G].unsqueeze(2).to_broadcast([128, G, 64]), ALU.mult,
             [ln + "xf", ln + "ssq2"], [ln + "sq"])
        gb = gain.unsqueeze(1).to_broadcast([128, G, 64])
        if cs is None:
            b.tt("pool", v3(out2d), v3(sq), gb, ALU.mult, [ln + "sq", "gains"], [Wres])
            return
        cos, sin, csres = cs
        b.tt("pool", v3(xf), v3(sq), gb, ALU.mult, [ln + "sq", "gains"], [ln + "xf"])
        b.tt("dve", v3(t1), v3(xf), cos.unsqueeze(1).to_broadcast([128, G, 64]), ALU.mult, [ln + "xf"] + csres, [ln + "t1"])
        v5 = lambda a: a.rearrange("p (g a h d) -> p g a h d", g=G, a=2, h=2)
        s4 = sin.rearrange("p (a h d) -> p a h d", a=2, h=2)
        for hh in range(2):
            b.tt("pool", v5(t2)[:, :, :, hh, :], v5(xf)[:, :, :, 1 - hh, :],
                 s4[:, :, hh, :].unsqueeze(1).to_broadcast([128, G, 2, 16]), ALU.mult, [ln + "xf"] + csres, [ln + "t2%d" % hh])
        if perm:
            o = out2d.rearrange("p (i g d) -> p g i d", i=4, g=2)
            a1 = t1.rearrange("p (g i d) -> p g i d", g=2, i=4)
            a2 = t2.rearrange("p (g i d) -> p g i d", g=2, i=4)
        else:
            o, a1, a2 = v3(out2d), v3(t1), v3(t2)
        b.tt("dve", o, a1, a2, ALU.add, [ln + "t1", ln + "t20", ln + "t21"], [Wres])

    def mk_lane(name):
        return dict(name=name, xt=AFp.get(1024), junk=AFp.get(1024), ss=AFp.get(4), xs=AB.get(1024),
                    xf=AFp.get(512), sq=AFp.get(512), t1=AFp.get(512), t2=AFp.get(512), ssq=AFp.get(24))

    def load_cs(tabc, tabs, row0, buf, res):
        b.dma("sp", buf[:, 0:64], tabc[row0:row0 + 128, :], W=[res + "c"])
        b.dma("sp", buf[:, 64:128], tabs[row0:row0 + 128, :], W=[res + "s"])

    gq = lambda m: P["gains"][:, (m * 2) * 64:(m * 2 + 1) * 64]
    gk = lambda m: P["gains"][:, (m * 2 + 1) * 64:(m * 2 + 2) * 64]

    phase()
    Wkv = AB.get(KC * 1280).rearrange("p (k n) -> p k n", k=KC)
    b.dma("pool", Wkv[:, :, 0:256], L["w_in_r"][:, :, 3840:4096], W=["Wkv"])
    b.dma("pool", Wkv[:, :, 256:1280], L["w_in_r"][:, :, 5120:6144], W=["Wkv"])
    lanes = [mk_lane("la0"), mk_lane("la1")]
    hTb = [AB.get(KC * 128).rearrange("p (k t) -> p k t", k=KC) for _ in range(2)]
    csb = [AFp.get(128), AFp.get(128)]
    knC = AB.get(512)
    knD = AB.get(2048)
    vCs = AB.get(4 * 256)
    vDs = AB.get(4 * 1024)
    kTs = AB.get(512)
    kTDs = AB.get(2048)
    b.memset("pool", vCs, 1.0, W=["vCs"])
    b.memset("pool", vDs, 1.0, W=["vDs"])
    rotM = Rot([(PS[2], "M0"), (PS[3], "M1"), (PS[4], "M2"), (PS[5], "M3")])
    rotK = Rot([(PSB[6], "K0"), (PSB[7], "K1")])
    T["rotT"] = Rot([(PSB[0], "T0"), (PSB[1], "T1")])
    all_tiles = []
    for g in range(17):
        if dbg == "all1" and g >= 1:
            break
        for tl in range(4 if g < 16 else 2):
            all_tiles.append((g, tl, g * 4 + tl))

    def hT_part(ti):
        g, tl, t = all_tiles[ti]
        if t < 64:
            src, wi = L["xall_tile"](t), 0
        else:
            src, wi = L["ctx"][(t - 64) * 128:(t - 63) * 128, :], 1
        load_cs(L["cosall"], L["sinall"], t * 128, csb[ti % 2], "cs%d" % (ti % 2))
        make_hT(lanes[ti % 2], src, wi, hTb[ti % 2], "hT%d" % (ti % 2))

    def kv_part(ti):
        g, tl, t = all_tiles[ti]
        lane = lanes[ti % 2]
        hT = hTb[ti % 2]
        hres = "hT%d" % (ti % 2)
        hR = [hres + "_e", hres + "_o"]
        cbuf = csb[ti % 2]
        cres = "cs%d" % (ti % 2)
        cs = (cbuf[:, 0:64], cbuf[:, 64:128], [cres + "c", cres + "s"])
        ps_, pres = rotM.next()
        for k in range(KC):
            b.mm(ps_[:, 0:256], hT[:, k, :], Wkv[:, k, 0:256], k == 0, k == KC - 1, hR + ["Wkv"], [pres])
        qk_post(lane, ps_[:, 0:128], pres, 2, gk(2), cs, knC[:, tl * 128:(tl + 1) * 128], "knC")
        b.copy("act", vCs.rearrange("p (t g e) -> p t g e", t=4, g=2)[:, tl, :, 0:64],
               ps_[:, 128:256].rearrange("p (g d) -> p g d", g=2), [pres], ["vCs"])
        ps_, pres = rotM.next()
        for k in range(KC):
            b.mm(ps_, hT[:, k, :], Wkv[:, k, 256:768], k == 0, k == KC - 1, hR + ["Wkv"], [pres])
        qk_post(lane, ps_, pres, 8, gk(3), cs, knD[:, tl * 512:(tl + 1) * 512], "knD")
        ps_, pres = rotM.next()
        for k in range(KC):
            b.mm(ps_, hT[:, k, :], Wkv[:, k, 768:1280], k == 0, k == KC - 1, hR + ["Wkv"], [pres])
        b.copy("dve", vDs.rearrange("p (t h v e) -> p t h v e", t=4, h=4, v=2)[:, tl, :, :, 0:64],
               ps_.rearrange("p (h v d) -> p h v d", h=4, v=2), [pres], ["vDs"])

    hT_part(0)
    ti = 0
    for g in range(17):
        if dbg == "all1" and g >= 1:
            break
        nt = 4 if g < 16 else 2
        for tl in range(nt):
            if ti + 1 < len(all_tiles):
                hT_part(ti + 1)
            kv_part(ti)
            ti += 1
        tok0 = g * 512
        ntok = nt * 128
        pk, pkres = rotK.next()
        for tl in range(nt):
            b.tr(pk[:, tl * 128:(tl + 1) * 128], knC[:, tl * 128:(tl + 1) * 128], ident, ["knC"], [pkres])
        b.copy("dve", kTs[:, 0:ntok], pk[:, 0:ntok], [pkres], ["kTs"])
        b.dma("sp", L["kTC"][:, tok0:tok0 + ntok], kTs[:, 0:ntok], ["kTs"], ["kTC"])
        for hp in range(2):
            pk, pkres = rotK.next()
            for hl in range(2):
                h = hp * 2 + hl
                for tl in range(nt):
                    b.tr(pk[:, hl * 512 + tl * 128:hl * 512 + (tl + 1) * 128],
                         knD[:, tl * 512 + h * 128:tl * 512 + (h + 1) * 128], ident, ["knD"], [pkres])
            for hl in range(2):
                h = hp * 2 + hl
                b.copy("dve", kTDs[:, h * 512:h * 512 + ntok], pk[:, hl * 512:hl * 512 + ntok], [pkres], ["kTDs%d" % h])
                b.dma("sp", L["kTD"][h, :, tok0:tok0 + ntok], kTDs[:, h * 512:h * 512 + ntok], ["kTDs%d" % h], ["kTD"])
        b.dma("sp", L["vC"][tok0:tok0 + ntok, :].rearrange("(t p) c -> p t c", p=128),
              vCs.rearrange("p (t c) -> p t c", t=4)[:, 0:nt, :], ["vCs"], ["vC"])
        b.dma("sp", L["vD"][tok0:tok0 + ntok, :].rearrange("(t p) c -> p t c", p=128),
              vDs.rearrange("p (t c) -> p t c", t=4)[:, 0:nt, :], ["vDs"], ["vD"])

    if dbg in ("all", "all1"):
        return

    phase()
    hTe = AB.get(KC * NEXT * 128).rearrange("p (k t) -> p k t", k=KC)
    cse = AFp.get(NEXT * 128).rearrange("p (t c) -> p t c", t=NEXT)
    lanes = [mk_lane("le0"), mk_lane("le1")]
    T["rotT"] = Rot([(PSB[0], "T0"), (PSB[1], "T1")])
    b.dma("sp", cse[:, :, 0:64], L["cosext"].rearrange("(t p) c -> p t c", p=128), W=["csec"])
    b.dma("sp", cse[:, :, 64:128], L["sinext"].rearrange("(t p) c -> p t c", p=128), W=["cses"])
    for t in range(NEXT):
        if t < 20:
            src, wi = L["xext"][t * 128:(t + 1) * 128, :], 0
        else:
            src, wi = L["ctx"][(t - 20) * 128:(t - 19) * 128, :], 1
        make_hT(lanes[t % 2], src, wi, hTe[:, :, t * 128:(t + 1) * 128], "hTe")
    Wb = [AB.get(KC * 512).rearrange("p (k n) -> p k n", k=KC) for _ in range(2)]
    qn = [AB.get(512) for _ in range(4)]
    stg = [AB.get(512) for _ in range(4)]
    vst = [AB.get(1024), AB.get(1024)]
    fst = [AB.get(512), AB.get(512)]
    for i in range(2):
        b.memset("pool", vst[i], 1.0, W=["vst%d" % i])
    rotM = Rot([(PS[2], "M0"), (PS[3], "M1"), (PS[4], "M2"), (PS[5], "M3")])
    rotK = Rot([(PSB[6], "K0"), (PSB[7], "K1")])
    qtiles = QT_EXT[:nqt]
    blocks = [("q", 0, 0, 512), ("kvA", 0, 512, 256), ("gp", 0, 768, 512),
              ("q", 1, 1280, 512), ("kB", 1, 1792, 512), ("vB", 1, 2304, 512), ("gp", 1, 2816, 512),
              ("q", 2, 3328, 512), ("gp", 2, 4096, 512), ("q", 3, 4608, 512), ("gp", 3, 6144, 512)]
    blocks += [("mg", j, 6656 + 512 * j, 512) for j in range(8)]
    qTd = [L["qTA"], L["qTB"], L["qTC"], None]
    cnt = 0
    pending = []

    def defer(fn):
        pending.append(fn)
        while len(pending) > 2:
            pending.pop(0)()

    def flush():
        while pending:
            pending.pop(0)()

    def load_w(bi_):
        _, _, c0_, ncol_ = blocks[bi_]
        b.dma("pool", Wb[bi_ % 2][:, :, 0:ncol_], L["w_in_r"][:, :, c0_:c0_ + ncol_], W=["Wb%d" % (bi_ % 2)])

    load_w(0)
    for bi, (kind, m, c0, ncol) in enumerate(blocks):
        W_ = Wb[bi % 2]
        wres = "Wb%d" % (bi % 2)
        if bi + 1 < len(blocks):
            load_w(bi + 1)
        if kind in ("q", "kvA", "kB", "vB"):
            tl_list = qtiles if kind == "q" else list(range(NEXT))
            for qi, t in enumerate(tl_list):
                lane = lanes[cnt % 2]
                par = cnt % 2
                cnt += 1
                ps_, pres = rotM.next()
                for k in range(KC):
                    b.mm(ps_[:, 0:ncol], hTe[:, k, t * 128:(t + 1) * 128], W_[:, k, 0:ncol], k == 0, k == KC - 1, ["hTe_e", "hTe_o", wres], [pres])
                cs = (cse[:, t, 0:64], cse[:, t, 64:128], ["csec", "cses"])
                p4 = (cnt - 1) % 4
                if kind == "q":
                    qo = qn[p4]
                    qk_post(lane, ps_, pres, 8, gq(m), None if m == 1 else cs, qo, "qn%d" % p4, perm=(m in (0, 2)))

                    def tail(qo=qo, p4=p4, m=m, qi=qi):
                        pk, pkres = rotK.next()
                        for i in range(4):
                            b.tr(pk[:, i * 128:(i + 1) * 128], qo[:, i * 128:(i + 1) * 128], ident, ["qn%d" % p4], [pkres])
                        b.copy("dve", stg[p4], pk[:, 0:512], [pkres], ["stg%d" % p4])
                        if m < 3:
                            b.dma("sp", qTd[m][qi], stg[p4], ["stg%d" % p4], ["qT%d" % m])
                        else:
                            b.dma("sp", L["qTD"][qi // 4, :, :, (qi % 4) * 128:(qi % 4 + 1) * 128],
                                  stg[p4].rearrange("p (h t) -> p h t", h=4), ["stg%d" % p4], ["qT3"])
                    defer(tail)
                elif kind == "kvA":
                    qo = qn[p4]
                    qk_post(lane, ps_[:, 0:128], pres, 2, gk(0), cs, qo[:, 0:128], "qn%d" % p4)

                    def tail(qo=qo, p4=p4, t=t):
                        pk, pkres = rotK.next()
                        b.tr(pk[:, 0:128], qo[:, 0:128], ident, ["qn%d" % p4], [pkres])
                        b.copy("dve", stg[p4][:, 0:128], pk[:, 0:128], [pkres], ["stg%d" % p4])
                        b.dma("sp", L["kTA"][:, t * 128:(t + 1) * 128], stg[p4][:, 0:128], ["stg%d" % p4], ["kTA"])
                    defer(tail)
                    vv = vst[par][:, 0:256].rearrange("p (g e) -> p g e", g=2)
                    b.copy("act", vv[:, :, 0:64], ps_[:, 128:256].rearrange("p (g d) -> p g d", g=2), [pres], ["vst%d" % par])
                    b.dma("sp", L["vA"][t * 128:(t + 1) * 128, :], vst[par][:, 0:256], ["vst%d" % par], ["vA"])
                elif kind == "kB":
                    qo = qn[p4]
                    qk_post(lane, ps_, pres, 8, gk(1), None, qo, "qn%d" % p4)

                    def tail(qo=qo, p4=p4, t=t):
                        pk, pkres = rotK.next()
                        for i in range(4):
                            b.tr(pk[:, i * 128:(i + 1) * 128], qo[:, i * 128:(i + 1) * 128], ident, ["qn%d" % p4], [pkres])
                        b.copy("dve", stg[p4], pk[:, 0:512], [pkres], ["stg%d" % p4])
                        b.dma("sp", L["kTB"][:, :, t * 128:(t + 1) * 128].rearrange("i p t -> p i t"),
                              stg[p4].rearrange("p (i t) -> p i t", i=4), ["stg%d" % p4], ["kTB"])
                    defer(tail)
                else:
                    vv = vst[par].rearrange("p (h e) -> p h e", h=8)
                    b.copy("dve", vv[:, :, 0:64], ps_.rearrange("p (h d) -> p h d", h=8), [pres], ["vst%d" % par])
                    b.dma("sp", L["vB"][t * 128:(t + 1) * 128, :], vst[par], ["vst%d" % par], ["vB"])
            flush()
        else:
            func = AF.Silu if kind == "gp" else AF.Sigmoid
            for gi in range(ngrp):
                if gi < 4:
                    e0, ntok, q0 = (2 + gi * 4) * 128, 512, gi * 512
                else:
                    e0, ntok, q0 = 20 * 128, 256, 2048
                for jc in range(4):
                    par = cnt % 2
                    cnt += 1
                    ps_, pres = rotM.next()
                    for k in range(KC):
                        b.mm(ps_[:, 0:ntok], W_[:, k, jc * 128:(jc + 1) * 128], hTe[:, k, e0:e0 + ntok], k == 0, k == KC - 1,
                             ["hTe_e", "hTe_o", wres], [pres])
                    b.act(fst[par][:, 0:ntok], ps_[:, 0:ntok], func, [pres], ["fst%d" % par])
                    if kind == "gp":
                        dst = L["gpT"][m * 4 + jc, :, q0:q0 + ntok]
                    else:
                        dst = L["mgT"][(m // 2) * 8 + (m % 2) * 4 + jc, :, q0:q0 + ntok]
                    b.dma("sp", dst, fst[par][:, 0:ntok], ["fst%d" % par], ["fT"])
    if dbg == "ext":
        return

    rotS = Rot([(PS[0], "S0"), (PS[1], "S1"), (PS[2], "S2")])
    ACC = [(PS[3], "acc0"), (PS[4], "acc1"), (PS[5], "acc2"), (PS[6], "acc3")]

    def att_common():
        c = {}
        c["Pb"] = Rot([(AB.get(512), "P%d" % i) for i in range(3)])
        c["qTb"] = [AB.get(512), AB.get(512)]
        c["gpb"] = [AB.get(512), AB.get(512)]
        c["yst"] = [AB.get(512), AB.get(512)]
        c["rbuf"] = AFp.get(512)
        c["tmp"] = AFp.get(512)
        return c

    def run_unit(c, N, keys, accs, sink=None, Rq=(), rot=None, prot=None):
        nk = len(keys)
        rot = rot or rotS
        prot = prot or c["Pb"]

        def do_S(j):
            kd = keys[j]
            sp, sres = rot.next()
            started = set()
            for (rhs_tab, c0, c1) in kd.get("tab", ()):
                b.mm(sp[:, c0:c1], ident, rhs_tab, (c0 // 512) not in started, False, kd["R"], [sres], skip=True)
                started.add(c0 // 512)
            for (kT, qT, c0, c1) in kd["S"]:
                b.mm(sp[:, c0:c1], kT, qT, (c0 // 512) not in started, True, list(kd["R"]) + list(Rq), [sres], skip=True)
                started.add(c0 // 512)
            return sp, sres

        cur = do_S(0)
        for j in range(nk):
            nxt = do_S(j + 1) if j + 1 < nk else None
            sp, sres = cur
            pb, pres = prot.next()
            b.act(pb[:, 0:N], sp[:, 0:N], AF.Exp, [sres], [pres])
            started = set()
            for pv in keys[j]["PV"]:
                V, ai, c0, c1 = pv[:4]
                d0, d1 = (pv[4], pv[5]) if len(pv) > 4 else (c0, c1)
                acc, ares = accs[ai]
                st = (j == 0) and (ai not in started)
                started.add(ai)
                b.mm(acc[:, d0:d1], V, pb[:, c0:c1], st, (j == nk - 1) and sink is None, [pres] + list(keys[j]["R"]), [ares], skip=True)
            cur = nxt
        if sink is not None:
            acc, ares = accs[0]
            b.mm(acc[:, 0:N], sink[0], sink[1], False, True, ["esink", "sinkV0", "sinkV1"], [ares], skip=True)

    def epi_heads(c, acc, ares, heads, gpb, gres, yst, yres):
        N = 128 * len(heads)
        rbuf, tmp = c["rbuf"], c["tmp"]
        b.recip(rbuf[64:128, 0:N], acc[64:128, 0:N], [ares], ["rbuf"])
        g3 = gpb.rearrange("p (c t) -> p c t", c=4)
        y3 = yst.rearrange("p (c t) -> p c t", c=4)
        for i, h in enumerate(heads):
            cc, pb_ = h // 2, (h % 2) * 64
            sl = slice(i * 128, (i + 1) * 128)
            b.tt("dve", tmp[pb_:pb_ + 64, sl], acc[0:64, sl], rbuf[64:128, sl], ALU.mult, [ares, "rbuf"], ["tmp"])
            b.tt("pool", y3[pb_:pb_ + 64, cc, :], tmp[pb_:pb_ + 64, sl], g3[pb_:pb_ + 64, cc, :], ALU.mult,
                 ["tmp", gres], [yres])

    def load_q(c, m, qt, par):
        b.dma("sp", c["qTb"][par], [L["qTA"], L["qTB"], L["qTC"]][m][qt], W=["qTb%d" % par])
        b.dma("sp", c["gpb"][par].rearrange("p (c t) -> p c t", c=4),
              L["gpT"][m * 4:(m + 1) * 4, :, qt * 128:(qt + 1) * 128].rearrange("c p t -> p c t"), W=["gpb%d" % par])

    def store_y(c, m, qt, par):
        b.dma("sp", L["yT"][m * 4:(m + 1) * 4, :, qt * 128:(qt + 1) * 128].rearrange("c p t -> p c t"),
              c["yst"][par].rearrange("p (c t) -> p c t", c=4), ["yst%d" % par], ["yT"])

    phase()
    c = att_common()
    KTA = AB.get(NEXT * 128)
    VA = AB.get(NEXT * 256).rearrange("p (t g e) -> p t g e", t=NEXT, g=2)
    tabA = AB.get(4 * 512).rearrange("p (c n) -> p c n", c=4)
    b.dma("sp", KTA, L["kTA"], W=["KT"])
    b.dma("sp", VA.rearrange("p t g e -> p t (g e)"), L["vA"].rearrange("(t p) c -> p t c", p=128), W=["V"])
    b.dma("pool", tabA, L["tabA"].rearrange("c p n -> p c n"), W=["tab"])
    ui = 0
    for qt in range(nqt):
        par = qt % 2
        e = QT_EXT[qt]
        load_q(c, 0, qt, par)
        for g in range(2):
            if qt < 16:
                kl = [(e - 1, 0 if qt == 0 else 1), (e, None), (e + 1, 3 if qt == 15 else 2), (20, None), (21, None)]
            else:
                kl = [(20, None), (21, None)]
            keys = []
            for (t, ti_) in kl:
                kd = dict(S=[(KTA[g * 64:(g + 1) * 64, t * 128:(t + 1) * 128], c["qTb"][par][g * 64:(g + 1) * 64, :], 0, 512)],
                          PV=[(VA[:, t, g, :], 0, 0, 512)], R=["KT", "V", "tab"])
                if ti_ is not None:
                    kd["tab"] = [(tabA[:, ti_, :], 0, 512)]
                keys.append(kd)
            acc = ACC[ui % 2]
            ui += 1
            run_unit(c, 512, keys, [acc], sink=(P["sinkV"][0:1, :], P["esink"][0:1, g * 512:(g + 1) * 512]), Rq=["qTb%d" % par])
            epi_heads(c, acc[0], acc[1], [g * 4 + i for i in range(4)], c["gpb"][par], "gpb%d" % par, c["yst"][par], "yst%d" % par)
        store_y(c, 0, qt, par)
    if dbg == "attA":
        return

    phase()
    c = att_common()
    KTB = AB.get(4 * NEXT * 128).rearrange("p (i t) -> p i t", i=4)
    VB = AB.get(NEXT * 1024).rearrange("p (t h e) -> p t h e", t=NEXT, h=8)
    tabB = AB.get(2 * 6 * 512).rearrange("p (q s n) -> p q s n", q=2, s=6)
    tabBf = AFp.get(2 * 6 * 512)
    b.dma("sp", KTB, L["kTB"].rearrange("i p t -> p i t"), W=["KT"])
    for t0 in range(0, NEXT, 11):
        b.dma("sp", VB[:, t0:t0 + 11].rearrange("p t h e -> p t (h e)"),
              L["vB"][t0 * 128:(t0 + 11) * 128, :].rearrange("(t p) c -> p t c", p=128), W=["V"])
    case_of = lambda qt: 0 if qt == 0 else 1 if qt == 1 else 3 if qt == 14 else 4 if qt == 15 else 2
    dl_of = {0: list(range(-2, 4)), 1: list(range(-2, 3)), 2: list(range(-2, 3)), 3: list(range(-2, 3)), 4: list(range(-3, 3))}
    cur_case = None
    ui = 0
    for qt in range(nqt):
        par = qt % 2
        e = QT_EXT[qt]
        load_q(c, 1, qt, par)
        if qt < 16 and case_of(qt) != cur_case:
            cur_case = case_of(qt)
            b.dma("sp", tabBf, L["tabB"][cur_case], W=["tabf"])
            b.copy("pool", tabB.rearrange("p q s n -> p (q s n)"), tabBf, ["tabf"], ["tab"])
        for quad in range(2):
            if qt < 16:
                kl = [(e + dl, si) for si, dl in enumerate(dl_of[cur_case])] + [(20, None), (21, None)]
            else:
                kl = [(20, None), (21, None)]
            keys = []
            for (t, si) in kl:
                kd = dict(S=[], PV=[], R=["KT", "V", "tab"])
                if si is not None:
                    kd["tab"] = []
                for hh in range(4):
                    h = 2 * hh + quad
                    i_, g_ = hh, quad
                    kd["S"].append((KTB[g_ * 64:(g_ + 1) * 64, i_, t * 128:(t + 1) * 128],
                                    c["qTb"][par][g_ * 64:(g_ + 1) * 64, i_ * 128:(i_ + 1) * 128], hh * 128, (hh + 1) * 128))
                    kd["PV"].append((VB[:, t, h, :], 0, hh * 128, (hh + 1) * 128))
                    if si is not None:
                        kd["tab"].append((tabB[:, quad, si, hh * 128:(hh + 1) * 128], hh * 128, (hh + 1) * 128))
                keys.append(kd)
            acc = ACC[ui % 2]
            ui += 1
            run_unit(c, 512, keys, [acc], Rq=["qTb%d" % par])
            epi_heads(c, acc[0], acc[1], [2 * i + quad for i in range(4)], c["gpb"][par], "gpb%d" % par, c["yst"][par], "yst%d" % par)
        store_y(c, 1, qt, par)
    if dbg == "attB":
        return

    phase()
    c = att_common()
    KTC = AB.get(NALL * 128)
    VC = AB.get(NALL * 256).rearrange("p (t g e) -> p t g e", t=NALL, g=2)
    b.dma("sp", KTC, L["kTC"], W=["KT"])
    for t0 in range(0, NALL, 11):
        b.dma("sp", VC[:, t0:t0 + 11].rearrange("p t g e -> p t (g e)"),
              L["vC"][t0 * 128:(t0 + 11) * 128, :].rearrange("(t p) c -> p t c", p=128), W=["V"])
    PSW = T["PSW"]
    rotS2 = Rot([(PSW[:, 0:1024], "SW0"), (PSW[:, 1024:2048], "SW1"), (PSW[:, 2048:3072], "SW2")])
    Pb2 = Rot([(AB.get(1024), "PW%d" % i) for i in range(3)])
    ACC2 = [(PS[6], "acc6"), (PS[7], "acc7")]
    for qt in range(nqt):
        par = qt % 2
        load_q(c, 2, qt, par)
        tl = list(range(NALL)) if qt < 16 else [64, 65]
        keys = [dict(S=[(KTC[g * 64:(g + 1) * 64, t * 128:(t + 1) * 128], c["qTb"][par][g * 64:(g + 1) * 64, :], g * 512, (g + 1) * 512)
                        for g in range(2)],
                     PV=[(VC[:, t, g, :], g, g * 512, (g + 1) * 512, 0, 512) for g in range(2)], R=["KT", "V"]) for t in tl]
        run_unit(c, 1024, keys, ACC2, Rq=["qTb%d" % par], rot=rotS2, prot=Pb2)
        for g in range(2):
            epi_heads(c, ACC2[g][0], ACC2[g][1], [g * 4 + i for i in range(4)], c["gpb"][par], "gpb%d" % par, c["yst"][par], "yst%d" % par)
        store_y(c, 2, qt, par)
    if dbg == "attC":
        return

    phase()
    c = att_common()
    KTD = AB.get(NALL * 128)
    VD = AB.get(NALL * 256).rearrange("p (t v e) -> p t v e", t=NALL, v=2)
    rb = [AFp.get(512), AFp.get(512)]
    dT = AFp.get(512)
    dsq = AFp.get(512)
    tA = AFp.get(512)
    tB = AFp.get(512)
    PSW = T["PSW"]
    rotSD = Rot([(PSW[:, 0:1024], "SW0"), (PSW[:, 1024:2048], "SW1")])
    PbD = Rot([(AB.get(1024), "PW%d" % i) for i in range(3)])
    ACCD = [(PS[4], "acc4"), (PS[5], "acc5"), (PS[6], "acc6"), (PS[7], "acc7")]
    ACC = ACCD
    ui = 0
    for h in range(4):
        b.dma("sp", KTD, L["kTD"][h], W=["KT"])
        for t0 in range(0, NALL, 11):
            b.dma("sp", VD[:, t0:t0 + 11].rearrange("p t v e -> p t (v e)"),
                  L["vD"][t0 * 128:(t0 + 11) * 128, h * 256:(h + 1) * 256].rearrange("(t p) c -> p t c", p=128), W=["V"])
        for gi in range(ngrp):
            par = ui % 2
            ui += 1
            N = 512 if gi < 4 else 256
            q0 = gi * 512
            b.dma("sp", c["qTb"][par], L["qTD"][gi, :, h, :], W=["qTb%d" % par])
            b.dma("sp", c["gpb"][par][:, 0:N], L["gpT"][12 + h, :, q0:q0 + N], W=["gpb%d" % par])
            tl = list(range(NALL)) if gi < 4 else [64, 65]
            keys = [dict(S=[(KTD[cc * 64:(cc + 1) * 64, t * 128:(t + 1) * 128], c["qTb"][par][cc * 64:(cc + 1) * 64, 0:N], cc * 512, cc * 512 + N)
                            for cc in range(2)],
                         PV=[(VD[:, t, v, :], cc * 2 + v, cc * 512, cc * 512 + N, 0, N) for cc in range(2) for v in range(2)],
                         R=["KT", "V"]) for t in tl]
            run_unit(c, 512 + N, keys, ACCD, Rq=["qTb%d" % par], rot=rotSD, prot=PbD)
            for cc in range(2):
                b.recip(rb[cc][64:128, 0:N], ACC[cc * 2][0][64:128, 0:N], [ACC[cc * 2][1]], ["rb%d" % cc])
            for v in range(2):
                b.tt("dve", tA[0:64, 0:N], ACC[v][0][0:64, 0:N], rb[0][64:128, 0:N], ALU.mult, [ACC[v][1], "rb0"], ["tA"])
                b.tt("dve", tB[0:64, 0:N], ACC[2 + v][0][0:64, 0:N], rb[1][64:128, 0:N], ALU.mult, [ACC[2 + v][1], "rb1"], ["tB"])
                b.stt("dve", dT[v * 64:(v + 1) * 64, 0:N], tB[0:64, 0:N], P["neglam"][0:64, :], tA[0:64, 0:N], ALU.mult, ALU.add,
                      ["tA", "tB", "neglam"], ["dT"])
            b.tt("pool", dsq[:, 0:N], dT[:, 0:N], dT[:, 0:N], ALU.mult, ["dT"], ["dsq"])
            ssp, ssres = rotSD.next()
            b.mm(ssp[:, 0:N], P["onesF"], dsq[:, 0:N], True, True, ["dsq", "onesF"], [ssres])
            b.act(dsq[:, 0:N], ssp[:, 0:N], AF.Sqrt, [ssres], ["dsq"], scale=1.0 / 128, bias=P["epsc"])
            b.recip(tA[:, 0:N], dsq[:, 0:N], ["dsq"], ["tA"])
            b.tt("dve", dT[:, 0:N], dT[:, 0:N], tA[:, 0:N], ALU.mult, ["dT", "tA"], ["dT"])
            b.stt("dve", c["yst"][par][:, 0:N], dT[:, 0:N], P["subln"], c["gpb"][par][:, 0:N], ALU.mult, ALU.mult,
                  ["dT", "subln", "gpb%d" % par], ["yst%d" % par])
            b.dma("sp", L["yT"][12 + h, :, q0:q0 + N], c["yst"][par][:, 0:N], ["yst%d" % par], ["yT"])
    if dbg == "attD":
        return

    phase()
    wbr = AB.get(16 * 1024).rearrange("p (c d) -> p c d", c=16)
    wout = AB.get(8 * 1024).rearrange("p (c d) -> p c d", c=8)
    b.dma("pool", wbr, L["w_br_r"], W=["wbr"])
    b.dma("pool", wout, L["w_out_r"], W=["wout"])
    yTg = [AB.get(16 * 512).rearrange("p (c t) -> p c t", c=16) for _ in range(2)]
    mgg = [AB.get(32 * 512).rearrange("p (c t) -> p c t", c=32) for _ in range(1)]
    mrg = AB.get(8 * 512).rearrange("p (c t) -> p c t", c=8)
    macc = AFp.get(512)
    mtmp = AFp.get(512)
    xres = [AFp.get(1024), AFp.get(1024)]
    xo = [AFp.get(1024), AFp.get(1024)]
    rotN = Rot([(PS[i], "N%d" % i) for i in range(4)])
    rotO = Rot([(PS[4 + i], "O%d" % i) for i in range(4)])
    oi = 0
    for gi in range(ngrp):
        N = 512 if gi < 4 else 256
        q0 = gi * 512
        yg = yTg[gi % 2]
        yres = "yTg%d" % (gi % 2)
        mg = mgg[0]
        b.dma("sp", yg[:, :, 0:N], L["yT"][:, :, q0:q0 + N].rearrange("c p t -> p c t"), W=[yres])
        b.dma("sp", mg[:, :, 0:N], L["mgT"][:, :, q0:q0 + N].rearrange("c p t -> p c t"), W=["mgg"])
        for dc in range(8):
            for n in range(4):
                ps_, pres = rotN.next()
                for wc in range(4):
                    b.mm(ps_[:, 0:N], wbr[:, n * 4 + wc, dc * 128:(dc + 1) * 128], yg[:, n * 4 + wc, 0:N], wc == 0, wc == 3,
                         ["wbr", yres], [pres])
                if n == 0:
                    b.tt("dve", macc[:, 0:N], ps_[:, 0:N], mg[:, n * 8 + dc, 0:N], ALU.mult, [pres, "mgg"], ["macc"])
                else:
                    b.tt("dve", mtmp[:, 0:N], ps_[:, 0:N], mg[:, n * 8 + dc, 0:N], ALU.mult, [pres, "mgg"], ["mtmp"])
                    if n < 3:
                        b.tt("pool", macc[:, 0:N], macc[:, 0:N], mtmp[:, 0:N], ALU.add, ["macc", "mtmp"], ["macc"])
                    else:
                        b.tt("pool", mrg[:, dc, 0:N], macc[:, 0:N], mtmp[:, 0:N], ALU.add, ["macc", "mtmp"], ["mrg"])
        for tl in range(N // 128):
            par = oi % 2
            oi += 1
            if gi < 4:
                qt = gi * 4 + tl
                src = L["xext"][(2 + qt) * 128:(3 + qt) * 128, :]
                dst = L["xnew"][qt * 128:(qt + 1) * 128, :]
                wi = 0
            else:
                src = L["ctx"][tl * 128:(tl + 1) * 128, :]
                dst = L["ctxnew"][tl * 128:(tl + 1) * 128, :]
                wi = 1
            b.dma("sp", xres[par], src, W=["xres%d" % par])
            for half in range(2):
                ps_, pres = rotO.next()
                for dc in range(8):
                    b.mm(ps_, mrg[:, dc, tl * 128:(tl + 1) * 128], wout[:, dc, half * 512:(half + 1) * 512], dc == 0, dc == 7,
                         ["mrg", "wout"], [pres])
                b.tt("dve", xo[par][:, half * 512:(half + 1) * 512], ps_, P["gate_rep"][wi][:, half * 512:(half + 1) * 512], ALU.mult,
                     [pres], ["xo%d_%d" % (par, half)])
                b.tt("pool", xo[par][:, half * 512:(half + 1) * 512], xo[par][:, half * 512:(half + 1) * 512],
                     xres[par][:, half * 512:(half + 1) * 512], ALU.add, ["xo%d_%d" % (par, half), "xres%d" % par], ["xo%d_%d" % (par, half)])
            b.dma("sp", dst, xo[par], ["xo%d_0" % par, "xo%d_1" % par], ["out"])


SCRATCH = {
    "kTC": ([128, NALL * 128], BF16), "vC": ([NALL * 128, 256], BF16),
    "kTD": ([4, 128, NALL * 128], BF16), "vD": ([NALL * 128, 1024], BF16),
    "kTA": ([128, NEXT * 128], BF16), "vA": ([NEXT * 128, 256], BF16),
    "kTB": ([4, 128, NEXT * 128], BF16), "vB": ([NEXT * 128, 1024], BF16),
    "qTA": ([NQT, 128, 512], BF16), "qTB": ([NQT, 128, 512], BF16), "qTC": ([NQT, 128, 512], BF16),
    "qTD": ([5, 128, 4, 512], BF16),
    "gpT": ([16, 128, NTOKQ], BF16), "mgT": ([32, 128, NTOKQ], BF16), "yT": ([16, 128, NTOKQ], BF16),
}
SHARED_INPUTS = {
    "xall": [64 * 128, D_MODEL], "xext": [20 * 128, D_MODEL], "ctx": [256, D_MODEL],
    "cosall": [NALL * 128, 64], "sinall": [NALL * 128, 64], "cosext": [NEXT * 128, 64], "sinext": [NEXT * 128, 64],
    "cT": [128, 8], "ccT": [128, 8], "tabA": [4, 128, 512], "halo_sel": [128, 6],
}
PER_LAYER_INPUTS = {
    "normwT": [128, 8], "badaT": [128, 24], "bada_rep": [128, 3072],
    "w_ada_r": [128, KC, 3072], "w_in_r": [128, KC, IN_COLS], "w_br_r": [128, 16, 1024], "w_out_r": [128, KC, 1024],
    "gains": [128, 512], "lam_rep": [128, 256], "sublnT": [128, 1], "lconst": [128, 2], "sink_rep": [1, 1024],
    "tabB": [5, 128, 2 * 6 * 512],
}
GROUPS = [[0, 1, 2, 3], [4, 5, 6, 7]]


def build_program(dbg=None, dbg_out=()):
    nc = bass.Bass("TRN2", target_bir_lowering=False)
    es = ExitStack()
    S = Sched()
    b = Bld(nc, S)
    shared = {}
    for name, shape in SHARED_INPUTS.items():
        shared[name] = nc.dram_tensor(name, shape, F32, kind="ExternalInput").ap()
    for name, (shape, dt) in SCRATCH.items():
        kind = "ExternalOutput" if name in dbg_out else "Internal"
        shared[name] = nc.dram_tensor(name, shape, dt, kind=kind).ap()
    Ls = []
    for l in range(2):
        L = dict(shared)
        for name, shape in PER_LAYER_INPUTS.items():
            L[name] = nc.dram_tensor("%s_%d" % (name, l), shape, F32, kind="ExternalInput").ap()
        Ls.append(L)
    x1own = nc.dram_tensor("x1own", [16 * 128, D_MODEL], F32, kind="Internal").ap()
    x1all = nc.dram_tensor("x1all", [16, 4 * 128, D_MODEL], F32, kind="Internal").ap()
    xext2 = nc.dram_tensor("xext2", [20 * 128, D_MODEL], F32, kind="Internal").ap()
    ctx1 = nc.dram_tensor("ctx1", [256, D_MODEL], F32, kind="Internal").ap()
    ctx2 = nc.dram_tensor("ctx2", [256, D_MODEL], F32, kind="Internal").ap()
    out = nc.dram_tensor("out", [16 * 128, D_MODEL], F32, kind="ExternalOutput").ap()
    Ls[0]["xnew"], Ls[0]["ctxnew"] = x1own, ctx1
    x1tile = lambda n: x1all[n % 16, (n // 16) * 128:(n // 16 + 1) * 128, :]
    xall0 = shared["xall"]
    Ls[0]["xall_tile"] = lambda n: xall0[n * 128:(n + 1) * 128, :]
    Ls[1]["xall_tile"] = x1tile
    Ls[1]["xext"], Ls[1]["ctx"] = xext2, ctx1
    Ls[1]["xnew"], Ls[1]["ctxnew"] = out, ctx2

    sb = lambda name, shape, dt: es.enter_context(nc.sbuf_tensor("sb_" + name, shape, dt))[:]
    T = {}
    T["ident"] = sb("ident", [128, 128], BF16)
    ones_b = sb("ones_b", [128, 128], BF16)
    T["arenaB"] = Arena(sb("arenaB", [128, 61440], BF16), 61440)
    T["arenaF"] = Arena(sb("arenaF", [128, 13500], F32), 13500)
    P = {}
    P["gains"] = sb("gains", [128, 512], F32)
    P["sT"] = [sb("sT%d" % i, [128, 8], F32) for i in range(2)]
    P["shT"] = [sb("shT%d" % i, [128, 8], F32) for i in range(2)]
    P["gate_rep"] = [sb("gate_rep%d" % i, [128, 1024], F32) for i in range(2)]
    P["esink"] = sb("esink", [1, 1024], BF16)
    P["sinkV"] = sb("sinkV", [1, 128], BF16)
    P["neglam"] = sb("neglam", [128, 1], F32)
    P["subln"] = sb("subln", [128, 1], F32)
    P["epsc"] = sb("epsc", [128, 1], F32)
    P["onesF"] = sb("onesF", [128, 128], F32)
    T["persist"] = P
    psw = es.enter_context(nc.psum_tensor("psw", [128, 4096], F32))[:]
    ps = [psw[:, i * 512:(i + 1) * 512] for i in range(8)]
    T["PS"] = ps
    T["PSW"] = psw
    T["PSB"] = [p.bitcast(BF16) for p in ps]

    b.memset("pool", ones_b, 1.0, W=["ones_b"])
    S.add("pool", lambda e: e.affine_select(out=T["ident"], in_=ones_b, pattern=[[-1, 128]], compare_op=ALU.is_equal,
                                            fill=0.0, base=0, channel_multiplier=1), ["ones_b"], ["ident"])
    b.memset("pool", P["epsc"], EPS, W=["epsc"])
    b.memset("pool", P["onesF"], 1.0, W=["onesF"])

    if dbg == "xchg":
        b.dma("sp", x1own, shared["xext"][2 * 128:18 * 128, :], W=["x1own"])
    else:
        build_layer(nc, S, b, T, Ls[0], need_ctx=True, dbg=dbg)
        if dbg is not None:
            return nc, S, es

    S.barrier()
    T["arenaB"].reset()
    T["arenaF"].reset()
    AFp = T["arenaF"]
    for t in range(16):
        S.add("pool", lambda e, t=t: e.collective_compute("AllGather", ALU.bypass, replica_groups=GROUPS,
                                                          ins=[x1own[t * 128:(t + 1) * 128, :].opt()], outs=[x1all[t].opt()]),
              (), ["x1all"], cc=True)
    b.dma("sp", xext2[2 * 128:18 * 128, :], x1own, W=["xext2own"])
    hs = AFp.get(6)
    b.dma("sp", hs, shared["halo_sel"], W=["hs"])
    cand = [AFp.get(1024) for _ in range(3)]
    hacc = [AFp.get(1024) for _ in range(2)]
    k = 0
    for side in range(2):
        for j in range(2):
            for r in range(3):
                rr = r if side == 0 else r + 1
                tile_ = 16 * rr + 14 + j if side == 0 else 16 * rr + j
                b.dma("sp", cand[r], x1tile(tile_), ["x1all"], ["cand%d" % r])
            acc_ = hacc[k % 2]
            ares = "hacc%d" % (k % 2)
            k += 1
            b.ts("dve", acc_, cand[0], hs[:, side * 3:side * 3 + 1], None, ALU.mult, R=["cand0", "hs"], W=[ares])
            for r in (1, 2):
                b.stt("dve", acc_, cand[r], hs[:, side * 3 + r:side * 3 + r + 1], acc_, ALU.mult, ALU.add,
                      ["cand%d" % r, "hs", ares], [ares])
            e_ = j if side == 0 else 18 + j
            b.dma("sp", xext2[e_ * 128:(e_ + 1) * 128, :], acc_, [ares], ["xext2h"])

    if dbg == "xchg":
        S.barrier()
        b.dma("sp", out, xext2[0:16 * 128, :], W=["out"])
        return nc, S, es
    build_layer(nc, S, b, T, Ls[1], need_ctx=False, dbg=None)
    return nc, S, es


def launch(nc, S, es, in_maps, trace=False):
    sems = {e: es.enter_context(nc.semaphore("sem_" + e)) for e in ENGS}
    dma_sems = {e: [es.enter_context(nc.semaphore("dsem_%s_%d" % (e, i))) for i in range(NSLOT)]
                for e in ("sp", "act", "pool")}
    dma_sems["cc"] = es.enter_context(nc.semaphore("cc_sem"))
    S.finalize()
    with nc.Block() as block:
        @block.sync
        def _(e):
            S.emit_engine("sp", e, sems, dma_sems, final_wait=True)

        @block.scalar
        def _(e):
            S.emit_engine("act", e, sems, dma_sems, final_wait=True)

        @block.vector
        def _(e):
            S.emit_engine("dve", e, sems, dma_sems)

        @block.gpsimd
        def _(e):
            S.emit_engine("pool", e, sems, dma_sems, final_wait=True)

        @block.tensor
        def _(e):
            S.emit_engine("pe", e, sems, dma_sems)


def rope_tables():
    t = np.arange(SEQ)
    pos = np.stack([t // GRID_W, t % GRID_W], -1).astype(np.float32)
    nf = 16
    freqs = (np.float32(10000.0) ** (-np.arange(nf, dtype=np.float32) / np.float32(nf))).astype(np.float32)
    ang = pos[:, :, None] * freqs[None, None, :]
    ang = np.concatenate([ang, ang], -1).reshape(SEQ, 64).astype(np.float32)
    cos = np.cos(ang).astype(np.float32)
    sin = np.sin(ang).astype(np.float32).reshape(SEQ, 2, 2, 16).copy()
    sin[:, :, 0, :] *= -1.0
    return cos, sin.reshape(SEQ, 64)


def tab_a(s):
    j = np.arange(128)[:, None]
    q = np.arange(128)[None, :]
    m1 = np.where(j >= q, 0.0, NEG).astype(np.float32)
    p1 = np.where(j <= q, 0.0, NEG).astype(np.float32)
    full = np.full((128, 128), NEG, np.float32)
    cases = [full if s == 0 else m1, m1, p1, full if s == 3 else p1]
    return np.stack([np.tile(c, (1, 4)) for c in cases], 0)


DL_OF = {0: list(range(-2, 4)), 1: list(range(-2, 3)), 2: list(range(-2, 3)), 3: list(range(-2, 3)), 4: list(range(-3, 3))}


def tab_b(s, rpb):
    out = np.full((5, 128, 2, 6, 4, 128), NEG, np.float32)
    j = np.arange(128)[:, None]
    q = np.arange(128)[None, :]
    for case, i in enumerate((0, 1, 5, 14, 15)):
        n = 16 * s + i
        for si, dl in enumerate(DL_OF[case]):
            kt = n + dl
            if kt < 0 or kt > 63:
                continue
            r = 2 * n + q // 64
            qc = q % 64
            kr = 2 * kt + j // 64
            kc = j % 64
            rstart = np.clip(r - 4, 0, 120)
            cstart = np.clip(qc - 8, 0, 48)
            valid = (kr >= rstart) & (kr < rstart + 8) & (kc >= cstart) & (kc < cstart + 16)
            dr = np.clip(kr - r + 7, 0, 14)
            dc = np.clip(kc - qc, -15, 15) + 15
            for h in range(8):
                vals = rpb[h][dr, dc]
                out[case, :, h % 2, si, h // 2, :] = np.where(valid, vals, NEG)
    return out.reshape(5, 128, 2 * 6 * 512)


def prep_inputs(inp, consts):
    cos, sin = consts["cos"], consts["sin"]
    f = lambda a: np.ascontiguousarray(a, dtype=np.float32)
    r8 = lambda v: f(v.reshape(8, 128).T)
    x, ctx = inp["x"], inp["ctx"]
    shared = {
        "ccT": r8(inp["c_ctx"]),
        "cosall": f(np.concatenate([cos, np.ones((256, 64), np.float32)], 0)),
        "sinall": f(np.concatenate([sin, np.zeros((256, 64), np.float32)], 0)),
    }
    for l in range(2):
        lam_init = 0.8 - 0.6 * math.exp(-0.3 * l)
        per = {
            "normwT": r8(inp["norm_w"][l]),
            "badaT": f(inp["b_ada"][l].reshape(24, 128).T),
            "bada_rep": f(np.broadcast_to(inp["b_ada"][l][None, :], (128, 3072))),
            "w_ada_r": f(inp["w_ada"][l].reshape(8, 128, 3072).transpose(1, 0, 2)),
            "w_in_r": f(inp["w_in"][l].reshape(8, 128, IN_COLS).transpose(1, 0, 2)),
            "w_br_r": f(inp["w_br"][l].reshape(16, 128, 1024).transpose(1, 0, 2)),
            "w_out_r": f(inp["w_out"][l].reshape(8, 128, 1024).transpose(1, 0, 2)),
            "gains": f(np.broadcast_to(inp["qk_gain"][l].reshape(1, 512), (128, 512))),
            "lam_rep": f(np.broadcast_to(inp["lam_d"][l].reshape(1, 256), (128, 256))),
            "sublnT": f(inp["subln_d"][l].reshape(128, 1)),
            "lconst": f(np.broadcast_to(np.array([[lam_init, 1.0 - lam_init]], np.float32), (128, 2))),
            "sink_rep": f(np.repeat(inp["sink_a"][l], 128).reshape(1, 1024)),
        }
        for k_, v in per.items():
            shared["%s_%d" % (k_, l)] = v
    maps = []
    for core in range(8):
        bi, s = core // 4, core % 4
        m = dict(shared)
        m["xall"] = f(x[bi])
        m["ctx"] = f(ctx[bi])
        m["cT"] = r8(inp["c"][bi])
        lo, hi = (16 * s - 2) * 128, (16 * s + 18) * 128
        xe = np.zeros((20 * 128, D_MODEL), np.float32)
        ce = np.ones((NEXT * 128, 64), np.float32)
        se = np.zeros((NEXT * 128, 64), np.float32)
        a, z = max(lo, 0), min(hi, SEQ)
        xe[a - lo:z - lo] = x[bi][a:z]
        ce[a - lo:z - lo] = cos[a:z]
        se[a - lo:z - lo] = sin[a:z]
        m["xext"], m["cosext"], m["sinext"] = xe, ce, se
        m["tabA"] = consts["tabA"][s]
        hsel = np.zeros((128, 6), np.float32)
        if s > 0:
            hsel[:, s - 1] = 1.0
        if s < 3:
            hsel[:, 3 + s] = 1.0
        m["halo_sel"] = hsel
        for l in range(2):
            m["tabB_%d" % l] = tab_b(s, inp["rpb_b"][l])
        maps.append(m)
    return maps


_PROG = {}


def get_program():
    if "p" not in _PROG:
        nc, S, es = build_program()
        launch(nc, S, es, None)
        _PROG["p"] = (nc, S, es)
    return _PROG["p"]


def kernel(**inputs):
    inp = {k: np.asarray(v, dtype=np.float32) for k, v in inputs.items()}
    cos, sin = rope_tables()
    consts = {"cos": cos, "sin": sin, "tabA": [tab_a(s) for s in range(4)]}
    nc, S, es = get_program()
    maps = prep_inputs(inp, consts)
    res = run_bass_kernel_spmd(nc, maps, core_ids=list(range(8)))
    outs = res.results
    x = np.stack([np.concatenate([outs[bi * 4 + s]["out"] for s in range(4)], 0) for bi in range(2)], 0)
    return x.astype(np.float32)
```
